# Optimizing a Trainium2 kernel written in Bass

```python
import math
import jax, jax.numpy as jnp
from jax import lax
import numpy as np

D_MODEL = 1024
BATCH = 8
SEQ = 2048
DEPTH = 2
DEC_BATCH = 128
DEC_SEQ = 4
PAST_LEN = 2048
PAGE_SIZE = 128

N_A_LAYERS = DEPTH // 2
N_B_LAYERS = DEPTH - N_A_LAYERS
RW_HEAD = 64
RW_HEADS = D_MODEL // RW_HEAD
LORA_W = 64
LORA_A = 64
LORA_G = 128
RW_GN_EPS = 64e-5
DA_HEAD = 64
DA_HEADS = D_MODEL // (2 * DA_HEAD)
DA_VDIM = 2 * DA_HEAD
Q_BLOCK = 128
RMS_EPS = 1e-5
NEG_INF = -1e30
PEER_HEADS = 8
PEER_NKEYS = 128
PEER_NEXPERTS = PEER_NKEYS * PEER_NKEYS
PEER_QDIM = 256
PEER_HALF = PEER_QDIM // 2
PEER_TOPK = 16
PEER_BLOCK = 128
PLE_DIM = 256
DN_ALPHA = (2.0 * DEPTH) ** 0.25
DN_BETA = (8.0 * DEPTH) ** -0.25
LN_EPS = 1e-5

kernel_name = "yoco_rwkv7_diffattn_peer_step"


def layer_norm(x, g, b):
    xf = x.astype(jnp.float32)
    mu = jnp.mean(xf, -1, keepdims=True)
    var = jnp.mean(jnp.square(xf - mu), -1, keepdims=True)
    return ((xf - mu) * lax.rsqrt(var + LN_EPS) * g + b).astype(x.dtype)


def wkv7_scan(S0, r, decay, k, v, kk, a):
    def step(S, inp):
        r_t, w_t, k_t, v_t, kk_t, a_t = inp
        sa = jnp.einsum('bhvk,bhk->bhv', S, -kk_t)
        S = (S * w_t[:, :, None, :] + sa[..., None] * (kk_t * a_t)[:, :, None, :]
             + v_t[..., None] * k_t[:, :, None, :])
        return S, jnp.einsum('bhvk,bhk->bhv', S, r_t)
    xs = tuple(jnp.moveaxis(t, 1, 0) for t in (r, decay, k, v, kk, a))
    S, ys = lax.scan(step, S0, xs)
    return jnp.moveaxis(ys, 0, 1), S


def rwkv7_time_mix(x, shift_row, S0, mu, w_r, w_k, w_v, w0, w1, w2, a0, a1, a2,
                   g1, g2, k_k, k_a, r_k, gn_g, gn_b, w_o):
    f32 = jnp.float32
    B, T, D = x.shape
    x_prev = jnp.concatenate([shift_row[:, None, :].astype(x.dtype), x[:, :-1]], axis=1)
    xx = x_prev - x
    xr, xw, xk, xv, xa, xg = (x + xx * mu[i] for i in range(6))
    r = (xr @ w_r).astype(f32)
    k = (xk @ w_k).astype(f32)
    v = (xv @ w_v).astype(f32)
    w_log = -jax.nn.softplus(-(w0 + jnp.tanh(xw @ w1) @ w2).astype(f32)) - 0.5
    decay = jnp.exp(-jnp.exp(w_log))
    a = jax.nn.sigmoid((a0 + (xa @ a1) @ a2).astype(f32))
    g = jax.nn.sigmoid(xg @ g1) @ g2
    heads = lambda t: t.reshape(B, T, RW_HEADS, RW_HEAD)
    kk = heads(k * k_k)
    kk = kk / jnp.maximum(jnp.sqrt(jnp.sum(kk * kk, -1, keepdims=True)), 1e-12)
    k = k * (1.0 + (a - 1.0) * k_a)
    rh, kh, vh = heads(r), heads(k), heads(v)
    y, S = wkv7_scan(S0.astype(f32), rh, heads(decay), kh, vh, kk, heads(a))
    mu_y = jnp.mean(y, -1, keepdims=True)
    var_y = jnp.mean(jnp.square(y - mu_y), -1, keepdims=True)
    yn = ((y - mu_y) * lax.rsqrt(var_y + RW_GN_EPS)).reshape(B, T, D) * gn_g + gn_b
    bonus = (jnp.sum(rh * kh * r_k, -1, keepdims=True) * vh).reshape(B, T, D)
    out = ((yn + bonus).astype(x.dtype) * g) @ w_o
    return out, x[:, -1], S.astype(S0.dtype)


def alibi_slopes():
    return 2.0 ** (-8.0 * jnp.arange(1, DA_HEADS + 1, dtype=jnp.float32) / DA_HEADS)


def diff_softmax_core(q, k, v, q_pos, k_pos, lam):
    s = jnp.einsum('bqhcd,bkhcd->bhcqk', q, k).astype(jnp.float32) * (DA_HEAD ** -0.5)
    dist = (q_pos[:, None] - k_pos[None, :]).astype(jnp.float32)
    s = s - alibi_slopes()[None, :, None, None, None] * dist
    s = jnp.where(dist >= 0, s, NEG_INF)
    p = jax.nn.softmax(s, axis=-1)
    att = p[:, :, 0] - lam * p[:, :, 1]
    return jnp.einsum('bhqk,bkhv->bqhv', att.astype(v.dtype), v)


def diff_attn_mixer(x, k_all, v_all, past_len, w_q, lam_vec, subln_g, w_o, lam_init):
    f32 = jnp.float32
    B, T, D = x.shape
    q = (x @ w_q).reshape(B, T, DA_HEADS, 2, DA_HEAD)
    lv = lam_vec.astype(f32)
    lam = jnp.exp(jnp.sum(lv[0] * lv[1])) - jnp.exp(jnp.sum(lv[2] * lv[3])) + lam_init
    Tk = k_all.shape[1]
    k_pos = jnp.arange(Tk)
    kh = k_all.reshape(B, Tk, DA_HEADS, 2, DA_HEAD)
    q_pos = past_len + jnp.arange(T)
    if T % Q_BLOCK == 0:
        nb = T // Q_BLOCK
        qb = jnp.moveaxis(q.reshape(B, nb, Q_BLOCK, DA_HEADS, 2, DA_HEAD), 1, 0)
        pb = q_pos.reshape(nb, Q_BLOCK)
        o = lax.map(lambda args: diff_softmax_core(args[0], kh, v_all, args[1], k_pos, lam), (qb, pb))
        o = jnp.moveaxis(o, 0, 1).reshape(B, T, DA_HEADS, DA_VDIM)
    else:
        o = diff_softmax_core(q, kh, v_all, q_pos, k_pos, lam)
    of = o.astype(f32)
    of = of * lax.rsqrt(jnp.mean(of * of, -1, keepdims=True) + RMS_EPS) * subln_g
    o = (of * (1.0 - lam_init)).astype(x.dtype).reshape(B, T, D)
    return o @ w_o


def peer_ffn(x, w_q, b_q, subkeys, u_tab, v_tab):
    B, T, D = x.shape
    n = B * T
    n_pad = (-n) % PEER_BLOCK
    xb = jnp.pad(x.reshape(n, D), ((0, n_pad), (0, 0))).reshape(-1, PEER_BLOCK, D)

    def block(xblk):
        q = (xblk @ w_q + b_q).reshape(PEER_BLOCK, PEER_HEADS, 2, PEER_HALF)
        sc = jnp.einsum('thcd,cnd->thcn', q, subkeys).astype(jnp.float32)
        s_half, i_half = lax.top_k(sc, PEER_TOPK)
        cand = (s_half[:, :, 0, :, None] + s_half[:, :, 1, None, :]).reshape(
            PEER_BLOCK, PEER_HEADS, PEER_TOPK * PEER_TOPK)
        s_sel, c_sel = lax.top_k(cand, PEER_TOPK)
        i1 = jnp.take_along_axis(i_half[:, :, 0], c_sel // PEER_TOPK, axis=-1)
        i2 = jnp.take_along_axis(i_half[:, :, 1], c_sel % PEER_TOPK, axis=-1)
        expert = i1 * PEER_NKEYS + i2
        gate = jax.nn.softmax(s_sel, axis=-1)
        h = jnp.einsum('td,thkd->thk', xblk, u_tab[expert])
        act = (gate * jax.nn.gelu(h.astype(jnp.float32))).astype(xblk.dtype)
        return jnp.einsum('thk,thkd->td', act, v_tab[expert])

    out = lax.map(block, xb).reshape(-1, D)[:n]
    return out.reshape(B, T, D)


def per_layer_embed(x, p_i, w_p, w_g, b_g):
    return x + (p_i @ w_p) * jax.nn.sigmoid(x @ w_g + b_g)


def lambda_init(layer):
    return 0.8 - 0.6 * math.exp(-0.3 * layer)


def setup_inputs(seed: int = 0) -> dict:
    key = jax.random.key(seed)
    keys = jax.random.split(key, 64)
    counter = [0]

    def nk():
        counter[0] += 1
        return keys[counter[0] - 1]

    f32 = jnp.float32
    D = D_MODEL

    def nrm(shape, scale):
        return jax.random.normal(nk(), shape, f32) * scale

    def unif(shape, lo, hi):
        return jax.random.uniform(nk(), shape, f32, lo, hi)

    n_pages = PAST_LEN // PAGE_SIZE
    n_used = DEC_BATCH * n_pages
    n_pool = n_used + max(1, n_used // 4)
    NA, NB = N_A_LAYERS, N_B_LAYERS
    inp = {}
    inp['x_prompt'] = nrm((BATCH, SEQ, D), 1.0)
    inp['x_sample'] = nrm((DEC_BATCH, DEC_SEQ, D), 1.0)
    inp['state_shift'] = nrm((NA, DEC_BATCH, D), 1.0)
    inp['state_wkv'] = nrm((NA, DEC_BATCH, RW_HEADS, RW_HEAD, RW_HEAD), 0.5)
    inp['cache_k'] = nrm((n_pool, PAGE_SIZE, DA_HEADS, 2 * DA_HEAD), 1.0)
    inp['cache_v'] = nrm((n_pool, PAGE_SIZE, DA_HEADS, DA_VDIM), 1.0)
    inp['page_table'] = jax.random.permutation(nk(), n_pool)[:n_used].reshape(
        DEC_BATCH, n_pages).astype(jnp.int32)
    inp['p_prompt'] = nrm((DEPTH, BATCH, SEQ, PLE_DIM), 1.0)
    inp['p_sample'] = nrm((DEPTH, DEC_BATCH, DEC_SEQ, PLE_DIM), 1.0)
    inp['rw_mu'] = unif((NA, 6, D), 0.0, 1.0)
    inp['rw_w_r'] = nrm((NA, D, D), D ** -0.5)
    inp['rw_w_k'] = nrm((NA, D, D), D ** -0.5)
    inp['rw_w_v'] = nrm((NA, D, D), D ** -0.5)
    inp['rw_w0'] = unif((NA, D), -6.0, -0.5)
    inp['rw_w1'] = nrm((NA, D, LORA_W), D ** -0.5)
    inp['rw_w2'] = nrm((NA, LORA_W, D), 0.1 * LORA_W ** -0.5)
    inp['rw_a0'] = nrm((NA, D), 0.1)
    inp['rw_a1'] = nrm((NA, D, LORA_A), D ** -0.5)
    inp['rw_a2'] = nrm((NA, LORA_A, D), 0.1 * LORA_A ** -0.5)
    inp['rw_g1'] = nrm((NA, D, LORA_G), D ** -0.5)
    inp['rw_g2'] = nrm((NA, LORA_G, D), LORA_G ** -0.5)
    inp['rw_k_k'] = 0.85 + nrm((NA, D), 0.05)
    inp['rw_k_a'] = 1.0 + nrm((NA, D), 0.05)
    inp['rw_r_k'] = nrm((NA, RW_HEADS, RW_HEAD), 0.1)
    inp['rw_gn_g'] = 1.0 + nrm((NA, D), 0.02)
    inp['rw_gn_b'] = nrm((NA, D), 0.02)
    inp['rw_w_o'] = nrm((NA, D, D), DN_BETA * D ** -0.5)
    inp['da_w_k'] = nrm((D, DA_HEADS * 2 * DA_HEAD), D ** -0.5)
    inp['da_w_v'] = nrm((D, DA_HEADS * DA_VDIM), D ** -0.5)
    inp['da_w_q'] = nrm((NB, D, DA_HEADS * 2 * DA_HEAD), D ** -0.5)
    inp['da_lam'] = nrm((NB, 4, DA_HEAD), 0.1)
    inp['da_subln_g'] = 1.0 + nrm((NB, DA_VDIM), 0.02)
    inp['da_w_o'] = nrm((NB, DA_HEADS * DA_VDIM, D), DN_BETA * D ** -0.5)
    inp['ln1_g'] = 1.0 + nrm((DEPTH, D), 0.02)
    inp['ln1_b'] = nrm((DEPTH, D), 0.02)
    inp['ln2_g'] = 1.0 + nrm((DEPTH, D), 0.02)
    inp['ln2_b'] = nrm((DEPTH, D), 0.02)
    inp['peer_w_q'] = nrm((DEPTH, D, PEER_HEADS * PEER_QDIM), D ** -0.5)
    inp['peer_b_q'] = nrm((DEPTH, PEER_HEADS * PEER_QDIM), 0.02)
    inp['peer_subkeys'] = nrm((DEPTH, 2, PEER_NKEYS, PEER_HALF), PEER_HALF ** -0.5)
    inp['peer_u'] = nrm((DEPTH, PEER_NEXPERTS, D), D ** -0.5)
    inp['peer_v'] = nrm((DEPTH, PEER_NEXPERTS, D), DN_BETA * PEER_HEADS ** -0.5)
    inp['ple_w_p'] = nrm((DEPTH, PLE_DIM, D), DN_BETA * PLE_DIM ** -0.5)
    inp['ple_w_g'] = nrm((DEPTH, D, D), D ** -0.5)
    inp['ple_b_g'] = nrm((DEPTH, D), 0.02)
    return inp


def reference(x_prompt, x_sample, state_shift, state_wkv, cache_k, cache_v, page_table,
              p_prompt, p_sample,
              rw_mu, rw_w_r, rw_w_k, rw_w_v, rw_w0, rw_w1, rw_w2, rw_a0, rw_a1, rw_a2,
              rw_g1, rw_g2, rw_k_k, rw_k_a, rw_r_k, rw_gn_g, rw_gn_b, rw_w_o,
              da_w_k, da_w_v, da_w_q, da_lam, da_subln_g, da_w_o,
              ln1_g, ln1_b, ln2_g, ln2_b,
              peer_w_q, peer_b_q, peer_subkeys, peer_u, peer_v,
              ple_w_p, ple_w_g, ple_b_g):

    def layer_stack(x, pemb, shift0, wkv0, past_k, past_v):
        B, T, D = x.shape
        past_len = 0 if past_k is None else past_k.shape[1]
        shifts, wkvs = [], []
        k_sh = v_sh = k_all = v_all = None
        for layer in range(DEPTH):
            if layer < N_A_LAYERS:
                i = layer
                h, last_row, S = rwkv7_time_mix(
                    x, shift0[i], wkv0[i], rw_mu[i], rw_w_r[i], rw_w_k[i], rw_w_v[i],
                    rw_w0[i], rw_w1[i], rw_w2[i], rw_a0[i], rw_a1[i], rw_a2[i],
                    rw_g1[i], rw_g2[i], rw_k_k[i], rw_k_a[i], rw_r_k[i],
                    rw_gn_g[i], rw_gn_b[i], rw_w_o[i])
                shifts.append(last_row)
                wkvs.append(S)
            else:
                j = layer - N_A_LAYERS
                h = diff_attn_mixer(x, k_all, v_all, past_len, da_w_q[j], da_lam[j],
                                    da_subln_g[j], da_w_o[j], lambda_init(layer))
            x = layer_norm(DN_ALPHA * x + h, ln1_g[layer], ln1_b[layer])
            c = peer_ffn(x, peer_w_q[layer], peer_b_q[layer], peer_subkeys[layer],
                         peer_u[layer], peer_v[layer])
            x = layer_norm(DN_ALPHA * x + c, ln2_g[layer], ln2_b[layer])
            x = per_layer_embed(x, pemb[layer], ple_w_p[layer], ple_w_g[layer], ple_b_g[layer])
            if layer == N_A_LAYERS - 1:
                k_sh = (x @ da_w_k).reshape(B, T, DA_HEADS, 2 * DA_HEAD)
                v_sh = (x @ da_w_v).reshape(B, T, DA_HEADS, DA_VDIM)
                if past_k is None:
                    k_all, v_all = k_sh, v_sh
                else:
                    k_all = jnp.concatenate([past_k.astype(k_sh.dtype), k_sh], axis=1)
                    v_all = jnp.concatenate([past_v.astype(v_sh.dtype), v_sh], axis=1)
        return x, jnp.stack(shifts), jnp.stack(wkvs), k_sh, v_sh

    Bp = x_prompt.shape[0]
    shift0_p = jnp.zeros((N_A_LAYERS, Bp, D_MODEL), x_prompt.dtype)
    wkv0_p = jnp.zeros((N_A_LAYERS, Bp, RW_HEADS, RW_HEAD, RW_HEAD), state_wkv.dtype)
    y_prompt, shift_p, wkv_p, k_p, v_p = layer_stack(x_prompt, p_prompt, shift0_p, wkv0_p, None, None)

    Bs = x_sample.shape[0]
    n_pages = page_table.shape[1]
    past_len = n_pages * cache_k.shape[1]
    past_k = cache_k[page_table].reshape(Bs, past_len, DA_HEADS, 2 * DA_HEAD)
    past_v = cache_v[page_table].reshape(Bs, past_len, DA_HEADS, DA_VDIM)
    y_sample, shift_s, wkv_s, k_s, v_s = layer_stack(x_sample, p_sample, state_shift, state_wkv,
                                                     past_k, past_v)
    return (y_prompt, y_sample, shift_p, wkv_p, k_p, v_p, shift_s, wkv_s, k_s, v_s)
```

```python
import numpy as np
from contextlib import ExitStack
import concourse.bass as bass
import concourse.mybir as mybir
from concourse.bass_utils import run_bass_kernel_spmd

F32 = mybir.dt.float32
BF16 = mybir.dt.bfloat16
I32 = mybir.dt.int32
U32 = mybir.dt.uint32
ALU = mybir.AluOpType
AF = mybir.ActivationFunctionType
AX = mybir.AxisListType

D = 1024
NP_ = 2048
NS = 64
NT = NP_ + NS
NBLK = 17
DN_ALPHA = 4.0 ** 0.25
LN_EPS = 1e-5


def blk_rows(b):
    return (b * 128, 128) if b < 16 else (2048, 64)


class Buf:
    __slots__ = ("w", "r", "x")

    def __init__(self, x=False):
        self.w = None
        self.r = {}
        self.x = x


class Tile:
    def __init__(self, t):
        self.t = t
        self.b = Buf()

    def __getitem__(self, k):
        return self.t[k]


def _b(x):
    return x.b if isinstance(x, Tile) else x


class Prog:
    ENG = ["pe", "dve", "act", "pool", "sp"]
    EPOCH = 30000
    NDMA = 32

    def __init__(self, nc, es):
        self.nc = nc
        self.es = es
        self.ops = {e: [] for e in self.ENG}
        self.cnt = {e: 0 for e in self.ENG}
        self.sems = {e: [] for e in self.ENG}
        self.dsem = [es.enter_context(nc.semaphore(f"dq{i}")) for i in range(self.NDMA)]
        self.dval = [0] * self.NDMA
        self.dnext = 0
        self.seen = {e: {} for e in self.ENG}

    def _sem(self, e, epoch):
        while len(self.sems[e]) <= epoch:
            self.sems[e].append(self.es.enter_context(self.nc.semaphore(f"c_{e}{len(self.sems[e])}")))
        return self.sems[e][epoch]

    def _need(self, e, tok, waits):
        if tok[0] == "c":
            _, e2, idx = tok
            if e2 == e and e == "pe":
                return
            key = ("c", e2)
            if self.seen[e].get(key, 0) >= idx:
                return
            self.seen[e][key] = idx
            waits.append((self._sem(e2, (idx - 1) // self.EPOCH), (idx - 1) % self.EPOCH + 1))
        else:
            _, k, val = tok
            key = ("d", k)
            if self.seen[e].get(key, 0) >= val:
                return
            self.seen[e][key] = val
            waits.append((self.dsem[k], val))

    def _deps(self, e, reads, writes):
        waits = []
        for b in reads:
            if b.w is not None:
                self._need(e, b.w, waits)
            if b.x:
                for key, val in b.r.items():
                    if key[1] != e:
                        self._need(e, (key[0], key[1], val), waits)
        for b in writes:
            if b.w is not None:
                self._need(e, b.w, waits)
            for key, val in b.r.items():
                self._need(e, (key[0], key[1], val), waits)
        return waits

    def _mark(self, tok, reads, writes):
        key = (tok[0], tok[1])
        for b in reads:
            if b.r.get(key, 0) < tok[2]:
                b.r[key] = tok[2]
        for b in writes:
            b.w = tok
            b.r = {}

    def op(self, e, fn, reads=(), writes=()):
        reads = [_b(x) for x in reads]
        writes = [_b(x) for x in writes]
        waits = self._deps(e, reads, writes)
        self.cnt[e] += 1
        idx = self.cnt[e]
        self.ops[e].append((waits, fn, self._sem(e, (idx - 1) // self.EPOCH), 1))
        self._mark(("c", e, idx), reads, writes)

    def dma(self, q, fn, reads=(), writes=()):
        reads = [_b(x) for x in reads]
        writes = [_b(x) for x in writes]
        waits = self._deps(q, reads, writes)
        k = self.dnext
        self.dnext = (k + 1) % self.NDMA
        if self.dval[k] > 0:
            self._need(q, ("d", k, self.dval[k]), waits)
        self.dval[k] += 16
        self.ops[q].append((waits, fn, self.dsem[k], 16))
        self._mark(("d", k, self.dval[k]), reads, writes)

    def barrier(self):
        for e in self.ENG:
            waits = []
            for e2 in self.ENG:
                if e2 != e and self.cnt[e2] > 0:
                    self._need(e, ("c", e2, self.cnt[e2]), waits)
            for k in range(self.NDMA):
                if self.dval[k] > 0:
                    self._need(e, ("d", k, self.dval[k]), waits)
            if waits:
                self.ops[e].append((waits, None, None, 0))

    def emit(self):
        P = self

        def replay(e, eng):
            for waits, fn, sem, inc in P.ops[e]:
                for s, v in waits:
                    eng.wait_ge(s, v)
                if fn is not None:
                    fn(eng).then_inc(sem, inc)
            P.ops[e] = []

        with self.nc.Block() as block:
            @block.tensor
            def _(eng):
                replay("pe", eng)

            @block.vector
            def _(eng):
                replay("dve", eng)

            @block.scalar
            def _(eng):
                replay("act", eng)

            @block.gpsimd
            def _(eng):
                replay("pool", eng)

            @block.sync
            def _(eng):
                replay("sp", eng)


class Ctx:
    def __init__(self, nc, es):
        self.nc = nc
        self.es = es
        self.P = Prog(nc, es)
        self.ses = None
        ps = es.enter_context(nc.psum_tensor("psum_all", [128, 8, 512], F32))
        self.ps = ps
        self.pb = [Buf(True) for _ in range(8)]
        self.dr = {}

    def sb(self, name, shape, dtype=F32):
        return Tile(self.ses.enter_context(self.nc.sbuf_tensor(f"s{self.stage_id}_{name}", shape, dtype)))

    def dram(self, name, shape, dtype=F32, kind="Internal"):
        t = Tile(self.nc.dram_tensor(name, shape, dtype, kind=kind))
        self.dr[name] = t
        return t

    def stage_begin(self):
        self.stage_id = getattr(self, "stage_id", 0) + 1
        self.ses = ExitStack()
        self.ses.__enter__()

    def stage_end(self):
        self.P.barrier()
        self.P.emit()
        self.ses.__exit__(None, None, None)
        self.ses = None

    def mm(self, out, lhsT, rhs, start, stop, r, w):
        self.P.op("pe", lambda e: e.matmul(out, lhsT=lhsT, rhs=rhs, start=start, stop=stop), r, w)

    def tr(self, out, in_, ident, r, w):
        self.P.op("pe", lambda e: e.transpose(out=out, in_=in_, identity=ident), r, w)

    def dma(self, out, in_, r, w, q="sp", slow=False):
        if slow:
            self.P.dma(q, lambda e: e.dma_start(out=out, in_=in_, allow_slow_non_contiguous=True), r, w)
        else:
            self.P.dma(q, lambda e: e.dma_start(out=out, in_=in_), r, w)

    def copy(self, eng, out, in_, r, w):
        if eng == "act":
            self.P.op("act", lambda e: e.copy(out=out, in_=in_), r, w)
        else:
            self.P.op(eng, lambda e: e.tensor_copy(out=out, in_=in_), r, w)

    def actf(self, out, in_, func, r, w, bias=None, scale=None, accum_out=None):
        kw = {}
        if bias is not None:
            kw["bias"] = bias
        if scale is not None:
            kw["scale"] = scale
        if accum_out is not None:
            kw["accum_out"] = accum_out
        self.P.op("act", lambda e: e.activation(out=out, in_=in_, func=func, **kw), r, w)

    def tt(self, eng, out, in0, in1, op, r, w):
        self.P.op(eng, lambda e: e.tensor_tensor(out=out, in0=in0, in1=in1, op=op), r, w)

    def ts(self, eng, out, in0, s1, s2, op0, op1, r, w):
        if op1 is None:
            self.P.op(eng, lambda e: e.tensor_scalar(out=out, in0=in0, scalar1=s1, scalar2=None, op0=op0), r, w)
        else:
            self.P.op(eng, lambda e: e.tensor_scalar(out=out, in0=in0, scalar1=s1, scalar2=s2, op0=op0, op1=op1), r, w)

    def stt(self, out, in0, scalar, in1, op0, op1, r, w):
        self.P.op("dve", lambda e: e.scalar_tensor_tensor(out=out, in0=in0, scalar=scalar, in1=in1, op0=op0, op1=op1), r, w)

    def red(self, out, in_, op, r, w, axis=AX.X):
        self.P.op("dve", lambda e: e.tensor_reduce(out=out, in_=in_, axis=axis, op=op), r, w)

    def memset(self, eng, out, val, w):
        self.P.op(eng, lambda e: e.memset(out, val), [], w)

    def recip(self, out, in_, r, w):
        self.P.op("dve", lambda e: e.reciprocal(out=out, in_=in_), r, w)

    def max8(self, out, in_, r, w):
        self.P.op("dve", lambda e: e.max(out=out, in_=in_), r, w)

    def maxidx(self, out, in_max, in_values, r, w):
        self.P.op("dve", lambda e: e.max_index(out=out, in_max=in_max, in_values=in_values), r, w)

    def mrep(self, out, rep, vals, r, w):
        self.P.op("dve", lambda e: e.match_replace(out=out, in_to_replace=rep, in_values=vals, imm_value=-1e30), r, w)

    def dvefn(self, fn, r, w):
        self.P.op("dve", fn, r, w)


def layer_norm_tok(C, nr, s, gam, bet, outp, scr):
    st, mv, rs = scr["st"], scr["mv"], scr["rs"]
    C.dvefn(lambda e: e.bn_stats(out=st[:nr, 0, :], in_=s[:nr, 0:512]), [s], [st])
    C.dvefn(lambda e: e.bn_stats(out=st[:nr, 1, :], in_=s[:nr, 512:1024]), [s], [st])
    C.dvefn(lambda e: e.bn_aggr(out=mv[:nr, :], in_=st[:nr, :, :].rearrange("p a b -> p (a b)")), [st], [mv])
    C.actf(rs[:nr, :], mv[:nr, 1:2], AF.Sqrt, [mv], [rs], bias=scr["eps"][:nr, 0:1])
    C.recip(rs[:nr, :], rs[:nr, :], [rs], [rs])
    C.ts("dve", outp[:nr, :], s[:nr, :], mv[:nr, 0:1], rs[:nr, 0:1], ALU.subtract, ALU.mult, [s, mv, rs], [outp])
    C.tt("dve", outp[:nr, :], outp[:nr, :], gam[:nr, :], ALU.mult, [outp, gam], [outp])
    C.tt("dve", outp[:nr, :], outp[:nr, :], bet[:nr, :], ALU.add, [outp, bet], [outp])


def to_featmajor(C, nr, xb16, outT, c0, identb, bank):
    pst = C.ps[:, bank, :].bitcast(BF16)
    for kc in range(8):
        C.tr(pst[:, kc * 128:kc * 128 + nr], xb16[:nr, kc * 128:(kc + 1) * 128], identb[:nr, :nr], [xb16, identb], [C.pb[bank]])
    C.copy("act", outT[:, :, c0:c0 + nr], pst.rearrange("p (k t) -> p k t", k=8)[:, :, :nr], [C.pb[bank]], [outT])


def ffn_stage_a(C, L, W, x_in, h_in, xa_d, xaT_d, idx_d):
    nc = C.nc
    C.stage_begin()
    cst = C.cst
    identf, identb, iotaf = load_consts(C)
    g1 = C.sb("g1", [128, D]); b1 = C.sb("b1", [128, D])
    C.dma(g1[:], W["ln1_g"][L:L + 1, :].partition_broadcast(128), [], [g1])
    C.dma(b1[:], W["ln1_b"][L:L + 1, :].partition_broadcast(128), [], [b1])
    wq = C.sb("wq", [128, 8, 2048], BF16)
    wq_src = W["peer_w_q"][L].rearrange("(k p) n -> p k n", p=128)
    for i in range(4):
        C.dma(wq[:, :, i * 512:(i + 1) * 512], wq_src[:, :, i * 512:(i + 1) * 512], [], [wq], q="pool")
    bq = C.sb("bq", [128, 16])
    C.dma(bq[:], W["peer_b_qT"][L], [], [bq])
    skT = C.sb("skT", [128, 2, 128], BF16)
    C.dma(skT[:], W["peer_skT"][L], [], [skT], q="pool")
    eps = C.sb("eps", [128, 1]); C.memset("dve", eps[:], LN_EPS, [eps])
    scr = {"st": C.sb("st", [128, 2, 6]), "mv": C.sb("mv", [128, 2]), "rs": C.sb("rs", [128, 1]), "eps": eps}
    NB = 2
    xin = [C.sb(f"xin{i}", [128, D]) for i in range(NB)]
    hin = [C.sb(f"hin{i}", [128, D]) for i in range(NB)]
    xa = [C.sb(f"xa{i}", [128, D]) for i in range(NB)]
    xab = [C.sb(f"xab{i}", [128, D], BF16) for i in range(NB)]
    xaT = [C.sb(f"xaT{i}", [128, 8, 128], BF16) for i in range(NB)]
    qT = [C.sb(f"qT{i}", [128, 16, 128], BF16) for i in range(NB)]
    sc = [C.sb(f"sc{i}", [128, 16, 128]) for i in range(NB)]
    top = C.sb("top", [128, 16, 16]); idxu = C.sb("idxu", [128, 16, 16], U32); idxf = C.sb("idxf", [128, 16, 16])
    cand = C.sb("cand", [128, 8, 256])
    sel = C.sb("sel", [128, 8, 16]); cidx = C.sb("cidx", [128, 8, 16], U32); cf = C.sb("cf", [128, 8, 16])
    thr = C.sb("thr", [128, 15])
    for m in range(15):
        C.memset("pool", thr[:, m:m + 1], 16.0 * (m + 1) - 0.5, [thr])
    ge = C.sb("ge", [128, 128, 15]); af = C.sb("af", [128, 8, 16]); bf_ = C.sb("bf_", [128, 8, 16])
    oh = C.sb("oh", [128, 8, 16, 16])
    res3 = [C.sb(f"res3_{i}", [128, 3, 128]) for i in range(NB)]
    mx = C.sb("mx", [128, 8]); zs = C.sb("zs", [128, 8])
    resT = [C.sb(f"resT{i}", [128, 3, 128]) for i in range(NB)]

    for blk in range(NBLK):
        r0, nr = blk_rows(blk)
        s_ = blk % NB
        X, H, XA, XAB, XAT, QT, SC, R3, RT = xin[s_], hin[s_], xa[s_], xab[s_], xaT[s_], qT[s_], sc[s_], res3[s_], resT[s_]
        C.dma(X[:nr, :], x_in[r0:r0 + nr, :], [x_in], [X])
        C.dma(H[:nr, :], h_in[r0:r0 + nr, :], [h_in], [H])
        C.stt(X[:nr, :], X[:nr, :], DN_ALPHA, H[:nr, :], ALU.mult, ALU.add, [X, H], [X])
        layer_norm_tok(C, nr, X, g1, b1, XA, scr)
        C.dma(xa_d[r0:r0 + nr, :], XA[:nr, :], [XA], [xa_d])
        C.copy("act", XAB[:nr, :], XA[:nr, :], [XA], [XAB])
        to_featmajor(C, nr, XAB, XAT, 0, identb, 0)
        C.dma(xaT_d[:, :, r0:r0 + nr], XAT[:, :, :nr], [XAT], [xaT_d])
        for c16 in range(16):
            bank = 1 + (c16 // 4) % 2
            pq = C.ps[:, bank, (c16 % 4) * 128:(c16 % 4) * 128 + nr]
            for kc in range(8):
                C.mm(pq, wq[:, kc, c16 * 128:(c16 + 1) * 128], XAT[:, kc, :nr], kc == 0, kc == 7, [wq, XAT], [C.pb[bank]])
            C.actf(QT[:, c16, :nr], pq, AF.Identity, [C.pb[bank], bq], [QT], bias=bq[:, c16:c16 + 1])
        for c16 in range(16):
            bank = 3 + c16 // 4
            C.mm(C.ps[:nr, bank, (c16 % 4) * 128:(c16 % 4 + 1) * 128], QT[:, c16, :nr], skT[:, c16 % 2, :], True, True, [QT, skT], [C.pb[bank]])
        for bk in range(4):
            C.copy("act" if bk % 2 else "dve", SC[:nr, bk * 4:(bk + 1) * 4, :], C.ps[:nr, 3 + bk, :].rearrange("p (a n) -> p a n", a=4), [C.pb[3 + bk]], [SC])
        if getattr(C, "dbg", None) and blk == 0:
            C.dma(C.dbg["sc"][:], SC[:], [SC], [C.dbg["sc"]])
        for sg in range(16):
            C.max8(top[:nr, sg, 0:8], SC[:nr, sg, :], [SC], [top])
            C.maxidx(idxu[:nr, sg, 0:8], top[:nr, sg, 0:8], SC[:nr, sg, :], [SC, top], [idxu])
            C.mrep(SC[:nr, sg, :], top[:nr, sg, 0:8], SC[:nr, sg, :], [SC, top], [SC])
            C.max8(top[:nr, sg, 8:16], SC[:nr, sg, :], [SC], [top])
            C.maxidx(idxu[:nr, sg, 8:16], top[:nr, sg, 8:16], SC[:nr, sg, :], [SC, top], [idxu])
        C.copy("dve", idxf[:nr], idxu[:nr], [idxu], [idxf])
        tv = top[:nr].rearrange("p (h c) k -> p h c k", c=2)
        C.tt("dve", cand[:nr].rearrange("p h (a b) -> p h a b", a=16), tv[:, :, 0, :].unsqueeze(3).to_broadcast([nr, 8, 16, 16]),
             tv[:, :, 1, :].unsqueeze(2).to_broadcast([nr, 8, 16, 16]), ALU.add, [top], [cand])
        for h in range(8):
            C.max8(sel[:nr, h, 0:8], cand[:nr, h, :], [cand], [sel])
            C.maxidx(cidx[:nr, h, 0:8], sel[:nr, h, 0:8], cand[:nr, h, :], [cand, sel], [cidx])
            C.mrep(cand[:nr, h, :], sel[:nr, h, 0:8], cand[:nr, h, :], [cand, sel], [cand])
            C.max8(sel[:nr, h, 8:16], cand[:nr, h, :], [cand], [sel])
            C.maxidx(cidx[:nr, h, 8:16], sel[:nr, h, 8:16], cand[:nr, h, :], [cand, sel], [cidx])
        C.copy("dve", cf[:nr], cidx[:nr], [cidx], [cf])
        if getattr(C, "dbg", None) and blk == 0:
            C.dma(C.dbg["top"][:], top[:], [top], [C.dbg["top"]])
            C.dma(C.dbg["idxf"][:], idxf[:], [idxf], [C.dbg["idxf"]])
            C.dma(C.dbg["sel"][:], sel[:], [sel], [C.dbg["sel"]])
            C.dma(C.dbg["cf"][:], cf[:], [cf], [C.dbg["cf"]])
        cfl = cf[:nr].rearrange("p h k -> p (h k)")
        C.tt("dve", ge[:nr], cfl.unsqueeze(2).to_broadcast([nr, 128, 15]), thr[:nr, :].unsqueeze(1).to_broadcast([nr, 128, 15]), ALU.is_ge, [cf, thr], [ge])
        C.red(af[:nr].rearrange("p h k -> p (h k)"), ge[:nr], ALU.add, [ge], [af])
        C.stt(bf_[:nr], af[:nr], -16.0, cf[:nr], ALU.mult, ALU.add, [af, cf], [bf_])
        iv = idxf[:nr].rearrange("p (h c) k -> p h c k", c=2)
        for which, sel_ab in ((0, af), (1, bf_)):
            C.tt("dve", oh[:nr], sel_ab[:nr].unsqueeze(3).to_broadcast([nr, 8, 16, 16]),
                 iotaf[:nr, 0:16].unsqueeze(1).unsqueeze(1).to_broadcast([nr, 8, 16, 16]), ALU.is_equal, [sel_ab, iotaf], [oh])
            C.tt("dve", oh[:nr], oh[:nr], iv[:, :, which, :].unsqueeze(2).to_broadcast([nr, 8, 16, 16]), ALU.mult, [oh, idxf], [oh])
            C.red(R3[:nr, which, :], oh[:nr].rearrange("p h k a -> p (h k) a"), ALU.add, [oh], [R3])
        C.ts("dve", mx[:nr, :], sel[:nr, :, 0], -1.0, None, ALU.mult, None, [sel], [mx])
        gv = R3[:nr, 2, :].rearrange("p (h k) -> p h k", h=8)
        C.tt("dve", gv, sel[:nr], mx[:nr, :].unsqueeze(2).to_broadcast([nr, 8, 16]), ALU.add, [sel, mx], [R3])
        C.actf(gv, gv, AF.Exp, [R3], [R3])
        C.red(zs[:nr, :], gv, ALU.add, [R3], [zs])
        C.recip(zs[:nr, :], zs[:nr, :], [zs], [zs])
        C.tt("dve", gv, gv, zs[:nr, :].unsqueeze(2).to_broadcast([nr, 8, 16]), ALU.mult, [R3, zs], [R3])
        for q3 in range(3):
            C.tr(C.ps[:, 7, q3 * 128:q3 * 128 + nr], R3[:nr, q3, :], identf[:nr, :nr], [R3, identf], [C.pb[7]])
        C.copy("act", RT[:, :, :nr], C.ps[:, 7, 0:384].rearrange("p (a t) -> p a t", a=3)[:, :, :nr], [C.pb[7]], [RT])
        C.dma(idx_d[:, :, r0:r0 + nr], RT[:, :, :nr], [RT], [idx_d])
    C.stage_end()


def load_consts(C):
    identf = C.sb("identf", [128, 128]); identb = C.sb("identb", [128, 128], BF16); iotaf = C.sb("iotaf", [128, 128])
    C.dma(identf[:], C.cst["ident"], [], [identf])
    C.dma(iotaf[:], C.cst["iota"], [], [iotaf])
    C.copy("dve", identb[:], identf[:], [identf], [identb])
    return identf, identb, iotaf


CHUNKS = [[0, 1, 2], [3, 4, 5], [6, 7, 8], [9, 10, 11], [12, 13, 14], [15, 16]]


def ffn_stage_b(C, L, W, xaT_d, idx_d, c_d, chunks=None):
    C.stage_begin()
    identf, identb, iotaf = load_consts(C)
    TC = 384
    Gall = C.sb("Gall", [128, 128, TC], BF16)
    xaT = C.sb("xaTc", [128, 8, TC], BF16)
    idxT = C.sb("idxTc", [128, 3, TC])
    NAB = 4
    At = [C.sb(f"At{i}", [128, 128], BF16) for i in range(NAB)]
    Bt = [C.sb(f"Bt{i}", [128, 128], BF16) for i in range(NAB)]
    NW = 3
    ut = [C.sb(f"ut{i}", [128, 8, 128], BF16) for i in range(NW)]
    vt = [C.sb(f"vt{i}", [128, D], BF16) for i in range(NW)]
    hg = [C.sb(f"hg{i}", [128, TC], BF16) for i in range(2)]
    actT = [C.sb(f"actT{i}", [128, TC], BF16) for i in range(2)]
    cout = [C.sb(f"cout{i}", [128, D]) for i in range(2)]
    U_src = W["peer_uh"][L]
    V_src = W["peer_v"][L]
    for ch in (chunks if chunks is not None else CHUNKS):
        cols = []
        c0 = 0
        for blk in ch:
            r0, nr = blk_rows(blk)
            cols.append((blk, r0, nr, c0))
            c0 += nr
        tc = c0
        R0 = cols[0][1]
        C.dma(xaT[:, :, :tc], xaT_d[:, :, R0:R0 + tc], [xaT_d], [xaT])
        C.dma(idxT[:, :, :tc], idx_d[:, :, R0:R0 + tc], [idx_d], [idxT])
        for t in range(tc):
            s_ = t % NAB
            C.ts("dve", At[s_][:], iotaf[:], idxT[:, 0, t:t + 1], idxT[:, 2, t:t + 1], ALU.is_equal, ALU.mult, [iotaf, idxT], [At[s_]])
            C.ts("dve", Bt[s_][:], iotaf[:], idxT[:, 1, t:t + 1], None, ALU.is_equal, None, [iotaf, idxT], [Bt[s_]])
            bank = (t // 4) % 2
            C.mm(C.ps[:, bank, (t % 4) * 128:(t % 4 + 1) * 128], Bt[s_][:], At[s_][:], True, True, [At[s_], Bt[s_]], [C.pb[bank]])
            if t % 4 == 3 or t == tc - 1:
                n = t % 4 + 1
                t0 = t - (n - 1)
                C.copy("act", Gall[:, :, t0:t0 + n], C.ps[:, bank, 0:n * 128].rearrange("p (q i) -> p i q", q=n), [C.pb[bank]], [Gall])
        for i1 in range(128):
            w_ = i1 % NW
            C.dma(ut[w_][:], U_src[i1].rearrange("p (k e) -> p k e", k=8), [], [ut[w_]], q="pool")
            C.dma(vt[w_][:], V_src[i1 * 128:(i1 + 1) * 128, :], [], [vt[w_]], q="pool")
            hb = 6 + i1 % 2
            for kc in range(8):
                C.mm(C.ps[:, hb, :tc], ut[w_][:, kc, :], xaT[:, kc, :tc], kc == 0, kc == 7, [ut[w_], xaT], [C.pb[hb]])
            HG, AT = hg[i1 % 2], actT[i1 % 2]
            C.actf(HG[:, :tc], C.ps[:, hb, :tc], AF.Gelu_apprx_tanh, [C.pb[hb]], [HG])
            C.tt("dve", AT[:, :tc], HG[:, :tc], Gall[:, i1, :tc], ALU.mult, [HG, Gall], [AT])
            for bi, (blk, r0, nr, cc) in enumerate(cols):
                for nh in range(2):
                    bank = 2 * bi + nh
                    C.mm(C.ps[:nr, bank, :], AT[:, cc:cc + nr], vt[w_][:, nh * 512:(nh + 1) * 512], i1 == 0, i1 == 127, [AT, vt[w_]], [C.pb[bank]])
        for bi, (blk, r0, nr, cc) in enumerate(cols):
            co = cout[bi % 2]
            C.copy("act", co[:nr, 0:512], C.ps[:nr, 2 * bi, :], [C.pb[2 * bi]], [co])
            C.copy("dve", co[:nr, 512:1024], C.ps[:nr, 2 * bi + 1, :], [C.pb[2 * bi + 1]], [co])
            C.dma(c_d[r0:r0 + nr, :], co[:nr, :], [co], [c_d])
    C.stage_end()


def ffn_stage_c(C, L, W, xa_d, c_d, pT_d, x_out, xoT_d=None):
    C.stage_begin()
    identf, identb, iotaf = load_consts(C)
    g2 = C.sb("g2", [128, D]); b2 = C.sb("b2", [128, D])
    C.dma(g2[:], W["ln2_g"][L:L + 1, :].partition_broadcast(128), [], [g2])
    C.dma(b2[:], W["ln2_b"][L:L + 1, :].partition_broadcast(128), [], [b2])
    wg = C.sb("wg", [128, 8, D], BF16)
    wg_src = W["ple_w_g"][L].rearrange("(k p) n -> p k n", p=128)
    for i in range(2):
        C.dma(wg[:, :, i * 512:(i + 1) * 512], wg_src[:, :, i * 512:(i + 1) * 512], [], [wg], q="pool")
    wp = C.sb("wp", [128, 2, D], BF16)
    C.dma(wp[:], W["ple_w_p"][L].rearrange("(k p) n -> p k n", p=128), [], [wp], q="pool")
    bg = C.sb("bg", [1, D], BF16)
    C.dma(bg[:], W["ple_b_g"][L:L + 1, :], [], [bg], q="pool")
    ones = C.sb("ones1", [1, 128], BF16); C.memset("dve", ones[:], 1.0, [ones])
    eps = C.sb("eps", [128, 1]); C.memset("dve", eps[:], LN_EPS, [eps])
    scr = {"st": C.sb("st", [128, 2, 6]), "mv": C.sb("mv", [128, 2]), "rs": C.sb("rs", [128, 1]), "eps": eps}
    NB = 2
    xa = [C.sb(f"xa{i}", [128, D]) for i in range(NB)]
    cc = [C.sb(f"cc{i}", [128, D]) for i in range(NB)]
    xb = [C.sb(f"xb{i}", [128, D]) for i in range(NB)]
    xbb = [C.sb(f"xbb{i}", [128, D], BF16) for i in range(NB)]
    xbT = [C.sb(f"xbT{i}", [128, 8, 128], BF16) for i in range(NB)]
    pT = [C.sb(f"pT{i}", [128, 2, 128], BF16) for i in range(NB)]
    sig = [C.sb(f"sig{i}", [128, D]) for i in range(NB)]
    xo = [C.sb(f"xo{i}", [128, D]) for i in range(NB)]
    xob = [C.sb(f"xob{i}", [128, D], BF16) for i in range(NB)]
    xoT = [C.sb(f"xoT{i}", [128, 8, 128], BF16) for i in range(NB)]
    for blk in range(NBLK):
        r0, nr = blk_rows(blk)
        s_ = blk % NB
        XA, CC, XB, XBB, XBT, PT, SG, XO = xa[s_], cc[s_], xb[s_], xbb[s_], xbT[s_], pT[s_], sig[s_], xo[s_]
        pbase = 4 * s_
        C.dma(XA[:nr, :], xa_d[r0:r0 + nr, :], [xa_d], [XA])
        C.dma(CC[:nr, :], c_d[r0:r0 + nr, :], [c_d], [CC])
        C.dma(PT[:, :, :nr], pT_d[:, r0:r0 + nr].rearrange("(k p) t -> p k t", p=128), [], [PT], q="pool")
        C.stt(XA[:nr, :], XA[:nr, :], DN_ALPHA, CC[:nr, :], ALU.mult, ALU.add, [XA, CC], [XA])
        layer_norm_tok(C, nr, XA, g2, b2, XB, scr)
        C.copy("act", XBB[:nr, :], XB[:nr, :], [XB], [XBB])
        to_featmajor(C, nr, XBB, XBT, 0, identb, pbase)
        for nh in range(2):
            bank = pbase + nh
            for kc in range(8):
                C.mm(C.ps[:nr, bank, :], XBT[:, kc, :nr], wg[:, kc, nh * 512:(nh + 1) * 512], kc == 0, False, [XBT, wg], [C.pb[bank]])
            C.mm(C.ps[:nr, bank, :], ones[0:1, :nr], bg[0:1, nh * 512:(nh + 1) * 512], False, True, [ones, bg], [C.pb[bank]])
            C.actf(SG[:nr, nh * 512:(nh + 1) * 512], C.ps[:nr, bank, :], AF.Sigmoid, [C.pb[bank]], [SG])
        for nh in range(2):
            bank = pbase + 2 + nh
            for kc in range(2):
                C.mm(C.ps[:nr, bank, :], PT[:, kc, :nr], wp[:, kc, nh * 512:(nh + 1) * 512], kc == 0, kc == 1, [PT, wp], [C.pb[bank]])
            C.tt("dve", SG[:nr, nh * 512:(nh + 1) * 512], SG[:nr, nh * 512:(nh + 1) * 512], C.ps[:nr, bank, :], ALU.mult, [SG, C.pb[bank]], [SG])
        C.tt("dve", XO[:nr, :], SG[:nr, :], XB[:nr, :], ALU.add, [SG, XB], [XO])
        C.dma(x_out[r0:r0 + nr, :], XO[:nr, :], [XO], [x_out])
        if xoT_d is not None:
            XOB, XOT = xob[s_], xoT[s_]
            C.copy("act", XOB[:nr, :], XO[:nr, :], [XO], [XOB])
            to_featmajor(C, nr, XOB, XOT, 0, identb, pbase)
            C.dma(xoT_d[:, :, r0:r0 + nr], XOT[:, :, :nr], [XOT], [xoT_d])
    C.stage_end()


def make_consts():
    ar = np.arange(128)
    pm = np.zeros((128, 2), np.float32)
    pm[:64, 0] = 1.0
    pm[64:, 1] = 1.0
    return {
        "ident": np.eye(128, dtype=np.float32),
        "iota": np.ascontiguousarray(np.broadcast_to(np.arange(128, dtype=np.float32)[None, :], (128, 128))),
        "lincl": (ar[:, None] <= ar[None, :]).astype(np.float32),
        "lstrict": (ar[:, None] < ar[None, :]).astype(np.float32),
        "llow": (ar[None, :] < ar[:, None]).astype(np.float32),
        "pm": pm,
        **attn_consts(),
    }


def attn_consts():
    p = np.arange(128, dtype=np.float64)
    slopes = 2.0 ** (-8.0 * np.arange(1, 9, dtype=np.float64) / 8.0)
    btab = slopes[None, :, None] * (p[:, None, None] - 127.0 - 128.0 * np.arange(16)[None, None, :])
    q = np.arange(4, dtype=np.float64)
    tabS = np.zeros((128, 16, 8, 2, 4))
    tabS += slopes[None, None, :, None, None] * (np.arange(16)[None, :, None, None, None] * 128.0 + p[:, None, None, None, None] - 2048.0 - q[None, None, None, None, :])
    tabN = np.full((128, 8, 2, 4), -1e30)
    for pp in range(4):
        for qq in range(4):
            if pp <= qq:
                tabN[pp, :, :, qq] = slopes[:, None] * (pp - qq)
    cA = np.zeros((8, 124)); cB = np.zeros((8, 124))
    for i in range(4):
        cA[i, 60 + i] = 1.0
        cB[4 + i, 60 + i] = -1.0
    f = lambda a: np.ascontiguousarray(a.astype(np.float32))
    return {"btab": f(btab), "tabS": f(tabS.reshape(128, 16, 64)), "tabN": f(tabN.reshape(128, 64)), "cmbA": f(cA), "cmbB": f(cB),
            "iotap": f(p.reshape(128, 1))}


def prep_weights(w):
    o = {}
    for k in ("ln1_g", "ln1_b", "ln2_g", "ln2_b", "peer_w_q", "peer_v", "ple_w_p", "ple_w_g", "ple_b_g"):
        if k in w:
            o[k] = np.ascontiguousarray(w[k])
    for k in ("rw_w_r", "rw_w_k", "rw_w_v", "rw_w0", "rw_w1", "rw_w2", "rw_a0", "rw_a1", "rw_a2", "rw_g1", "rw_g2", "rw_k_k", "rw_k_a",
              "rw_gn_g", "rw_gn_b", "rw_w_o", "da_w_k", "da_w_v", "da_w_q", "da_lam", "da_subln_g", "da_w_o"):
        if k in w:
            o[k] = np.ascontiguousarray(w[k])
    if "rw_mu" in w:
        o["rw_muT"] = np.ascontiguousarray(w["rw_mu"][0].reshape(6, 8, 128).transpose(2, 0, 1))
        o["rw_r_kf"] = np.ascontiguousarray(w["rw_r_k"].reshape(1, 1024))
    if "peer_b_q" in w:
        o["peer_b_qT"] = np.ascontiguousarray(w["peer_b_q"].reshape(2, 16, 128).transpose(0, 2, 1))
        o["peer_skT"] = np.ascontiguousarray(w["peer_subkeys"].transpose(0, 3, 1, 2))
        u = w["peer_u"].reshape(2, 128, 128, 8, 128)
        o["peer_uh"] = np.ascontiguousarray(u.transpose(0, 1, 4, 3, 2)).reshape(2, 128, 128, 1024)
    return o


class BankRot:
    def __init__(self):
        self.i1 = 0
        self.i2 = 0

    def one(self):
        b = self.i1 % 8
        self.i1 += 1
        return b

    def two(self):
        b = (self.i2 % 4) * 2
        self.i2 += 1
        return b


def rw_stage_a(C, W, x0T_d, xpT_d, sc_d, blks=None):
    import os
    _STOP = int(os.environ.get('RWA_STOP', '99'))
    C.stage_begin()
    identf, identb, iotaf = load_consts(C)
    BR = BankRot()
    wts = {}
    for nm in ("rw_w_r", "rw_w_k", "rw_w_v"):
        t = C.sb(nm, [128, 8, D], BF16)
        src = W[nm][0].rearrange("(k p) n -> p k n", p=128)
        for i in range(2):
            C.dma(t[:, :, i * 512:(i + 1) * 512], src[:, :, i * 512:(i + 1) * 512], [], [t], q="pool")
        wts[nm] = t
    w1 = C.sb("w1", [128, 8, 64], BF16); C.dma(w1[:], W["rw_w1"][0].rearrange("(k p) n -> p k n", p=128), [], [w1], q="pool")
    a1 = C.sb("a1", [128, 8, 64], BF16); C.dma(a1[:], W["rw_a1"][0].rearrange("(k p) n -> p k n", p=128), [], [a1], q="pool")
    g1 = C.sb("g1w", [128, 8, 128], BF16); C.dma(g1[:], W["rw_g1"][0].rearrange("(k p) n -> p k n", p=128), [], [g1], q="pool")
    w2 = C.sb("w2", [64, D], BF16); C.dma(w2[:], W["rw_w2"][0], [], [w2], q="pool")
    a2 = C.sb("a2", [64, D], BF16); C.dma(a2[:], W["rw_a2"][0], [], [a2], q="pool")
    g2 = C.sb("g2w", [128, D], BF16); C.dma(g2[:], W["rw_g2"][0], [], [g2], q="pool")
    bc = {}
    for nm in ("rw_w0", "rw_a0", "rw_k_k", "rw_k_a", "rw_r_kf"):
        t = C.sb("b_" + nm, [128, D])
        C.dma(t[:], W[nm][0:1, :].partition_broadcast(128), [], [t])
        bc[nm] = t
    mu = C.sb("mu", [128, 6, 8]); C.dma(mu[:], W["rw_muT"], [], [mu])
    Lt = C.sb("Lt", [128, 128]); C.dma(Lt[:], C.cst["lincl"], [], [Lt])
    onec = C.sb("onec", [128, 2]); C.memset("dve", onec[:], 1.0, [onec])
    xc = C.sb("xc", [128, 8, 128]); xp = C.sb("xp", [128, 8, 128]); xx = C.sb("xx", [128, 8, 128])
    mix = [C.sb(f"mix{i}", [128, 8, 128], BF16) for i in range(6)]
    hwT = C.sb("hwT", [64, 128], BF16); haT = C.sb("haT", [64, 128], BF16); hgT = C.sb("hgT", [128, 128], BF16)
    r_ = C.sb("r_", [128, D]); k_ = C.sb("k_", [128, D]); v_ = C.sb("v_", [128, D]); lw = C.sb("lw", [128, D]); a_ = C.sb("a_", [128, D]); g_ = C.sb("g_", [128, D])
    kk = C.sb("kk", [128, D]); t1 = C.sb("t1", [128, D]); t2 = C.sb("t2", [128, D]); t3 = C.sb("t3", [128, D])
    ss = C.sb("ss", [128, 16]); bon = C.sb("bon", [128, 16]); wc = C.sb("wc", [128, 8])
    ob = [C.sb(f"ob{i}", [128, D], BF16) for i in range(5)]

    def v3(t, nr):
        return t[:nr, :].rearrange("p (h k) -> p h k", h=16)

    for blk in (range(NBLK) if blks is None else blks):
        r0, nr = blk_rows(blk)
        smp = blk == 16
        xsrc = x0T_d.t.rearrange("(k p) t -> p k t", p=128)
        psrc = xpT_d.t.rearrange("(k p) t -> p k t", p=128)
        C.dma(xc[:, :, :nr], xsrc[:, :, r0:r0 + nr], [], [xc])
        C.dma(xp[:, :, :nr], psrc[:, :, r0:r0 + nr], [], [xp])
        C.tt("dve", xx[:, :, :nr], xp[:, :, :nr], xc[:, :, :nr], ALU.subtract, [xp, xc], [xx])
        for i in range(6):
            for kc in range(8):
                C.stt(mix[i][:, kc, :nr], xx[:, kc, :nr], mu[:, i, kc:kc + 1], xc[:, kc, :nr], ALU.mult, ALU.add, [xx, xc, mu], [mix[i]])
        if _STOP == 1:
            continue
        xr, xw, xk, xv, xa, xg = mix
        for nm, xm, dst in (("rw_w_r", xr, r_), ("rw_w_k", xk, k_), ("rw_w_v", xv, v_)):
            b2 = BR.two()
            for nh in range(2):
                for kc in range(8):
                    C.mm(C.ps[:nr, b2 + nh, :], xm[:, kc, :nr], wts[nm][:, kc, nh * 512:(nh + 1) * 512], kc == 0, kc == 7, [xm, wts[nm]], [C.pb[b2 + nh]])
                C.copy("act", dst[:nr, nh * 512:(nh + 1) * 512], C.ps[:nr, b2 + nh, :], [C.pb[b2 + nh]], [dst])
        if _STOP == 2:
            continue
        b1 = BR.one()
        for kc in range(8):
            C.mm(C.ps[:64, b1, :nr], w1[:, kc, :], xw[:, kc, :nr], kc == 0, kc == 7, [w1, xw], [C.pb[b1]])
        C.actf(hwT[:, :nr], C.ps[:64, b1, :nr], AF.Tanh, [C.pb[b1]], [hwT])
        b2 = BR.two()
        for nh in range(2):
            C.mm(C.ps[:nr, b2 + nh, :], hwT[:, :nr], w2[:, nh * 512:(nh + 1) * 512], True, True, [hwT, w2], [C.pb[b2 + nh]])
            C.tt("dve", lw[:nr, nh * 512:(nh + 1) * 512], C.ps[:nr, b2 + nh, :], bc["rw_w0"][:nr, nh * 512:(nh + 1) * 512], ALU.add, [C.pb[b2 + nh], bc["rw_w0"]], [lw])
        C.actf(lw[:nr, :], lw[:nr, :], AF.Sigmoid, [lw], [lw])
        C.ts("dve", lw[:nr, :], lw[:nr, :], -float(np.exp(-0.5)), None, ALU.mult, None, [lw], [lw])
        if _STOP == 3:
            continue
        b1 = BR.one()
        for kc in range(8):
            C.mm(C.ps[:64, b1, :nr], a1[:, kc, :], xa[:, kc, :nr], kc == 0, kc == 7, [a1, xa], [C.pb[b1]])
        C.copy("act", haT[:, :nr], C.ps[:64, b1, :nr], [C.pb[b1]], [haT])
        b2 = BR.two()
        for nh in range(2):
            C.mm(C.ps[:nr, b2 + nh, :], haT[:, :nr], a2[:, nh * 512:(nh + 1) * 512], True, True, [haT, a2], [C.pb[b2 + nh]])
            C.tt("dve", a_[:nr, nh * 512:(nh + 1) * 512], C.ps[:nr, b2 + nh, :], bc["rw_a0"][:nr, nh * 512:(nh + 1) * 512], ALU.add, [C.pb[b2 + nh], bc["rw_a0"]], [a_])
        C.actf(a_[:nr, :], a_[:nr, :], AF.Sigmoid, [a_], [a_])
        if _STOP == 4:
            continue
        b1 = BR.one()
        for kc in range(8):
            C.mm(C.ps[:, b1, :nr], g1[:, kc, :], xg[:, kc, :nr], kc == 0, kc == 7, [g1, xg], [C.pb[b1]])
        C.actf(hgT[:, :nr], C.ps[:, b1, :nr], AF.Sigmoid, [C.pb[b1]], [hgT])
        b2 = BR.two()
        for nh in range(2):
            C.mm(C.ps[:nr, b2 + nh, :], hgT[:, :nr], g2[:, nh * 512:(nh + 1) * 512], True, True, [hgT, g2], [C.pb[b2 + nh]])
            C.copy("act", g_[:nr, nh * 512:(nh + 1) * 512], C.ps[:nr, b2 + nh, :], [C.pb[b2 + nh]], [g_])
        C.dma(sc_d["g32"][r0:r0 + nr, :], g_[:nr, :], [g_], [sc_d["g32"]])
        C.dma(sc_d["v32"][r0:r0 + nr, :], v_[:nr, :], [v_], [sc_d["v32"]])
        if _STOP == 5:
            continue
        C.tt("pool", kk[:nr, :], k_[:nr, :], bc["rw_k_k"][:nr, :], ALU.mult, [k_, bc["rw_k_k"]], [kk])
        C.tt("dve", t1[:nr, :], kk[:nr, :], kk[:nr, :], ALU.mult, [kk], [t1])
        C.red(ss[:nr, :], v3(t1, nr), ALU.add, [t1], [ss])
        C.actf(ss[:nr, :], ss[:nr, :], AF.Sqrt, [ss], [ss])
        C.ts("dve", ss[:nr, :], ss[:nr, :], 1e-12, None, ALU.max, None, [ss], [ss])
        C.recip(ss[:nr, :], ss[:nr, :], [ss], [ss])
        C.tt("dve", v3(kk, nr), v3(kk, nr), ss[:nr, :].unsqueeze(2).to_broadcast([nr, 16, 64]), ALU.mult, [kk, ss], [kk])
        if _STOP == 6:
            continue
        C.ts("dve", t1[:nr, :], a_[:nr, :], -1.0, None, ALU.add, None, [a_], [t1])
        C.tt("pool", t1[:nr, :], t1[:nr, :], bc["rw_k_a"][:nr, :], ALU.mult, [t1, bc["rw_k_a"]], [t1])
        C.tt("dve", t1[:nr, :], t1[:nr, :], k_[:nr, :], ALU.mult, [t1, k_], [t1])
        C.tt("dve", k_[:nr, :], k_[:nr, :], t1[:nr, :], ALU.add, [t1, k_], [k_])
        C.tt("pool", t1[:nr, :], r_[:nr, :], bc["rw_r_kf"][:nr, :], ALU.mult, [r_, bc["rw_r_kf"]], [t1])
        C.tt("dve", t1[:nr, :], t1[:nr, :], k_[:nr, :], ALU.mult, [t1, k_], [t1])
        C.red(bon[:nr, :], v3(t1, nr), ALU.add, [t1], [bon])
        C.dma(sc_d["bon"][r0:r0 + nr, :], bon[:nr, :], [bon], [sc_d["bon"]])
        C.tt("dve", a_[:nr, :], a_[:nr, :], kk[:nr, :], ALU.mult, [a_, kk], [a_])
        if _STOP == 7:
            continue
        if smp:
            C.actf(lw[:nr, :], lw[:nr, :], AF.Exp, [lw], [lw])
            C.ts("dve", kk[:nr, :], kk[:nr, :], -1.0, None, ALU.mult, None, [kk], [kk])
            for qi, src in enumerate((r_, lw, k_, v_, kk, a_)):
                dst = sc_d["smp"].t[qi].rearrange("h b t k -> (b t) h k")
                C.dma(dst, v3(src, nr), [src], [sc_d["smp"]])
            continue
        b2 = BR.two()
        for nh in range(2):
            C.mm(C.ps[:nr, b2 + nh, :], Lt[:nr, :nr], lw[:nr, nh * 512:(nh + 1) * 512], True, True, [Lt, lw], [C.pb[b2 + nh]])
        if _STOP == 71:
            continue
        for nh in range(2):
            sl = slice(nh * 512, (nh + 1) * 512)
            pc = C.ps[:nr, b2 + nh, :]
            pbs = [C.pb[b2 + nh]]
            _SUB = os.environ.get("RWA_SUB", "1234")
            if "1" in _SUB:
                C.actf(t1[:nr, sl], pc, AF.Exp, pbs, [t1])
            if "2" in _SUB:
                C.actf(t2[:nr, sl], pc, AF.Exp, pbs, [t2], scale=-1.0)
            if "3" in _SUB:
                C.tt("dve", t3[:nr, sl], pc, lw[:nr, sl], ALU.subtract, pbs + [lw], [t3])
        if "4" in _SUB:
            C.actf(t3[:nr, :], t3[:nr, :], AF.Exp, [t3], [t3])
        if _STOP == 72:
            continue
        C.tt("dve", ob[0][:nr, :], r_[:nr, :], t1[:nr, :], ALU.mult, [r_, t1], [ob[0]])
        C.tt("pool", ob[1][:nr, :], k_[:nr, :], t2[:nr, :], ALU.mult, [k_, t2], [ob[1]])
        C.stt(ob[2][:nr, :], kk[:nr, :], -1.0, t3[:nr, :], ALU.mult, ALU.mult, [kk, t3], [ob[2]])
        C.tt("pool", ob[3][:nr, :], a_[:nr, :], t2[:nr, :], ALU.mult, [a_, t2], [ob[3]])
        C.copy("act", ob[4][:nr, :], v_[:nr, :], [v_], [ob[4]])
        if _STOP == 73:
            continue
        for i, nm in enumerate(("Rt", "Kt", "At", "Bt", "Vb")):
            C.dma(sc_d[nm][r0:r0 + nr, :], ob[i][:nr, :], [ob[i]], [sc_d[nm]])
        if _STOP == 8:
            continue
        b1 = BR.one()
        for hp in range(8):
            C.mm(C.ps[:, b1, 2 * hp:2 * hp + 2], lw[:nr, hp * 128:(hp + 1) * 128], onec[:nr, :], True, True, [lw, onec], [C.pb[b1]])
        C.actf(wc[:, :], C.ps[:, b1, 0:16].rearrange("p (h a) -> p h a", a=2)[:, :, 0], AF.Exp, [C.pb[b1]], [wc])
        C.dma(sc_d["wc"][blk], wc[:, :], [wc], [sc_d["wc"]])
    C.stage_end()


def rw_stage_b(C, sc_d, y_d, wkvp_out, nchunks=16):
    C.stage_begin()
    identf, identb, iotaf = load_consts(C)
    BR = BankRot()
    M2 = C.sb("M2", [128, 2, 128])
    C.dma(M2[:, 0, :], C.cst["lstrict"], [], [M2]); C.dma(M2[:, 1, :], C.cst["lincl"], [], [M2])
    mlow = C.sb("mlow", [128, 128]); C.dma(mlow[:], C.cst["llow"], [], [mlow])
    pm = C.sb("pm", [128, 2]); C.dma(pm[:], C.cst["pm"], [], [pm])
    tok = [[C.sb(f"tok{s}_{i}", [128, D], BF16) for i in range(5)] for s in range(2)]
    KtT = [C.sb(f"KtT{s}", [128, 8, 128], BF16) for s in range(2)]
    BtT = [C.sb(f"BtT{s}", [128, 8, 128], BF16) for s in range(2)]
    ARm = [[C.sb(f"ARm{s}_{h2}", [128, 8, 2, 128], BF16) for h2 in range(2)] for s in range(2)]
    NA = [C.sb(f"NA{s}", [128, 16, 2, 2, 128], BF16) for s in range(2)]
    AA = C.sb("AA", [128, 16, 128], BF16)
    Np = [C.sb(f"Np{i}", [128, 16, 128], BF16) for i in range(2)]
    Ap = [C.sb(f"Ap{i}", [128, 16, 128], BF16) for i in range(2)]
    T32 = C.sb("T32", [128, 16, 128]); Tb = [C.sb(f"Tb{s}", [128, 16, 128], BF16) for s in range(2)]
    Xb = C.sb("Xb", [128, 16, 64], BF16); Ub = C.sb("Ub", [128, 16, 64], BF16)
    P32 = C.sb("P32", [128, 8, 64]); Pb = [C.sb(f"Pb{i}", [128, 8, 64], BF16) for i in range(2)]
    Yo = [C.sb(f"Yo{i}", [128, D]) for i in range(2)]
    wcs = [C.sb(f"wcs{i}", [128, 8]) for i in range(2)]
    C.memset("dve", P32[:], 0.0, [P32]); C.memset("dve", Pb[0][:], 0.0, [Pb[0]])

    def hv(bank2, h):
        return C.ps[:, bank2 + h // 8, (h % 8) * 64:(h % 8 + 1) * 64], C.pb[bank2 + h // 8]

    for c in range(nchunks):
        s = c % 2
        r0 = c * 128
        Rtk, Ktk, Atk, Btk, Vtk = tok[s]
        for i, nm in enumerate(("Rt", "Kt", "At", "Bt", "Vb")):
            C.dma(tok[s][i][:], sc_d[nm][r0:r0 + 128, :], [sc_d[nm]], [tok[s][i]])
        C.dma(wcs[s][:], sc_d["wc"][c], [sc_d["wc"]], [wcs[s]])
        for src, kind in ((Ktk, "K"), (Btk, "B"), (Atk, 0), (Rtk, 1)):
            bank = BR.one()
            pst = C.ps[:, bank, :].bitcast(BF16)
            for hp in range(8):
                C.tr(pst[:, hp * 128:(hp + 1) * 128], src[:, hp * 128:(hp + 1) * 128], identb[:], [src, identb], [C.pb[bank]])
            pv = pst.rearrange("p (a t) -> p a t", a=8)
            if kind == "K":
                C.copy("act", KtT[s][:], pv, [C.pb[bank]], [KtT[s]])
            elif kind == "B":
                C.copy("act", BtT[s][:], pv, [C.pb[bank]], [BtT[s]])
            else:
                for h2 in range(2):
                    C.ts("dve", ARm[s][h2][:, :, kind, :], pv, pm[:, h2:h2 + 1], None, ALU.mult, None, [C.pb[bank], pm], [ARm[s][h2]])
        for h in range(16):
            hp, h2 = h // 2, h % 2
            bank = BR.one()
            rhs2 = ARm[s][h2][:, hp, :, :].rearrange("p a t -> p (a t)")
            C.mm(C.ps[:, bank, 0:256], BtT[s][:, hp, :], rhs2, True, True, [BtT[s], ARm[s][h2]], [C.pb[bank]])
            C.mm(C.ps[:, bank, 256:512], KtT[s][:, hp, :], rhs2, True, True, [KtT[s], ARm[s][h2]], [C.pb[bank]])
            C.tt("dve", NA[s][:, h], C.ps[:, bank, :].rearrange("p (q a t) -> p q a t", q=2, a=2),
                 M2[:].unsqueeze(1).to_broadcast([128, 2, 2, 128]), ALU.mult, [C.pb[bank], M2], [NA[s]])
        for g in range(4):
            bank = BR.one()
            for j in range(4):
                h = 4 * g + j
                hp, h2 = h // 2, h % 2
                C.mm(C.ps[:, bank, j * 128:(j + 1) * 128], ARm[s][h2][:, hp, 0, :], BtT[s][:, hp, :], True, True, [ARm[s][h2], BtT[s]], [C.pb[bank]])
            C.tt("dve", AA[:, 4 * g:4 * g + 4, :], C.ps[:, bank, :].rearrange("p (j t) -> p j t", j=4),
                 mlow[:].unsqueeze(1).to_broadcast([128, 4, 128]), ALU.mult, [C.pb[bank], mlow], [AA])
        N1 = NA[s][:, :, 0, 0, :]
        C.tt("dve", T32[:], N1, identf[:].unsqueeze(1).to_broadcast([128, 16, 128]), ALU.add, [NA[s], identf], [T32])
        C.copy("act", Tb[s][:], T32[:], [T32], [Tb[s]])
        Ncur, Nt, Acur, At_ = N1, NA[s], AA[:], AA
        for i in range(1, 7):
            last = i == 6
            Nn, An = Np[i % 2], Ap[i % 2]
            for g in range(4):
                bankA = BR.one()
                bankN = None if last else BR.one()
                for j in range(4):
                    h = 4 * g + j
                    if not last:
                        C.mm(C.ps[:, bankN, j * 128:(j + 1) * 128], Acur[:, h, :], Ncur[:, h, :], True, True, [At_, Nt], [C.pb[bankN]])
                    C.mm(C.ps[:, bankA, j * 128:(j + 1) * 128], Ncur[:, h, :], Acur[:, h, :], True, True, [At_, Nt], [C.pb[bankA]])
                if not last:
                    C.copy("act", Nn[:, 4 * g:4 * g + 4, :], C.ps[:, bankN, :].rearrange("p (j t) -> p j t", j=4), [C.pb[bankN]], [Nn])
                C.copy("dve", An[:, 4 * g:4 * g + 4, :], C.ps[:, bankA, :].rearrange("p (j t) -> p j t", j=4), [C.pb[bankA]], [An])
            for g in range(4):
                bankT = BR.one()
                for j in range(4):
                    h = 4 * g + j
                    C.mm(C.ps[:, bankT, j * 128:(j + 1) * 128], An[:, h, :], Tb[s][:, h, :], True, True, [An, Tb[s]], [C.pb[bankT]])
                C.tt("dve", T32[:, 4 * g:4 * g + 4, :], T32[:, 4 * g:4 * g + 4, :], C.ps[:, bankT, :].rearrange("p (j t) -> p j t", j=4), ALU.add, [T32, C.pb[bankT]], [T32])
            C.copy("act", Tb[s][:], T32[:], [T32], [Tb[s]])
            Ncur, Nt, Acur, At_ = Nn[:], Nn, An[:], An
        cur, nxt = Pb[c % 2], Pb[(c + 1) % 2]
        b2 = BR.two()
        for h in range(16):
            hp, h2 = h // 2, h % 2
            o, ob_ = hv(b2, h)
            C.mm(o, ARm[s][h2][:, hp, 0, :], cur[:, hp, :], True, False, [ARm[s][h2], cur], [ob_])
            C.mm(o, NA[s][:, h, 1, 0, :], Vtk[:, h * 64:(h + 1) * 64], False, True, [NA[s], Vtk], [ob_])
        for a in range(2):
            C.copy("act" if a else "dve", Xb[:, 8 * a:8 * a + 8, :], C.ps[:, b2 + a, :].rearrange("p (h v) -> p h v", h=8), [C.pb[b2 + a]], [Xb])
        b2 = BR.two()
        for h in range(16):
            o, ob_ = hv(b2, h)
            C.mm(o, Tb[s][:, h, :], Xb[:, h, :], True, True, [Tb[s], Xb], [ob_])
        for a in range(2):
            C.copy("act" if a else "dve", Ub[:, 8 * a:8 * a + 8, :], C.ps[:, b2 + a, :].rearrange("p (h v) -> p h v", h=8), [C.pb[b2 + a]], [Ub])
        b2 = BR.two()
        for h in range(16):
            hp, h2 = h // 2, h % 2
            o, ob_ = hv(b2, h)
            C.mm(o, ARm[s][h2][:, hp, 1, :], cur[:, hp, :], True, False, [ARm[s][h2], cur], [ob_])
            C.mm(o, NA[s][:, h, 0, 1, :], Ub[:, h, :], False, False, [NA[s], Ub], [ob_])
            C.mm(o, NA[s][:, h, 1, 1, :], Vtk[:, h * 64:(h + 1) * 64], False, True, [NA[s], Vtk], [ob_])
        YO = Yo[c % 2]
        for a in range(2):
            C.copy("act", YO[:, 512 * a:512 * a + 512], C.ps[:, b2 + a, :], [C.pb[b2 + a]], [YO])
        C.dma(y_d[r0:r0 + 128, :], YO[:], [YO], [y_d])
        b2 = BR.two()
        for hp in range(8):
            o = C.ps[:, b2 + hp // 4, (hp % 4) * 128:(hp % 4 + 1) * 128]
            ob_ = C.pb[b2 + hp // 4]
            C.mm(o, Btk[:, hp * 128:(hp + 1) * 128], Ub[:, 2 * hp:2 * hp + 2, :].rearrange("p a v -> p (a v)"), True, False, [Btk, Ub], [ob_])
            C.mm(o, Ktk[:, hp * 128:(hp + 1) * 128], Vtk[:, hp * 128:(hp + 1) * 128], False, True, [Ktk, Vtk], [ob_])
        for h2 in range(2):
            p0 = 64 * h2
            psv = C.ps[p0:p0 + 64, b2:b2 + 2, :].rearrange("p a (q h v) -> p (a q) h v", q=4, h=2)[:, :, h2, :]
            C.tt("dve", P32[p0:p0 + 64, :, :], P32[p0:p0 + 64, :, :], psv, ALU.add, [P32, C.pb[b2], C.pb[b2 + 1]], [P32])
        C.tt("dve", P32[:], P32[:], wcs[s][:, :].unsqueeze(2).to_broadcast([128, 8, 64]), ALU.mult, [P32, wcs[s]], [P32])
        C.copy("act", nxt[:], P32[:], [P32], [nxt])
    b2 = BR.two()
    for hp in range(8):
        C.tr(C.ps[:64, b2 + hp // 4, (hp % 4) * 128:(hp % 4 + 1) * 128], P32[:, hp, :], identf[:], [P32, identf], [C.pb[b2 + hp // 4]])
    so = C.sb("so", [64, D])
    for a in range(2):
        C.copy("act", so[:, 512 * a:512 * a + 512], C.ps[:64, b2 + a, :], [C.pb[b2 + a]], [so])
    C.dma(wkvp_out.t.rearrange("h v k -> v h k"), so[:, :].rearrange("p (h k) -> p h k", h=16), [so], [wkvp_out])
    C.stage_end()


def rw_stage_s(C, sc_d, wkv_in, y_d, wkvs_out):
    C.stage_begin()
    S = [C.sb(f"S{g}", [128, 64, 64]) for g in range(2)]
    q = [[C.sb(f"q{g}_{i}", [128, 4, 64]) for i in range(6)] for g in range(2)]
    tmp = C.sb("tmp", [128, 64, 64]); sa = C.sb("sa", [128, 64]); yy = [C.sb(f"yy{g}", [128, 4, 64]) for g in range(2)]
    for g in range(2):
        for hl in range(8):
            h = 8 * g + hl
            C.dma(S[g][hl * 16:(hl + 1) * 16, :, :], wkv_in[:, h, :, :], [], [S[g]])
        for i in range(6):
            C.dma(q[g][i][:], sc_d["smp"].t[i, 8 * g:8 * g + 8].rearrange("h b t k -> (h b) t k"), [sc_d["smp"]], [q[g][i]])
    for g in range(2):
        r_, w_, k_, v_, al, be = q[g]
        St = S[g]
        for t in range(4):
            def bk(x):
                return x[:, t, :].unsqueeze(1).to_broadcast([128, 64, 64])

            def bv(x):
                return x.unsqueeze(2).to_broadcast([128, 64, 64])
            C.tt("dve", tmp[:], St[:], bk(al), ALU.mult, [St, al], [tmp])
            C.red(sa[:], tmp[:], ALU.add, [tmp], [sa])
            C.tt("dve", St[:], St[:], bk(w_), ALU.mult, [St, w_], [St])
            C.tt("dve", tmp[:], bv(sa[:, :]), bk(be), ALU.mult, [sa, be], [tmp])
            C.tt("dve", St[:], St[:], tmp[:], ALU.add, [St, tmp], [St])
            C.tt("dve", tmp[:], bv(v_[:, t, :]), bk(k_), ALU.mult, [v_, k_], [tmp])
            C.tt("dve", St[:], St[:], tmp[:], ALU.add, [St, tmp], [St])
            C.tt("dve", tmp[:], St[:], bk(r_), ALU.mult, [St, r_], [tmp])
            C.red(yy[g][:, t, :], tmp[:], ALU.add, [tmp], [yy[g]])
        for hl in range(8):
            h = 8 * g + hl
            C.dma(wkvs_out[:, h, :, :], St[hl * 16:(hl + 1) * 16, :, :], [St], [wkvs_out])
            dst = y_d.t[NP_:NT, h * 64:(h + 1) * 64].rearrange("(b t) v -> b t v", t=4)
            C.dma(dst, yy[g][hl * 16:(hl + 1) * 16, :, :], [yy[g]], [y_d])
    C.stage_end()


def rw_stage_c(C, W, sc_d, y_d, h_out):
    C.stage_begin()
    identf, identb, iotaf = load_consts(C)
    BR = BankRot()
    wo = C.sb("wo", [128, 8, D], BF16)
    src = W["rw_w_o"][0].rearrange("(k p) n -> p k n", p=128)
    for i in range(2):
        C.dma(wo[:, :, i * 512:(i + 1) * 512], src[:, :, i * 512:(i + 1) * 512], [], [wo], q="pool")
    gg = C.sb("gng", [128, D]); gb = C.sb("gnb", [128, D])
    C.dma(gg[:], W["rw_gn_g"][0:1, :].partition_broadcast(128), [], [gg])
    C.dma(gb[:], W["rw_gn_b"][0:1, :].partition_broadcast(128), [], [gb])
    eps = C.sb("epsg", [128, 1]); C.memset("dve", eps[:], 64e-5, [eps])
    NB = 2
    y = [C.sb(f"y{i}", [128, D]) for i in range(NB)]; v = [C.sb(f"v{i}", [128, D]) for i in range(NB)]; g = [C.sb(f"g{i}", [128, D]) for i in range(NB)]
    bon = [C.sb(f"bon{i}", [128, 16]) for i in range(NB)]
    sq = C.sb("sq", [128, D]); mu = C.sb("mu", [128, 16]); var = C.sb("var", [128, 16])
    zb = [C.sb(f"zb{i}", [128, D], BF16) for i in range(NB)]; zT = [C.sb(f"zT{i}", [128, 8, 128], BF16) for i in range(NB)]
    ho = [C.sb(f"ho{i}", [128, D]) for i in range(NB)]

    def v3(t, nr):
        return t[:nr, :].rearrange("p (h k) -> p h k", h=16)

    def b16(t, nr):
        return t[:nr, :].unsqueeze(2).to_broadcast([nr, 16, 64])

    for blk in range(NBLK):
        r0, nr = blk_rows(blk)
        s = blk % NB
        Y, V, G, BON, ZB, ZT, HO = y[s], v[s], g[s], bon[s], zb[s], zT[s], ho[s]
        C.dma(Y[:nr, :], y_d[r0:r0 + nr, :], [y_d], [Y])
        C.dma(V[:nr, :], sc_d["v32"][r0:r0 + nr, :], [sc_d["v32"]], [V])
        C.dma(G[:nr, :], sc_d["g32"][r0:r0 + nr, :], [sc_d["g32"]], [G])
        C.dma(BON[:nr, :], sc_d["bon"][r0:r0 + nr, :], [sc_d["bon"]], [BON])
        C.red(mu[:nr, :], v3(Y, nr), ALU.add, [Y], [mu])
        C.ts("dve", mu[:nr, :], mu[:nr, :], 1.0 / 64, None, ALU.mult, None, [mu], [mu])
        C.tt("dve", v3(Y, nr), v3(Y, nr), b16(mu, nr), ALU.subtract, [Y, mu], [Y])
        C.tt("pool", sq[:nr, :], Y[:nr, :], Y[:nr, :], ALU.mult, [Y], [sq])
        C.red(var[:nr, :], v3(sq, nr), ALU.add, [sq], [var])
        C.actf(var[:nr, :], var[:nr, :], AF.Sqrt, [var], [var], bias=eps[:nr, 0:1], scale=1.0 / 64)
        C.recip(var[:nr, :], var[:nr, :], [var], [var])
        C.tt("dve", v3(Y, nr), v3(Y, nr), b16(var, nr), ALU.mult, [Y, var], [Y])
        C.tt("dve", Y[:nr, :], Y[:nr, :], gg[:nr, :], ALU.mult, [Y, gg], [Y])
        C.tt("pool", Y[:nr, :], Y[:nr, :], gb[:nr, :], ALU.add, [Y, gb], [Y])
        C.tt("dve", v3(V, nr), v3(V, nr), b16(BON, nr), ALU.mult, [V, BON], [V])
        C.tt("pool", Y[:nr, :], Y[:nr, :], V[:nr, :], ALU.add, [Y, V], [Y])
        C.tt("dve", ZB[:nr, :], Y[:nr, :], G[:nr, :], ALU.mult, [Y, G], [ZB])
        to_featmajor(C, nr, ZB, ZT, 0, identb, BR.one())
        b2 = BR.two()
        for nh in range(2):
            for kc in range(8):
                C.mm(C.ps[:nr, b2 + nh, :], ZT[:, kc, :nr], wo[:, kc, nh * 512:(nh + 1) * 512], kc == 0, kc == 7, [ZT, wo], [C.pb[b2 + nh]])
            C.copy("act", HO[:nr, nh * 512:(nh + 1) * 512], C.ps[:nr, b2 + nh, :], [C.pb[b2 + nh]], [HO])
        C.dma(h_out[r0:r0 + nr, :], HO[:nr, :], [HO], [h_out])
    C.stage_end()


def rw_scratch(C, kind="Internal"):
    sc = {}
    for nm in ("Rt", "Kt", "At", "Bt", "Vb"):
        sc[nm] = C.dram("rw_" + nm, [NT, D], BF16, kind=kind)
    sc["v32"] = C.dram("rw_v32", [NT, D], F32, kind=kind)
    sc["g32"] = C.dram("rw_g32", [NT, D], F32, kind=kind)
    sc["bon"] = C.dram("rw_bon", [NT, 16], F32, kind=kind)
    sc["wc"] = C.dram("rw_wc", [16, 128, 8], F32, kind=kind)
    sc["smp"] = C.dram("rw_smp", [6, 16, 16, 4, 64], F32, kind=kind)
    return sc


LAM_INIT = 0.8 - 0.6 * float(np.exp(-0.3 * 1))


def kvq_stage(C, W, x1T_d, k_out, v_out, kT_d, qT_d, vb_d):
    C.stage_begin()
    BR = BankRot()
    wts = {}
    for nm, src in (("wk", W["da_w_k"]), ("wv", W["da_w_v"]), ("wq", W["da_w_q"][0])):
        t = C.sb(nm, [128, 8, D], BF16)
        s_ = src.rearrange("(k p) n -> p k n", p=128)
        for i in range(2):
            C.dma(t[:, :, i * 512:(i + 1) * 512], s_[:, :, i * 512:(i + 1) * 512], [], [t], q="pool")
        wts[nm] = t
    NB = 2
    xT = [C.sb(f"xT{i}", [128, 8, 128], BF16) for i in range(NB)]
    ko = [C.sb(f"ko{i}", [128, D]) for i in range(NB)]; vo = [C.sb(f"vo{i}", [128, D]) for i in range(NB)]
    vb = [C.sb(f"vb{i}", [128, D], BF16) for i in range(NB)]
    kT = [C.sb(f"kT{i}", [128, 8, 128], BF16) for i in range(NB)]; qT = [C.sb(f"qT{i}", [128, 8, 128], BF16) for i in range(NB)]
    for blk in range(NBLK):
        r0, nr = blk_rows(blk)
        s = blk % NB
        XT = xT[s]
        C.dma(XT[:, :, :nr], x1T_d[:, :, r0:r0 + nr], [x1T_d], [XT])
        for wn, dst, outd in (("wk", ko[s], k_out), ("wv", vo[s], v_out)):
            b2 = BR.two()
            for nh in range(2):
                for kc in range(8):
                    C.mm(C.ps[:nr, b2 + nh, :], XT[:, kc, :nr], wts[wn][:, kc, nh * 512:(nh + 1) * 512], kc == 0, kc == 7, [XT, wts[wn]], [C.pb[b2 + nh]])
                C.copy("act" if nh else "dve", dst[:nr, nh * 512:(nh + 1) * 512], C.ps[:nr, b2 + nh, :], [C.pb[b2 + nh]], [dst])
            C.dma(outd[r0:r0 + nr, :], dst[:nr, :], [dst], [outd])
        C.copy("act", vb[s][:nr, :], vo[s][:nr, :], [vo[s]], [vb[s]])
        C.dma(vb_d[r0:r0 + nr, :], vb[s][:nr, :], [vb[s]], [vb_d])
        for wn, dst, outd, scl in (("wk", kT[s], kT_d, 1.0), ("wq", qT[s], qT_d, 0.125)):
            for hg in range(2):
                bank = BR.one()
                for j in range(4):
                    h = 4 * hg + j
                    for kc in range(8):
                        C.mm(C.ps[:, bank, j * 128:j * 128 + nr], wts[wn][:, kc, h * 128:(h + 1) * 128], XT[:, kc, :nr], kc == 0, kc == 7, [XT, wts[wn]], [C.pb[bank]])
                C.actf(dst[:, 4 * hg:4 * hg + 4, :nr], C.ps[:, bank, :].rearrange("p (j t) -> p j t", j=4)[:, :, :nr], AF.Identity, [C.pb[bank]], [dst], scale=scl)
            C.dma(outd[:, :, r0:r0 + nr], dst[:, :, :nr], [dst], [outd])
    C.stage_end()


def lam_compute(C, W):
    lv = C.sb("lv", [128, 256]); pr = C.sb("lpr", [128, 2, 64]); s2 = C.sb("ls2", [128, 2]); lam = C.sb("lam", [128, 1])
    C.dma(lv[:], W["da_lam"][0].rearrange("a b -> (a b)").partition_broadcast(128), [], [lv])
    l4 = lv[:, :].rearrange("p (a b) -> p a b", a=4)
    C.tt("dve", pr[:, 0, :], l4[:, 0, :], l4[:, 1, :], ALU.mult, [lv], [pr])
    C.tt("dve", pr[:, 1, :], l4[:, 2, :], l4[:, 3, :], ALU.mult, [lv], [pr])
    C.red(s2[:, :], pr[:, :, :], ALU.add, [pr], [s2])
    C.actf(s2[:, :], s2[:, :], AF.Exp, [s2], [s2])
    C.tt("dve", lam[:, :], s2[:, 0:1], s2[:, 1:2], ALU.subtract, [s2], [lam])
    C.ts("dve", lam[:, :], lam[:, :], LAM_INIT, None, ALU.add, None, [lam], [lam])
    return lam


def attn_finish(C, W, nr, O, scr, identb, wo, sg, h_out, r0, BR):
    sq, ss, ob, oT, ho, eps = scr["sq"], scr["ss"], scr["ob"], scr["oT"], scr["ho"], scr["eps"]
    o3 = O[:nr, :].rearrange("p (h v) -> p h v", h=8)
    C.tt("pool", sq[:nr, :], O[:nr, :], O[:nr, :], ALU.mult, [O], [sq])
    C.red(ss[:nr, :], sq[:nr, :].rearrange("p (h v) -> p h v", h=8), ALU.add, [sq], [ss])
    C.actf(ss[:nr, :], ss[:nr, :], AF.Sqrt, [ss], [ss], bias=eps[:nr, 0:1], scale=1.0 / 128)
    C.recip(ss[:nr, :], ss[:nr, :], [ss], [ss])
    C.tt("dve", o3, o3, ss[:nr, :].unsqueeze(2).to_broadcast([nr, 8, 128]), ALU.mult, [O, ss], [O])
    C.tt("dve", ob[:nr, :].rearrange("p (h v) -> p h v", h=8), o3, sg[:nr, :].unsqueeze(1).to_broadcast([nr, 8, 128]), ALU.mult, [O, sg], [ob])
    to_featmajor(C, nr, ob, oT, 0, identb, BR.one())
    b2 = BR.two()
    for nh in range(2):
        for h in range(8):
            C.mm(C.ps[:nr, b2 + nh, :], oT[:, h, :nr], wo[:, h, nh * 512:(nh + 1) * 512], h == 0, h == 7, [oT, wo], [C.pb[b2 + nh]])
        C.copy("act", ho[:nr, nh * 512:(nh + 1) * 512], C.ps[:nr, b2 + nh, :], [C.pb[b2 + nh]], [ho])
    C.dma(h_out[r0:r0 + nr, :], ho[:nr, :], [ho], [h_out])


def attn_common(C, W):
    identf, identb, iotaf = load_consts(C)
    wo = C.sb("wo_a", [128, 8, D], BF16)
    src = W["da_w_o"][0].rearrange("(h p) n -> p h n", p=128)
    for i in range(2):
        C.dma(wo[:, :, i * 512:(i + 1) * 512], src[:, :, i * 512:(i + 1) * 512], [], [wo], q="pool")
    sg = C.sb("sg_a", [128, 128])
    C.dma(sg[:], W["da_subln_g"][0:1, :].partition_broadcast(128), [], [sg])
    C.ts("dve", sg[:], sg[:], 1.0 - LAM_INIT, None, ALU.mult, None, [sg], [sg])
    eps = C.sb("eps_a", [128, 1]); C.memset("dve", eps[:], 1e-5, [eps])
    scr = {"sq": C.sb("sq_a", [128, D]), "ss": C.sb("ss_a", [128, 8]), "ob": C.sb("ob_a", [128, D], BF16),
           "oT": C.sb("oT_a", [128, 8, 128], BF16), "ho": C.sb("ho_a", [128, D]), "eps": eps}
    lam = lam_compute(C, W)
    return identf, identb, wo, sg, scr, lam


def attn_prompt_stage(C, W, kT_d, qT_d, vb_d, h_out, nqb=16):
    C.stage_begin()
    identf, identb, wo, sg, scr, lam = attn_common(C, W)
    BR = BankRot()
    kT = C.sb("kTa", [128, 8, NP_], BF16); qT = C.sb("qTa", [128, 8, NP_], BF16)
    for i in range(4):
        C.dma(kT[:, :, i * 512:(i + 1) * 512], kT_d[:, :, i * 512:(i + 1) * 512], [kT_d], [kT])
        C.dma(qT[:, :, i * 512:(i + 1) * 512], qT_d[:, :, i * 512:(i + 1) * 512], [qT_d], [qT])
    vA = C.sb("vA", [128, 16, 8, 129], BF16)
    C.memset("pool", vA[:, :, :, 128:129], 1.0, [vA])
    for kb in range(16):
        C.dma(vA[:, kb, :, 0:128], vb_d[kb * 128:(kb + 1) * 128, :].rearrange("p (h v) -> p h v", h=8), [vb_d], [vA])
    btab = C.sb("btab", [128, 8, 16]); C.dma(btab[:], C.cst["btab"], [], [btab])
    msk = C.sb("msk", [128, 128], BF16); mskf = C.sb("mskf", [128, 128])
    C.dma(mskf[:], C.cst["lincl"], [], [mskf]); C.copy("dve", msk[:], mskf[:], [mskf], [msk])
    PT = [C.sb(f"PT{i}", [128, 2, 128], BF16) for i in range(3)]
    pm = C.sb("pm_a", [128, 2]); C.dma(pm[:], C.cst["pm"], [], [pm])
    qz = [C.sb(f"qz{i}", [128, 8, 2, 128], BF16) for i in range(2)]
    O = [C.sb(f"O{i}", [128, D]) for i in range(2)]
    rr = C.sb("rr", [128, 2]); tmp = C.sb("tmpo", [128, 128])
    it = 0
    for qb in range(nqb):
        OO = O[qb % 2]
        QZ = qz[qb % 2]
        for c in range(2):
            C.ts("dve" if c else "pool", QZ[:, :, c, :], qT[:, :, qb * 128:(qb + 1) * 128], pm[:, c:c + 1], None, ALU.mult, None, [qT, pm], [QZ])
        for h in range(8):
            accb = (0, 1) if it % 2 == 0 else (2, 3)
            it += 1
            for kb in range(qb + 1):
                bs = 4 + (kb % 4)
                C.mm(C.ps[:, bs, 0:256], kT[:, h, kb * 128:(kb + 1) * 128], QZ[:, h, :, :].rearrange("p c q -> p (c q)"), True, True, [kT, QZ], [C.pb[bs]])
                P_ = PT[kb % 3]
                C.actf(P_[:, :, :], C.ps[:, bs, 0:256].rearrange("p (c q) -> p c q", c=2), AF.Exp, [C.pb[bs], btab], [P_], bias=btab[:, h, qb - kb:qb - kb + 1])
                if kb == qb:
                    C.tt("dve", P_[:, :, :], P_[:, :, :], msk[:].unsqueeze(1).to_broadcast([128, 2, 128]), ALU.mult, [P_, msk], [P_])
                for c in range(2):
                    C.mm(C.ps[:, accb[c], 0:129], P_[:, c, :], vA[:, kb, h, :], kb == 0, kb == qb, [P_, vA], [C.pb[accb[c]]])
            for c in range(2):
                C.recip(rr[:, c:c + 1], C.ps[:, accb[c], 128:129], [C.pb[accb[c]]], [rr])
            C.tt("dve", rr[:, 1:2], rr[:, 1:2], lam[:, :], ALU.mult, [rr, lam], [rr])
            C.ts("dve", tmp[:], C.ps[:, accb[1], 0:128], rr[:, 1:2], None, ALU.mult, None, [C.pb[accb[1]], rr], [tmp])
            C.stt(OO[:, h * 128:(h + 1) * 128], C.ps[:, accb[0], 0:128], rr[:, 0:1], tmp[:], ALU.mult, ALU.subtract, [C.pb[accb[0]], rr, tmp], [OO])
        attn_finish(C, W, 128, OO, scr, identb, wo, sg, h_out, qb * 128, BR2(BR))
    C.stage_end()


class BR2:
    def __init__(self, br):
        self.i1 = 0
        self.i2 = 0

    def one(self):
        b = 4 + self.i1 % 4
        self.i1 += 1
        return b

    def two(self):
        b = 4 + (self.i2 % 2) * 2
        self.i2 += 1
        return b


def attn_sample_stage(C, W, pt_ap, ckT_ap, cv_ap, kT_d, qT_d, vb_d, h_out, nseq=16):
    C.stage_begin()
    identf, identb, wo, sg, scr, lam = attn_common(C, W)
    pti = C.sb("pti", [128, 256], I32); ptf = C.sb("ptf", [128, 256]); idxi = C.sb("idxi", [128, 256], I32)
    iop = C.sb("iop", [128, 1]); C.dma(iop[:], C.cst["iotap"], [], [iop])
    C.dma(pti[:], pt_ap.rearrange("a b -> (a b)").partition_broadcast(128), [], [pti])
    C.copy("dve", ptf[:], pti[:], [pti], [ptf])
    C.ts("dve", ptf[:], ptf[:], 128.0, iop[:, 0:1], ALU.mult, ALU.add, [ptf, iop], [ptf])
    C.copy("dve", idxi[:], ptf[:], [ptf], [idxi])
    qTs = C.sb("qTs", [128, 8, NS], BF16); kTs = C.sb("kTs", [128, 8, NS], BF16)
    C.dma(qTs[:], qT_d[:, :, NP_:NT], [qT_d], [qTs]); C.dma(kTs[:], kT_d[:, :, NP_:NT], [kT_d], [kTs])
    qblk = C.sb("qblk", [128, 16, 8, 8], BF16)
    C.memset("pool", qblk[:], 0.0, [qblk])
    for c in range(2):
        C.copy("dve", qblk[c * 64:(c + 1) * 64, :, :, c * 4:(c + 1) * 4], qTs[c * 64:(c + 1) * 64, :, :].rearrange("p h (b q) -> p b h q", q=4), [qTs], [qblk])
    tabS = C.sb("tabS", [128, 16, 64]); C.dma(tabS[:], C.cst["tabS"], [], [tabS])
    tabN = C.sb("tabN", [128, 64]); C.dma(tabN[:], C.cst["tabN"], [], [tabN])
    onesf = C.sb("onesf", [64, 128]); C.memset("dve", onesf[:], 1.0, [onesf])
    cmbW = C.sb("cmbW", [8, 124]); cA = C.sb("cA", [8, 124]); cB = C.sb("cB", [8, 124])
    C.dma(cA[:], C.cst["cmbA"], [], [cA]); C.dma(cB[:], C.cst["cmbB"], [], [cB])
    C.stt(cmbW[:], cB[:], lam[:8, 0:1], cA[:], ALU.mult, ALU.add, [cB, cA, lam], [cmbW])
    stg = [C.sb(f"stg{i}", [128, D]) for i in range(4)]
    kpg = [C.sb(f"kpg{i}", [128, 8, 128], BF16) for i in range(4)]
    vres = [C.sb(f"vres{i}", [128, 17, 8, 129], BF16) for i in range(2)]
    for i in range(2):
        C.memset("pool", vres[i][:, 0:16, :, 128:129], 1.0, [vres[i]])
        C.memset("pool", vres[i][:, 16, :, :], 0.0, [vres[i]])
        C.memset("pool", vres[i][0:32, 16, :, 128:129], 1.0, [vres[i]])
    E = [C.sb(f"E{i}", [128, 17, 64]) for i in range(2)]
    PTs = [C.sb(f"PTs{i}", [128, 17, 64], BF16) for i in range(2)]
    m1 = C.sb("m1", [128, 64]); mcol = C.sb("mcol", [64, 1]); dg = C.sb("dg", [64, 64])
    rrs = C.sb("rrs", [8, 8]); on32 = C.sb("on32", [8, D])
    Os = C.sb("Os", [128, D])
    ng = 0
    for b in range(nseq):
        s = b % 2
        VR, EE, PP = vres[s], E[s], PTs[s]
        for pg in range(16):
            col = b * 16 + pg
            for which in range(2):
                st = stg[ng % 4]
                ng += 1
                src = ckT_ap if which == 0 else cv_ap
                C.P.dma("pool", (lambda st, src, col: (lambda e: e.indirect_dma_start(out=st[:], out_offset=None, in_=src,
                        in_offset=bass.IndirectOffsetOnAxis(ap=idxi[:, col:col + 1], axis=0))))(st, src, col), [_b(idxi)], [_b(st)])
                if which == 0:
                    KP = kpg[pg % 4]
                    C.copy("act", KP[:], st[:, :].rearrange("p (h k) -> p h k", h=8), [st], [KP])
                    bank = pg % 2
                    for h in range(8):
                        C.mm(C.ps[:, bank, h * 8:(h + 1) * 8], KP[:, h, :], qblk[:, b, h, :], True, True, [KP, qblk], [C.pb[bank]])
                    C.tt("dve", EE[:, pg, :], C.ps[:, bank, 0:64], tabS[:, pg, :], ALU.add, [C.pb[bank], tabS], [EE])
                else:
                    C.copy("pool", VR[:, pg, :, 0:128], st[:, :].rearrange("p (h v) -> p h v", h=8), [st], [VR])
        C.dma(VR[0:4, 16, :, 0:128], vb_d[NP_ + 4 * b:NP_ + 4 * b + 4, :].rearrange("p (h v) -> p h v", h=8), [vb_d], [VR])
        C.copy("dve", EE[:, 16, :], tabN[:], [tabN], [EE])
        for h in range(8):
            C.mm(C.ps[:4, 2, h * 8:(h + 1) * 8], kTs[:, h, 4 * b:4 * b + 4], qblk[:, b, h, :], True, True, [kTs, qblk], [C.pb[2]])
        C.tt("dve", EE[0:4, 16, :], C.ps[0:4, 2, 0:64], tabN[0:4, :], ALU.add, [C.pb[2], tabN], [EE])
        C.red(m1[:], EE[:].rearrange("p k j -> p j k"), ALU.max, [EE], [m1])
        C.tr(C.ps[:64, 2, 0:128], m1[:], identf[:], [m1, identf], [C.pb[2]])
        C.red(mcol[:], C.ps[:64, 2, 0:128], ALU.max, [C.pb[2]], [mcol])
        C.ts("dve", dg[:], identf[:64, :64], mcol[:, 0:1], None, ALU.mult, None, [identf, mcol], [dg])
        C.mm(C.ps[:, 2, 128:192], onesf[:], dg[:], True, True, [onesf, dg], [C.pb[2]])
        C.tt("dve", EE[:], EE[:], C.ps[:, 2, 128:192].unsqueeze(1).to_broadcast([128, 17, 64]), ALU.subtract, [EE, C.pb[2]], [EE])
        C.actf(PP[:], EE[:], AF.Exp, [EE], [PP])
        for h in range(8):
            bank = 3 + h // 3
            o = C.ps[:8, bank, (h % 3) * 129:(h % 3) * 129 + 129]
            for kb in range(17):
                C.mm(o, PP[:, kb, h * 8:(h + 1) * 8], VR[:, kb, h, :], kb == 0, kb == 16, [PP, VR], [C.pb[bank]])
        for g3 in range(3):
            nh_ = 3 if g3 < 2 else 2
            pv = C.ps[:8, 3 + g3, 0:nh_ * 129].rearrange("p (h v) -> p h v", v=129)
            C.recip(rrs[:, 3 * g3:3 * g3 + nh_], pv[:, :, 128], [C.pb[3 + g3]], [rrs])
            C.tt("dve", on32[:, 3 * g3 * 128:(3 * g3 + nh_) * 128].rearrange("p (h v) -> p h v", v=128), pv[:, :, 0:128],
                 rrs[:, 3 * g3:3 * g3 + nh_].unsqueeze(2).to_broadcast([8, nh_, 128]), ALU.mult, [C.pb[3 + g3], rrs], [on32])
        for nh in range(2):
            C.mm(C.ps[:64, 6 + nh, :], cmbW[:, 60 - 4 * b:124 - 4 * b], on32[:, nh * 512:(nh + 1) * 512], b == 0, b == nseq - 1, [cmbW, on32], [C.pb[6 + nh]])
    for nh in range(2):
        C.copy("act", Os[:64, nh * 512:(nh + 1) * 512], C.ps[:64, 6 + nh, :], [C.pb[6 + nh]], [Os])
    attn_finish(C, W, 64, Os, scr, identb, wo, sg, h_out, NP_, BankRot())
    C.stage_end()


W_SHAPES = None


def build_program(wshapes, cshapes, n_pool_rows, dev=False):
    nc = bass.Bass("TRN2", target_bir_lowering=False)
    es = ExitStack()
    C = Ctx(nc, es)

    def ext(name, shape, dt=F32):
        return nc.dram_tensor(name, list(shape), dt, kind="ExternalInput").ap()

    W = {k: ext(k, s) for k, s in wshapes.items()}
    C.cst = {k: ext("cst_" + k, s) for k, s in cshapes.items()}
    x0 = Tile(ext("x0", [NT, D])); x0T = Tile(ext("x0T", [D, NT])); xpT = Tile(ext("xpT", [D, NT]))
    wkv_in = ext("wkv_in", [16, 16, 64, 64])
    pT = [ext("pT0", [256, NT]), ext("pT1", [256, NT])]
    pt = ext("ptab", [16, 16], I32)
    ckT = ext("ckT", [n_pool_rows, D]); cv = ext("cv", [n_pool_rows, D])
    kind = "ExternalOutput" if dev else "Internal"
    out = lambda name, shape: C.dram(name, shape, F32, kind="ExternalOutput")
    y = out("y", [NT, D]); shp = out("shift_p", [1, D]); wkvp = out("wkv_p", [16, 64, 64]); k_o = out("k_o", [NT, D]); v_o = out("v_o", [NT, D])
    shs = out("shift_s", [16, D]); wkvs = out("wkv_s", [16, 16, 64, 64])
    sc = rw_scratch(C)
    y_d = C.dram("y_d", [NT, D], F32, kind=kind); h0 = C.dram("h0_d", [NT, D], F32, kind=kind)
    xa = C.dram("xa_d", [NT, D], F32); xaT = C.dram("xaT_d", [128, 8, NT], BF16); idx = C.dram("idx_d", [128, 3, NT], F32)
    c_d = C.dram("c_d", [NT, D], F32, kind=kind); x1 = C.dram("x1_d", [NT, D], F32, kind=kind); x1T = C.dram("x1T_d", [128, 8, NT], BF16)
    kT = C.dram("kT_d", [128, 8, NT], BF16); qT = C.dram("qT_d", [128, 8, NT], BF16); vb = C.dram("vb_d", [NT, D], BF16)
    h1 = C.dram("h1_d", [NT, D], F32, kind=kind)
    C.stage_begin()
    C.dma(shp[:, :], x0[NP_ - 1:NP_, :], [], [shp])
    C.dma(shs[:, :], x0.t[NP_:NT, :].rearrange("(b t) d -> b t d", t=4)[:, 3, :], [], [shs])
    C.stage_end()
    rw_stage_a(C, W, x0T, xpT, sc)
    rw_stage_b(C, sc, y_d, wkvp)
    rw_stage_s(C, sc, wkv_in, y_d, wkvs)
    rw_stage_c(C, W, sc, y_d, h0)
    ffn_stage_a(C, 0, W, x0, h0, xa, xaT, idx)
    ffn_stage_b(C, 0, W, xaT, idx, c_d)
    ffn_stage_c(C, 0, W, xa, c_d, pT[0], x1, x1T)
    kvq_stage(C, W, x1T, k_o, v_o, kT, qT, vb)
    attn_prompt_stage(C, W, kT, qT, vb, h1)
    attn_sample_stage(C, W, pt, ckT, cv, kT, qT, vb, h1)
    ffn_stage_a(C, 1, W, x1, h1, xa, xaT, idx)
    ffn_stage_b(C, 1, W, xaT, idx, c_d)
    ffn_stage_c(C, 1, W, xa, c_d, pT[1], y)
    es.close()
    return nc


def host_inputs(inputs, cores):
    w = prep_weights({k: v for k, v in inputs.items() if k.startswith(("rw_", "da_", "ln", "peer", "ple"))})
    cst = make_consts()
    ck = inputs["cache_k"]
    npool = ck.shape[0]
    ckT = np.ascontiguousarray(ck.transpose(0, 3, 2, 1)).reshape(npool * 128, D)
    cv = np.ascontiguousarray(inputs["cache_v"]).reshape(npool * 128, D)
    shared = dict(w)
    shared.update({"cst_" + k: v for k, v in cst.items()})
    shared["ckT"] = ckT
    shared["cv"] = cv
    maps = []
    for c in cores:
        sl = slice(16 * c, 16 * c + 16)
        xs = inputs["x_sample"][sl]
        x0 = np.concatenate([inputs["x_prompt"][c], xs.reshape(NS, D)], 0)
        xprev = np.zeros_like(x0)
        xprev[1:NP_] = x0[0:NP_ - 1]
        xprev[NP_:] = np.concatenate([inputs["state_shift"][0, sl][:, None, :], xs[:, :3]], 1).reshape(NS, D)
        m = dict(shared)
        m["x0"] = np.ascontiguousarray(x0)
        m["x0T"] = np.ascontiguousarray(x0.T)
        m["xpT"] = np.ascontiguousarray(xprev.T)
        m["wkv_in"] = np.ascontiguousarray(inputs["state_wkv"][0, sl])
        for L in range(2):
            p = np.concatenate([inputs["p_prompt"][L, c], inputs["p_sample"][L, sl].reshape(NS, 256)], 0)
            m[f"pT{L}"] = np.ascontiguousarray(p.T)
        m["ptab"] = np.ascontiguousarray(inputs["page_table"][sl]).astype(np.int32)
        maps.append(m)
    return maps, {k: v.shape for k, v in w.items()}, {k: v.shape for k, v in cst.items()}, npool * 128


def kernel(**inputs):
    inputs = {k: np.asarray(v) for k, v in inputs.items()}
    n = 8
    maps, wsh, csh, nrows = host_inputs(inputs, list(range(n)))
    nc = build_program(wsh, csh, nrows)
    res = run_bass_kernel_spmd(nc, maps, core_ids=list(range(n)))
    R = res.results
    f = np.float32
    y_p = np.stack([R[c]["y"][:NP_] for c in range(n)]).astype(f)
    y_s = np.concatenate([R[c]["y"][NP_:].reshape(16, 4, D) for c in range(n)]).astype(f)
    sh_p = np.stack([R[c]["shift_p"][0] for c in range(n)])[None].astype(f)
    wkv_p = np.stack([R[c]["wkv_p"] for c in range(n)])[None].astype(f)
    k_p = np.stack([R[c]["k_o"][:NP_].reshape(NP_, 8, 128) for c in range(n)]).astype(f)
    v_p = np.stack([R[c]["v_o"][:NP_].reshape(NP_, 8, 128) for c in range(n)]).astype(f)
    sh_s = np.concatenate([R[c]["shift_s"] for c in range(n)])[None].astype(f)
    wkv_s = np.concatenate([R[c]["wkv_s"] for c in range(n)])[None].astype(f)
    k_s = np.concatenate([R[c]["k_o"][NP_:].reshape(16, 4, 8, 128) for c in range(n)]).astype(f)
    v_s = np.concatenate([R[c]["v_o"][NP_:].reshape(16, 4, 8, 128) for c in range(n)]).astype(f)
    return (y_p, y_s, sh_p, wkv_p, k_p, v_p, sh_s, wkv_s, k_s, v_s)
```

```python
import numpy as np
from contextlib import ExitStack
import concourse.bass as bass
import concourse.mybir as mybir
from concourse.bass_utils import run_bass_kernel_spmd

F32 = mybir.dt.float32
BF16 = mybir.dt.bfloat16
I32 = mybir.dt.int32
U32 = mybir.dt.uint32
ALU = mybir.AluOpType
AF = mybir.ActivationFunctionType
AX = mybir.AxisListType

D = 1024
NP_ = 2048
NS = 64
NT = NP_ + NS
NBLK = 17
DN_ALPHA = 4.0 ** 0.25
LN_EPS = 1e-5


def blk_rows(b):
    return (b * 128, 128) if b < 16 else (2048, 64)


class Buf:
    __slots__ = ("w", "r", "x")

    def __init__(self, x=False):
        self.w = None
        self.r = {}
        self.x = x


class Tile:
    def __init__(self, t):
        self.t = t
        self.b = Buf()

    def __getitem__(self, k):
        return self.t[k]


def _b(x):
    return x.b if isinstance(x, Tile) else x


class Prog:
    ENG = ["pe", "dve", "act", "pool", "sp"]
    EPOCH = 30000
    NDMA = 32

    def __init__(self, nc, es):
        self.nc = nc
        self.es = es
        self.ops = {e: [] for e in self.ENG}
        self.cnt = {e: 0 for e in self.ENG}
        self.sems = {e: [] for e in self.ENG}
        self.dsem = [es.enter_context(nc.semaphore(f"dq{i}")) for i in range(self.NDMA)]
        self.dval = [0] * self.NDMA
        self.dnext = 0
        self.seen = {e: {} for e in self.ENG}

    def _sem(self, e, epoch):
        while len(self.sems[e]) <= epoch:
            self.sems[e].append(self.es.enter_context(self.nc.semaphore(f"c_{e}{len(self.sems[e])}")))
        return self.sems[e][epoch]

    def _need(self, e, tok, waits):
        if tok[0] == "c":
            _, e2, idx = tok
            if e2 == e and e == "pe":
                return
            key = ("c", e2)
            if self.seen[e].get(key, 0) >= idx:
                return
            self.seen[e][key] = idx
            waits.append((self._sem(e2, (idx - 1) // self.EPOCH), (idx - 1) % self.EPOCH + 1))
        else:
            _, k, val = tok
            key = ("d", k)
            if self.seen[e].get(key, 0) >= val:
                return
            self.seen[e][key] = val
            waits.append((self.dsem[k], val))

    def _deps(self, e, reads, writes):
        waits = []
        for b in reads:
            if b.w is not None:
                self._need(e, b.w, waits)
            if b.x:
                for key, val in b.r.items():
                    if key[1] != e:
                        self._need(e, (key[0], key[1], val), waits)
        for b in writes:
            if b.w is not None:
                self._need(e, b.w, waits)
            for key, val in b.r.items():
                self._need(e, (key[0], key[1], val), waits)
        return waits

    def _mark(self, tok, reads, writes):
        key = (tok[0], tok[1])
        for b in reads:
            if b.r.get(key, 0) < tok[2]:
                b.r[key] = tok[2]
        for b in writes:
            b.w = tok
            b.r = {}

    def op(self, e, fn, reads=(), writes=()):
        reads = [_b(x) for x in reads]
        writes = [_b(x) for x in writes]
        waits = self._deps(e, reads, writes)
        self.cnt[e] += 1
        idx = self.cnt[e]
        self.ops[e].append((waits, fn, self._sem(e, (idx - 1) // self.EPOCH), 1))
        self._mark(("c", e, idx), reads, writes)

    def dma(self, q, fn, reads=(), writes=()):
        reads = [_b(x) for x in reads]
        writes = [_b(x) for x in writes]
        waits = self._deps(q, reads, writes)
        k = self.dnext
        self.dnext = (k + 1) % self.NDMA
        if self.dval[k] > 0:
            self._need(q, ("d", k, self.dval[k]), waits)
        self.dval[k] += 16
        self.ops[q].append((waits, fn, self.dsem[k], 16))
        self._mark(("d", k, self.dval[k]), reads, writes)

    def barrier(self):
        for e in self.ENG:
            waits = []
            for e2 in self.ENG:
                if e2 != e and self.cnt[e2] > 0:
                    self._need(e, ("c", e2, self.cnt[e2]), waits)
            for k in range(self.NDMA):
                if self.dval[k] > 0:
                    self._need(e, ("d", k, self.dval[k]), waits)
            if waits:
                self.ops[e].append((waits, None, None, 0))

    def emit(self):
        P = self

        def replay(e, eng):
            for waits, fn, sem, inc in P.ops[e]:
                for s, v in waits:
                    eng.wait_ge(s, v)
                if fn is not None:
                    fn(eng).then_inc(sem, inc)
            P.ops[e] = []

        with self.nc.Block() as block:
            @block.tensor
            def _(eng):
                replay("pe", eng)

            @block.vector
            def _(eng):
                replay("dve", eng)

            @block.scalar
            def _(eng):
                replay("act", eng)

            @block.gpsimd
            def _(eng):
                replay("pool", eng)

            @block.sync
            def _(eng):
                replay("sp", eng)


class Ctx:
    def __init__(self, nc, es):
        self.nc = nc
        self.es = es
        self.P = Prog(nc, es)
        self.ses = None
        ps = es.enter_context(nc.psum_tensor("psum_all", [128, 8, 512], F32))
        self.ps = ps
        self.pb = [Buf(True) for _ in range(8)]
        self.dr = {}

    def sb(self, name, shape, dtype=F32):
        return Tile(self.ses.enter_context(self.nc.sbuf_tensor(f"s{self.stage_id}_{name}", shape, dtype)))

    def dram(self, name, shape, dtype=F32, kind="Internal"):
        t = Tile(self.nc.dram_tensor(name, shape, dtype, kind=kind))
        self.dr[name] = t
        return t

    def stage_begin(self):
        self.stage_id = getattr(self, "stage_id", 0) + 1
        self.ses = ExitStack()
        self.ses.__enter__()

    def stage_end(self):
        self.P.barrier()
        if getattr(self, "scopes", False):
            import inspect
            nm = inspect.stack()[1].function + f"_{self.stage_id}"
            with self.nc.named_scope(nm):
                self.P.emit()
        else:
            self.P.emit()
        self.ses.__exit__(None, None, None)
        self.ses = None

    def mm(self, out, lhsT, rhs, start, stop, r, w):
        self.P.op("pe", lambda e: e.matmul(out, lhsT=lhsT, rhs=rhs, start=start, stop=stop), r, w)

    def tr(self, out, in_, ident, r, w):
        self.P.op("pe", lambda e: e.transpose(out=out, in_=in_, identity=ident), r, w)

    def dma(self, out, in_, r, w, q="sp", slow=False):
        if slow:
            self.P.dma(q, lambda e: e.dma_start(out=out, in_=in_, allow_slow_non_contiguous=True), r, w)
        else:
            self.P.dma(q, lambda e: e.dma_start(out=out, in_=in_), r, w)

    def copy(self, eng, out, in_, r, w):
        if eng == "act":
            self.P.op("act", lambda e: e.copy(out=out, in_=in_), r, w)
        else:
            self.P.op(eng, lambda e: e.tensor_copy(out=out, in_=in_), r, w)

    def actf(self, out, in_, func, r, w, bias=None, scale=None, accum_out=None):
        kw = {}
        if bias is not None:
            kw["bias"] = bias
        if scale is not None:
            kw["scale"] = scale
        if accum_out is not None:
            kw["accum_out"] = accum_out
        self.P.op("act", lambda e: e.activation(out=out, in_=in_, func=func, **kw), r, w)

    def tt(self, eng, out, in0, in1, op, r, w):
        self.P.op(eng, lambda e: e.tensor_tensor(out=out, in0=in0, in1=in1, op=op), r, w)

    def ts(self, eng, out, in0, s1, s2, op0, op1, r, w):
        if op1 is None:
            self.P.op(eng, lambda e: e.tensor_scalar(out=out, in0=in0, scalar1=s1, scalar2=None, op0=op0), r, w)
        else:
            self.P.op(eng, lambda e: e.tensor_scalar(out=out, in0=in0, scalar1=s1, scalar2=s2, op0=op0, op1=op1), r, w)

    def stt(self, out, in0, scalar, in1, op0, op1, r, w):
        self.P.op("dve", lambda e: e.scalar_tensor_tensor(out=out, in0=in0, scalar=scalar, in1=in1, op0=op0, op1=op1), r, w)

    def red(self, out, in_, op, r, w, axis=AX.X):
        self.P.op("dve", lambda e: e.tensor_reduce(out=out, in_=in_, axis=axis, op=op), r, w)

    def memset(self, eng, out, val, w):
        self.P.op(eng, lambda e: e.memset(out, val), [], w)

    def recip(self, out, in_, r, w):
        self.P.op("dve", lambda e: e.reciprocal(out=out, in_=in_), r, w)

    def max8(self, out, in_, r, w):
        self.P.op("dve", lambda e: e.max(out=out, in_=in_), r, w)

    def maxidx(self, out, in_max, in_values, r, w):
        self.P.op("dve", lambda e: e.max_index(out=out, in_max=in_max, in_values=in_values), r, w)

    def mrep(self, out, rep, vals, r, w):
        self.P.op("dve", lambda e: e.match_replace(out=out, in_to_replace=rep, in_values=vals, imm_value=-1e30), r, w)

    def dvefn(self, fn, r, w):
        self.P.op("dve", fn, r, w)


def layer_norm_tok(C, nr, s, gam, bet, outp, scr):
    st, mv, rs = scr["st"], scr["mv"], scr["rs"]
    C.dvefn(lambda e: e.bn_stats(out=st[:nr, 0, :], in_=s[:nr, 0:512]), [s], [st])
    C.dvefn(lambda e: e.bn_stats(out=st[:nr, 1, :], in_=s[:nr, 512:1024]), [s], [st])
    C.dvefn(lambda e: e.bn_aggr(out=mv[:nr, :], in_=st[:nr, :, :].rearrange("p a b -> p (a b)")), [st], [mv])
    C.actf(rs[:nr, :], mv[:nr, 1:2], AF.Sqrt, [mv], [rs], bias=scr["eps"][:nr, 0:1])
    C.recip(rs[:nr, :], rs[:nr, :], [rs], [rs])
    C.ts("dve", outp[:nr, :], s[:nr, :], mv[:nr, 0:1], rs[:nr, 0:1], ALU.subtract, ALU.mult, [s, mv, rs], [outp])
    C.tt("dve", outp[:nr, :], outp[:nr, :], gam[:nr, :], ALU.mult, [outp, gam], [outp])
    C.tt("dve", outp[:nr, :], outp[:nr, :], bet[:nr, :], ALU.add, [outp, bet], [outp])


def to_featmajor(C, nr, xb16, outT, c0, identb, bank):
    pst = C.ps[:, bank, :].bitcast(BF16)
    for kc in range(8):
        C.tr(pst[:, kc * 128:kc * 128 + nr], xb16[:nr, kc * 128:(kc + 1) * 128], identb[:nr, :nr], [xb16, identb], [C.pb[bank]])
    C.copy("act", outT[:, :, c0:c0 + nr], pst.rearrange("p (k t) -> p k t", k=8)[:, :, :nr], [C.pb[bank]], [outT])


def ffn_stage_a(C, L, W, x_in, h_in, xa_d, xaT_d, idx_d):
    nc = C.nc
    C.stage_begin()
    cst = C.cst
    identf, identb, iotaf = load_consts(C)
    g1 = C.sb("g1", [128, D]); b1 = C.sb("b1", [128, D])
    C.dma(g1[:], W["ln1_g"][L:L + 1, :].partition_broadcast(128), [], [g1])
    C.dma(b1[:], W["ln1_b"][L:L + 1, :].partition_broadcast(128), [], [b1])
    wq = C.sb("wq", [128, 8, 2048], BF16)
    wq_src = W["peer_w_q"][L].rearrange("(k p) n -> p k n", p=128)
    for i in range(4):
        C.dma(wq[:, :, i * 512:(i + 1) * 512], wq_src[:, :, i * 512:(i + 1) * 512], [], [wq], q="pool")
    bq = C.sb("bq", [128, 16])
    C.dma(bq[:], W["peer_b_qT"][L], [], [bq])
    skT = C.sb("skT", [128, 2, 128], BF16)
    C.dma(skT[:], W["peer_skT"][L], [], [skT], q="pool")
    eps = C.sb("eps", [128, 1]); C.memset("dve", eps[:], LN_EPS, [eps])
    scr = {"st": C.sb("st", [128, 2, 6]), "mv": C.sb("mv", [128, 2]), "rs": C.sb("rs", [128, 1]), "eps": eps}
    NB = 2
    xin = [C.sb(f"xin{i}", [128, D]) for i in range(NB)]
    hin = [C.sb(f"hin{i}", [128, D]) for i in range(NB)]
    xa = [C.sb(f"xa{i}", [128, D]) for i in range(NB)]
    xab = [C.sb(f"xab{i}", [128, D], BF16) for i in range(NB)]
    xaT = [C.sb(f"xaT{i}", [128, 8, 128], BF16) for i in range(NB)]
    qT = [C.sb(f"qT{i}", [128, 16, 128], BF16) for i in range(NB)]
    sc = [C.sb(f"sc{i}", [128, 16, 128]) for i in range(NB)]
    top = C.sb("top", [128, 16, 16]); idxu = C.sb("idxu", [128, 16, 16], U32); idxf = C.sb("idxf", [128, 16, 16])
    cand = C.sb("cand", [128, 8, 256])
    sel = C.sb("sel", [128, 8, 16]); cidx = C.sb("cidx", [128, 8, 16], U32); cf = C.sb("cf", [128, 8, 16])
    thr = C.sb("thr", [128, 15])
    for m in range(15):
        C.memset("pool", thr[:, m:m + 1], 16.0 * (m + 1) - 0.5, [thr])
    ge = C.sb("ge", [128, 128, 15]); af = C.sb("af", [128, 8, 16]); bf_ = C.sb("bf_", [128, 8, 16])
    oh = C.sb("oh", [128, 8, 16, 16])
    res3 = [C.sb(f"res3_{i}", [128, 3, 128]) for i in range(NB)]
    mx = C.sb("mx", [128, 8]); zs = C.sb("zs", [128, 8])
    resT = [C.sb(f"resT{i}", [128, 3, 128]) for i in range(NB)]
    scbs = [[Buf() for _ in range(16)] for _ in range(NB)]
    topb = [Buf() for _ in range(16)]; idxb = [Buf() for _ in range(16)]
    candb = [Buf() for _ in range(8)]; selb = [Buf() for _ in range(8)]; cidxb = [Buf() for _ in range(8)]

    for blk in range(NBLK):
        r0, nr = blk_rows(blk)
        s_ = blk % NB
        X, H, XA, XAB, XAT, QT, SC, R3, RT = xin[s_], hin[s_], xa[s_], xab[s_], xaT[s_], qT[s_], sc[s_], res3[s_], resT[s_]
        C.dma(X[:nr, :], x_in[r0:r0 + nr, :], [x_in], [X])
        C.dma(H[:nr, :], h_in[r0:r0 + nr, :], [h_in], [H])
        C.stt(X[:nr, :], X[:nr, :], DN_ALPHA, H[:nr, :], ALU.mult, ALU.add, [X, H], [X])
        layer_norm_tok(C, nr, X, g1, b1, XA, scr)
        C.dma(xa_d[r0:r0 + nr, :], XA[:nr, :], [XA], [xa_d])
        C.copy("act", XAB[:nr, :], XA[:nr, :], [XA], [XAB])
        to_featmajor(C, nr, XAB, XAT, 0, identb, 0)
        C.dma(xaT_d[:, :, r0:r0 + nr], XAT[:, :, :nr], [XAT], [xaT_d])
        for c16 in range(16):
            bank = 1 + (c16 // 4) % 2
            pq = C.ps[:, bank, (c16 % 4) * 128:(c16 % 4) * 128 + nr]
            for kc in range(8):
                C.mm(pq, wq[:, kc, c16 * 128:(c16 + 1) * 128], XAT[:, kc, :nr], kc == 0, kc == 7, [wq, XAT], [C.pb[bank]])
            C.actf(QT[:, c16, :nr], pq, AF.Identity, [C.pb[bank], bq], [QT], bias=bq[:, c16:c16 + 1])
        for c16 in range(16):
            bank = 3 + c16 // 4
            C.mm(C.ps[:nr, bank, (c16 % 4) * 128:(c16 % 4 + 1) * 128], QT[:, c16, :nr], skT[:, c16 % 2, :], True, True, [QT, skT], [C.pb[bank]])
        for bk in range(4):
            C.copy("act" if bk % 2 else "dve", SC[:nr, bk * 4:(bk + 1) * 4, :], C.ps[:nr, 3 + bk, :].rearrange("p (a n) -> p a n", a=4), [C.pb[3 + bk]], scbs[s_][4 * bk:4 * bk + 4])
        if getattr(C, "dbg", None) and blk == 0:
            C.dma(C.dbg["sc"][:], SC[:], scbs[s_], [C.dbg["sc"]])
        scb = scbs[s_]
        for g4 in range(4):
            sgs = range(4 * g4, 4 * g4 + 4)
            for sg in sgs:
                C.max8(top[:nr, sg, 0:8], SC[:nr, sg, :], [scb[sg]], [topb[sg]])
            for sg in sgs:
                C.maxidx(idxu[:nr, sg, 0:8], top[:nr, sg, 0:8], SC[:nr, sg, :], [scb[sg], topb[sg]], [idxb[sg]])
            for sg in sgs:
                C.mrep(SC[:nr, sg, :], top[:nr, sg, 0:8], SC[:nr, sg, :], [scb[sg], topb[sg]], [scb[sg]])
            for sg in sgs:
                C.max8(top[:nr, sg, 8:16], SC[:nr, sg, :], [scb[sg]], [topb[sg]])
            for sg in sgs:
                C.maxidx(idxu[:nr, sg, 8:16], top[:nr, sg, 8:16], SC[:nr, sg, :], [scb[sg], topb[sg]], [idxb[sg]])
        C.copy("dve", idxf[:nr], idxu[:nr], idxb, [idxf])
        tv = top[:nr].rearrange("p (h c) k -> p h c k", c=2)
        C.tt("dve", cand[:nr].rearrange("p h (a b) -> p h a b", a=16), tv[:, :, 0, :].unsqueeze(3).to_broadcast([nr, 8, 16, 16]),
             tv[:, :, 1, :].unsqueeze(2).to_broadcast([nr, 8, 16, 16]), ALU.add, topb, candb)
        for g4 in range(2):
            hs = range(4 * g4, 4 * g4 + 4)
            for h in hs:
                C.max8(sel[:nr, h, 0:8], cand[:nr, h, :], [candb[h]], [selb[h]])
            for h in hs:
                C.maxidx(cidx[:nr, h, 0:8], sel[:nr, h, 0:8], cand[:nr, h, :], [candb[h], selb[h]], [cidxb[h]])
            for h in hs:
                C.mrep(cand[:nr, h, :], sel[:nr, h, 0:8], cand[:nr, h, :], [candb[h], selb[h]], [candb[h]])
            for h in hs:
                C.max8(sel[:nr, h, 8:16], cand[:nr, h, :], [candb[h]], [selb[h]])
            for h in hs:
                C.maxidx(cidx[:nr, h, 8:16], sel[:nr, h, 8:16], cand[:nr, h, :], [candb[h], selb[h]], [cidxb[h]])
        C.copy("dve", cf[:nr], cidx[:nr], cidxb, [cf])
        if getattr(C, "dbg", None) and blk == 0:
            C.dma(C.dbg["top"][:], top[:], topb, [C.dbg["top"]])
            C.dma(C.dbg["idxf"][:], idxf[:], [idxf], [C.dbg["idxf"]])
            C.dma(C.dbg["sel"][:], sel[:], selb, [C.dbg["sel"]])
            C.dma(C.dbg["cf"][:], cf[:], [cf], [C.dbg["cf"]])
        cfl = cf[:nr].rearrange("p h k -> p (h k)")
        C.tt("dve", ge[:nr], cfl.unsqueeze(2).to_broadcast([nr, 128, 15]), thr[:nr, :].unsqueeze(1).to_broadcast([nr, 128, 15]), ALU.is_ge, [cf, thr], [ge])
        C.red(af[:nr].rearrange("p h k -> p (h k)"), ge[:nr], ALU.add, [ge], [af])
        C.stt(bf_[:nr], af[:nr], -16.0, cf[:nr], ALU.mult, ALU.add, [af, cf], [bf_])
        iv = idxf[:nr].rearrange("p (h c) k -> p h c k", c=2)
        for which, sel_ab in ((0, af), (1, bf_)):
            C.tt("dve", oh[:nr], sel_ab[:nr].unsqueeze(3).to_broadcast([nr, 8, 16, 16]),
                 iotaf[:nr, 0:16].unsqueeze(1).unsqueeze(1).to_broadcast([nr, 8, 16, 16]), ALU.is_equal, [sel_ab, iotaf], [oh])
            C.tt("dve", oh[:nr], oh[:nr], iv[:, :, which, :].unsqueeze(2).to_broadcast([nr, 8, 16, 16]), ALU.mult, [oh, idxf], [oh])
            C.red(R3[:nr, which, :], oh[:nr].rearrange("p h k a -> p (h k) a"), ALU.add, [oh], [R3])
        C.ts("dve", mx[:nr, :], sel[:nr, :, 0], -1.0, None, ALU.mult, None, selb, [mx])
        gv = R3[:nr, 2, :].rearrange("p (h k) -> p h k", h=8)
        C.tt("dve", gv, sel[:nr], mx[:nr, :].unsqueeze(2).to_broadcast([nr, 8, 16]), ALU.add, selb + [mx], [R3])
        C.actf(gv, gv, AF.Exp, [R3], [R3])
        C.red(zs[:nr, :], gv, ALU.add, [R3], [zs])
        C.recip(zs[:nr, :], zs[:nr, :], [zs], [zs])
        C.tt("dve", gv, gv, zs[:nr, :].unsqueeze(2).to_broadcast([nr, 8, 16]), ALU.mult, [R3, zs], [R3])
        for q3 in range(3):
            C.tr(C.ps[:, 7, q3 * 128:q3 * 128 + nr], R3[:nr, q3, :], identf[:nr, :nr], [R3, identf], [C.pb[7]])
        C.copy("act", RT[:, :, :nr], C.ps[:, 7, 0:384].rearrange("p (a t) -> p a t", a=3)[:, :, :nr], [C.pb[7]], [RT])
        C.dma(idx_d[:, :, r0:r0 + nr], RT[:, :, :nr], [RT], [idx_d])
    C.stage_end()


def load_consts(C):
    identf = C.sb("identf", [128, 128]); identb = C.sb("identb", [128, 128], BF16); iotaf = C.sb("iotaf", [128, 128])
    C.dma(identf[:], C.cst["ident"], [], [identf])
    C.dma(iotaf[:], C.cst["iota"], [], [iotaf])
    C.copy("dve", identb[:], identf[:], [identf], [identb])
    return identf, identb, iotaf


CHUNKS = [[0, 1, 2], [3, 4, 5], [6, 7, 8], [9, 10, 11], [12, 13, 14], [15, 16]]


def ffn_stage_b(C, L, W, xaT_d, idx_d, c_d, chunks=None):
    C.stage_begin()
    identf, identb, iotaf = load_consts(C)
    TC = 384
    Gall = C.sb("Gall", [128, 128, TC], BF16)
    xaT = C.sb("xaTc", [128, 8, TC], BF16)
    idxT = C.sb("idxTc", [128, 3, TC])
    NAB = 4
    At = [C.sb(f"At{i}", [128, 128], BF16) for i in range(NAB)]
    Bt = [C.sb(f"Bt{i}", [128, 128], BF16) for i in range(NAB)]
    NW = 4
    ut = [C.sb(f"ut{i}", [128, 8, 128], BF16) for i in range(NW)]
    vt = [C.sb(f"vt{i}", [128, D], BF16) for i in range(NW)]
    hg = [C.sb(f"hg{i}", [128, TC], BF16) for i in range(2)]
    actT = [C.sb(f"actT{i}", [128, TC], BF16) for i in range(2)]
    cout = [C.sb(f"cout{i}", [128, D]) for i in range(2)]
    U_src = W["peer_uh"][L]
    V_src = W["peer_v"][L]
    for ch in (chunks if chunks is not None else CHUNKS):
        cols = []
        c0 = 0
        for blk in ch:
            r0, nr = blk_rows(blk)
            cols.append((blk, r0, nr, c0))
            c0 += nr
        tc = c0
        R0 = cols[0][1]
        C.dma(xaT[:, :, :tc], xaT_d[:, :, R0:R0 + tc], [xaT_d], [xaT])
        C.dma(idxT[:, :, :tc], idx_d[:, :, R0:R0 + tc], [idx_d], [idxT])
        for t in range(tc):
            s_ = t % NAB
            C.ts("dve", At[s_][:], iotaf[:], idxT[:, 0, t:t + 1], idxT[:, 2, t:t + 1], ALU.is_equal, ALU.mult, [iotaf, idxT], [At[s_]])
            C.ts("dve", Bt[s_][:], iotaf[:], idxT[:, 1, t:t + 1], None, ALU.is_equal, None, [iotaf, idxT], [Bt[s_]])
            bank = (t // 4) % 2
            C.mm(C.ps[:, bank, (t % 4) * 128:(t % 4 + 1) * 128], Bt[s_][:], At[s_][:], True, True, [At[s_], Bt[s_]], [C.pb[bank]])
            if t % 4 == 3 or t == tc - 1:
                n = t % 4 + 1
                t0 = t - (n - 1)
                C.copy("act", Gall[:, :, t0:t0 + n], C.ps[:, bank, 0:n * 128].rearrange("p (q i) -> p i q", q=n), [C.pb[bank]], [Gall])
        def emit_h(i1):
            w_ = i1 % NW
            C.dma(ut[w_][:], U_src[i1].rearrange("p (k e) -> p k e", k=8), [], [ut[w_]], q="pool")
            C.dma(vt[w_][:], V_src[i1 * 128:(i1 + 1) * 128, :], [], [vt[w_]], q="pool")
            hb = 6 + i1 % 2
            for kc in range(8):
                C.mm(C.ps[:, hb, :tc], ut[w_][:, kc, :], xaT[:, kc, :tc], kc == 0, kc == 7, [ut[w_], xaT], [C.pb[hb]])

        def emit_rest(i1):
            w_ = i1 % NW
            hb = 6 + i1 % 2
            HG, AT = hg[i1 % 2], actT[i1 % 2]
            C.actf(HG[:, :tc], C.ps[:, hb, :tc], AF.Gelu_apprx_tanh, [C.pb[hb]], [HG])
            C.tt("dve", AT[:, :tc], HG[:, :tc], Gall[:, i1, :tc], ALU.mult, [HG, Gall], [AT])
            for bi, (blk, r0, nr, cc) in enumerate(cols):
                for nh in range(2):
                    bank = 2 * bi + nh
                    C.mm(C.ps[:nr, bank, :], AT[:, cc:cc + nr], vt[w_][:, nh * 512:(nh + 1) * 512], i1 == 0, i1 == 127, [AT, vt[w_]], [C.pb[bank]])

        emit_h(0)
        for i1 in range(128):
            if i1 + 1 < 128:
                emit_h(i1 + 1)
            emit_rest(i1)
        for bi, (blk, r0, nr, cc) in enumerate(cols):
            co = cout[bi % 2]
            C.copy("act", co[:nr, 0:512], C.ps[:nr, 2 * bi, :], [C.pb[2 * bi]], [co])
            C.copy("dve", co[:nr, 512:1024], C.ps[:nr, 2 * bi + 1, :], [C.pb[2 * bi + 1]], [co])
            C.dma(c_d[r0:r0 + nr, :], co[:nr, :], [co], [c_d])
    C.stage_end()


def ffn_stage_c(C, L, W, xa_d, c_d, pT_d, x_out, xoT_d=None):
    C.stage_begin()
    identf, identb, iotaf = load_consts(C)
    g2 = C.sb("g2", [128, D]); b2 = C.sb("b2", [128, D])
    C.dma(g2[:], W["ln2_g"][L:L + 1, :].partition_broadcast(128), [], [g2])
    C.dma(b2[:], W["ln2_b"][L:L + 1, :].partition_broadcast(128), [], [b2])
    wg = C.sb("wg", [128, 8, D], BF16)
    wg_src = W["ple_w_g"][L].rearrange("(k p) n -> p k n", p=128)
    for i in range(2):
        C.dma(wg[:, :, i * 512:(i + 1) * 512], wg_src[:, :, i * 512:(i + 1) * 512], [], [wg], q="pool")
    wp = C.sb("wp", [128, 2, D], BF16)
    C.dma(wp[:], W["ple_w_p"][L].rearrange("(k p) n -> p k n", p=128), [], [wp], q="pool")
    bg = C.sb("bg", [1, D], BF16)
    C.dma(bg[:], W["ple_b_g"][L:L + 1, :], [], [bg], q="pool")
    ones = C.sb("ones1", [1, 128], BF16); C.memset("dve", ones[:], 1.0, [ones])
    eps = C.sb("eps", [128, 1]); C.memset("dve", eps[:], LN_EPS, [eps])
    scr = {"st": C.sb("st", [128, 2, 6]), "mv": C.sb("mv", [128, 2]), "rs": C.sb("rs", [128, 1]), "eps": eps}
    NB = 2
    xa = [C.sb(f"xa{i}", [128, D]) for i in range(NB)]
    cc = [C.sb(f"cc{i}", [128, D]) for i in range(NB)]
    xb = [C.sb(f"xb{i}", [128, D]) for i in range(NB)]
    xbb = [C.sb(f"xbb{i}", [128, D], BF16) for i in range(NB)]
    xbT = [C.sb(f"xbT{i}", [128, 8, 128], BF16) for i in range(NB)]
    pT = [C.sb(f"pT{i}", [128, 2, 128], BF16) for i in range(NB)]
    sig = [C.sb(f"sig{i}", [128, D]) for i in range(NB)]
    xo = [C.sb(f"xo{i}", [128, D]) for i in range(NB)]
    xob = [C.sb(f"xob{i}", [128, D], BF16) for i in range(NB)]
    xoT = [C.sb(f"xoT{i}", [128, 8, 128], BF16) for i in range(NB)]
    for blk in range(NBLK):
        r0, nr = blk_rows(blk)
        s_ = blk % NB
        XA, CC, XB, XBB, XBT, PT, SG, XO = xa[s_], cc[s_], xb[s_], xbb[s_], xbT[s_], pT[s_], sig[s_], xo[s_]
        pbase = 4 * s_
        C.dma(XA[:nr, :], xa_d[r0:r0 + nr, :], [xa_d], [XA])
        C.dma(CC[:nr, :], c_d[r0:r0 + nr, :], [c_d], [CC])
        C.dma(PT[:, :, :nr], pT_d[:, r0:r0 + nr].rearrange("(k p) t -> p k t", p=128), [], [PT], q="pool")
        C.stt(XA[:nr, :], XA[:nr, :], DN_ALPHA, CC[:nr, :], ALU.mult, ALU.add, [XA, CC], [XA])
        layer_norm_tok(C, nr, XA, g2, b2, XB, scr)
        C.copy("act", XBB[:nr, :], XB[:nr, :], [XB], [XBB])
        to_featmajor(C, nr, XBB, XBT, 0, identb, pbase)
        for nh in range(2):
            bank = pbase + nh
            for kc in range(8):
                C.mm(C.ps[:nr, bank, :], XBT[:, kc, :nr], wg[:, kc, nh * 512:(nh + 1) * 512], kc == 0, False, [XBT, wg], [C.pb[bank]])
            C.mm(C.ps[:nr, bank, :], ones[0:1, :nr], bg[0:1, nh * 512:(nh + 1) * 512], False, True, [ones, bg], [C.pb[bank]])
            C.actf(SG[:nr, nh * 512:(nh + 1) * 512], C.ps[:nr, bank, :], AF.Sigmoid, [C.pb[bank]], [SG])
        for nh in range(2):
            bank = pbase + 2 + nh
            for kc in range(2):
                C.mm(C.ps[:nr, bank, :], PT[:, kc, :nr], wp[:, kc, nh * 512:(nh + 1) * 512], kc == 0, kc == 1, [PT, wp], [C.pb[bank]])
            C.tt("dve", SG[:nr, nh * 512:(nh + 1) * 512], SG[:nr, nh * 512:(nh + 1) * 512], C.ps[:nr, bank, :], ALU.mult, [SG, C.pb[bank]], [SG])
        C.tt("dve", XO[:nr, :], SG[:nr, :], XB[:nr, :], ALU.add, [SG, XB], [XO])
        C.dma(x_out[r0:r0 + nr, :], XO[:nr, :], [XO], [x_out])
        if xoT_d is not None:
            XOB, XOT = xob[s_], xoT[s_]
            C.copy("act", XOB[:nr, :], XO[:nr, :], [XO], [XOB])
            to_featmajor(C, nr, XOB, XOT, 0, identb, pbase)
            C.dma(xoT_d[:, :, r0:r0 + nr], XOT[:, :, :nr], [XOT], [xoT_d])
    C.stage_end()


def make_consts():
    ar = np.arange(128)
    pm = np.zeros((128, 2), np.float32)
    pm[:64, 0] = 1.0
    pm[64:, 1] = 1.0
    return {
        "ident": np.eye(128, dtype=np.float32),
        "iota": np.ascontiguousarray(np.broadcast_to(np.arange(128, dtype=np.float32)[None, :], (128, 128))),
        "lincl": (ar[:, None] <= ar[None, :]).astype(np.float32),
        "lstrict": (ar[:, None] < ar[None, :]).astype(np.float32),
        "llow": (ar[None, :] < ar[:, None]).astype(np.float32),
        "pm": pm,
        **attn_consts(),
    }


def attn_consts():
    p = np.arange(128, dtype=np.float64)
    slopes = 2.0 ** (-8.0 * np.arange(1, 9, dtype=np.float64) / 8.0)
    btab = slopes[None, :, None] * (p[:, None, None] - 127.0 - 128.0 * np.arange(16)[None, None, :])
    q = np.arange(4, dtype=np.float64)
    tabS = np.zeros((128, 16, 8, 2, 4))
    tabS += slopes[None, None, :, None, None] * (np.arange(16)[None, :, None, None, None] * 128.0 + p[:, None, None, None, None] - 2048.0 - q[None, None, None, None, :])
    tabN = np.full((128, 8, 2, 4), -1e30)
    for pp in range(4):
        for qq in range(4):
            if pp <= qq:
                tabN[pp, :, :, qq] = slopes[:, None] * (pp - qq)
    cA = np.zeros((8, 124)); cB = np.zeros((8, 124))
    for i in range(4):
        cA[i, 60 + i] = 1.0
        cB[4 + i, 60 + i] = -1.0
    f = lambda a: np.ascontiguousarray(a.astype(np.float32))
    return {"btab": f(btab), "tabS": f(tabS.reshape(128, 16, 64)), "tabN": f(tabN.reshape(128, 64)), "cmbA": f(cA), "cmbB": f(cB),
            "iotap": f(p.reshape(128, 1))}


def prep_weights(w):
    o = {}
    for k in ("ln1_g", "ln1_b", "ln2_g", "ln2_b", "peer_w_q", "peer_v", "ple_w_p", "ple_w_g", "ple_b_g"):
        if k in w:
            o[k] = np.ascontiguousarray(w[k])
    for k in ("rw_w_r", "rw_w_k", "rw_w_v", "rw_w0", "rw_w1", "rw_w2", "rw_a0", "rw_a1", "rw_a2", "rw_g1", "rw_g2", "rw_k_k", "rw_k_a",
              "rw_gn_g", "rw_gn_b", "rw_w_o", "da_w_k", "da_w_v", "da_w_q", "da_lam", "da_subln_g", "da_w_o"):
        if k in w:
            o[k] = np.ascontiguousarray(w[k])
    if "rw_mu" in w:
        o["rw_muT"] = np.ascontiguousarray(w["rw_mu"][0].reshape(6, 8, 128).transpose(2, 0, 1))
        o["rw_r_kf"] = np.ascontiguousarray(w["rw_r_k"].reshape(1, 1024))
    if "peer_b_q" in w:
        o["peer_b_qT"] = np.ascontiguousarray(w["peer_b_q"].reshape(2, 16, 128).transpose(0, 2, 1))
        o["peer_skT"] = np.ascontiguousarray(w["peer_subkeys"].transpose(0, 3, 1, 2))
        u = w["peer_u"].reshape(2, 128, 128, 8, 128)
        o["peer_uh"] = np.ascontiguousarray(u.transpose(0, 1, 4, 3, 2)).reshape(2, 128, 128, 1024)
    return o


class BankRot:
    def __init__(self):
        self.i1 = 0
        self.i2 = 0

    def one(self):
        b = self.i1 % 8
        self.i1 += 1
        return b

    def two(self):
        b = (self.i2 % 4) * 2
        self.i2 += 1
        return b

    def pair(self):
        b = self.two()
        return (b, b + 1)


def rw_stage_a(C, W, x0T_d, xpT_d, sc_d, blks=None):
    import os
    _STOP = int(os.environ.get('RWA_STOP', '99'))
    C.stage_begin()
    identf, identb, iotaf = load_consts(C)
    BR = BankRot()
    wts = {}
    for nm in ("rw_w_r", "rw_w_k", "rw_w_v"):
        t = C.sb(nm, [128, 8, D], BF16)
        src = W[nm][0].rearrange("(k p) n -> p k n", p=128)
        for i in range(2):
            C.dma(t[:, :, i * 512:(i + 1) * 512], src[:, :, i * 512:(i + 1) * 512], [], [t], q="pool")
        wts[nm] = t
    w1 = C.sb("w1", [128, 8, 64], BF16); C.dma(w1[:], W["rw_w1"][0].rearrange("(k p) n -> p k n", p=128), [], [w1], q="pool")
    a1 = C.sb("a1", [128, 8, 64], BF16); C.dma(a1[:], W["rw_a1"][0].rearrange("(k p) n -> p k n", p=128), [], [a1], q="pool")
    g1 = C.sb("g1w", [128, 8, 128], BF16); C.dma(g1[:], W["rw_g1"][0].rearrange("(k p) n -> p k n", p=128), [], [g1], q="pool")
    w2 = C.sb("w2", [64, D], BF16); C.dma(w2[:], W["rw_w2"][0], [], [w2], q="pool")
    a2 = C.sb("a2", [64, D], BF16); C.dma(a2[:], W["rw_a2"][0], [], [a2], q="pool")
    g2 = C.sb("g2w", [128, D], BF16); C.dma(g2[:], W["rw_g2"][0], [], [g2], q="pool")
    bc = {}
    for nm in ("rw_w0", "rw_a0", "rw_k_k", "rw_k_a", "rw_r_kf"):
        t = C.sb("b_" + nm, [128, D])
        C.dma(t[:], W[nm][0:1, :].partition_broadcast(128), [], [t])
        bc[nm] = t
    mu = C.sb("mu", [128, 6, 8]); C.dma(mu[:], W["rw_muT"], [], [mu])
    Lt = C.sb("Lt", [128, 128]); C.dma(Lt[:], C.cst["lincl"], [], [Lt])
    onec = C.sb("onec", [128, 2]); C.memset("dve", onec[:], 1.0, [onec])
    xc = C.sb("xc", [128, 8, 128]); xp = C.sb("xp", [128, 8, 128]); xx = C.sb("xx", [128, 8, 128])
    mix = [C.sb(f"mix{i}", [128, 8, 128], BF16) for i in range(6)]
    hwT = C.sb("hwT", [64, 128], BF16); haT = C.sb("haT", [64, 128], BF16); hgT = C.sb("hgT", [128, 128], BF16)
    r_ = C.sb("r_", [128, D]); k_ = C.sb("k_", [128, D]); v_ = C.sb("v_", [128, D]); lw = C.sb("lw", [128, D]); a_ = C.sb("a_", [128, D]); g_ = C.sb("g_", [128, D])
    kk = C.sb("kk", [128, D]); t1 = C.sb("t1", [128, D]); t2 = C.sb("t2", [128, D]); t3 = C.sb("t3", [128, D])
    ss = C.sb("ss", [128, 16]); bon = C.sb("bon", [128, 16]); wc = C.sb("wc", [128, 8])
    ob = [C.sb(f"ob{i}", [128, D], BF16) for i in range(5)]

    def v3(t, nr):
        return t[:nr, :].rearrange("p (h k) -> p h k", h=16)

    for blk in (range(NBLK) if blks is None else blks):
        r0, nr = blk_rows(blk)
        smp = blk == 16
        xsrc = x0T_d.t.rearrange("(k p) t -> p k t", p=128)
        psrc = xpT_d.t.rearrange("(k p) t -> p k t", p=128)
        C.dma(xc[:, :, :nr], xsrc[:, :, r0:r0 + nr], [], [xc])
        C.dma(xp[:, :, :nr], psrc[:, :, r0:r0 + nr], [], [xp])
        C.tt("dve", xx[:, :, :nr], xp[:, :, :nr], xc[:, :, :nr], ALU.subtract, [xp, xc], [xx])
        for i in range(6):
            for kc in range(8):
                C.stt(mix[i][:, kc, :nr], xx[:, kc, :nr], mu[:, i, kc:kc + 1], xc[:, kc, :nr], ALU.mult, ALU.add, [xx, xc, mu], [mix[i]])
        if _STOP == 1:
            continue
        xr, xw, xk, xv, xa, xg = mix
        for nm, xm, dst in (("rw_w_r", xr, r_), ("rw_w_k", xk, k_), ("rw_w_v", xv, v_)):
            b2 = BR.two()
            for nh in range(2):
                for kc in range(8):
                    C.mm(C.ps[:nr, b2 + nh, :], xm[:, kc, :nr], wts[nm][:, kc, nh * 512:(nh + 1) * 512], kc == 0, kc == 7, [xm, wts[nm]], [C.pb[b2 + nh]])
                C.copy("act", dst[:nr, nh * 512:(nh + 1) * 512], C.ps[:nr, b2 + nh, :], [C.pb[b2 + nh]], [dst])
        if _STOP == 2:
            continue
        b1 = BR.one()
        for kc in range(8):
            C.mm(C.ps[:64, b1, :nr], w1[:, kc, :], xw[:, kc, :nr], kc == 0, kc == 7, [w1, xw], [C.pb[b1]])
        C.actf(hwT[:, :nr], C.ps[:64, b1, :nr], AF.Tanh, [C.pb[b1]], [hwT])
        b2 = BR.two()
        for nh in range(2):
            C.mm(C.ps[:nr, b2 + nh, :], hwT[:, :nr], w2[:, nh * 512:(nh + 1) * 512], True, True, [hwT, w2], [C.pb[b2 + nh]])
            C.tt("dve", lw[:nr, nh * 512:(nh + 1) * 512], C.ps[:nr, b2 + nh, :], bc["rw_w0"][:nr, nh * 512:(nh + 1) * 512], ALU.add, [C.pb[b2 + nh], bc["rw_w0"]], [lw])
        C.actf(lw[:nr, :], lw[:nr, :], AF.Sigmoid, [lw], [lw])
        C.ts("dve", lw[:nr, :], lw[:nr, :], -float(np.exp(-0.5)), None, ALU.mult, None, [lw], [lw])
        if _STOP == 3:
            continue
        b1 = BR.one()
        for kc in range(8):
            C.mm(C.ps[:64, b1, :nr], a1[:, kc, :], xa[:, kc, :nr], kc == 0, kc == 7, [a1, xa], [C.pb[b1]])
        C.copy("act", haT[:, :nr], C.ps[:64, b1, :nr], [C.pb[b1]], [haT])
        b2 = BR.two()
        for nh in range(2):
            C.mm(C.ps[:nr, b2 + nh, :], haT[:, :nr], a2[:, nh * 512:(nh + 1) * 512], True, True, [haT, a2], [C.pb[b2 + nh]])
            C.tt("dve", a_[:nr, nh * 512:(nh + 1) * 512], C.ps[:nr, b2 + nh, :], bc["rw_a0"][:nr, nh * 512:(nh + 1) * 512], ALU.add, [C.pb[b2 + nh], bc["rw_a0"]], [a_])
        C.actf(a_[:nr, :], a_[:nr, :], AF.Sigmoid, [a_], [a_])
        if _STOP == 4:
            continue
        b1 = BR.one()
        for kc in range(8):
            C.mm(C.ps[:, b1, :nr], g1[:, kc, :], xg[:, kc, :nr], kc == 0, kc == 7, [g1, xg], [C.pb[b1]])
        C.actf(hgT[:, :nr], C.ps[:, b1, :nr], AF.Sigmoid, [C.pb[b1]], [hgT])
        b2 = BR.two()
        for nh in range(2):
            C.mm(C.ps[:nr, b2 + nh, :], hgT[:, :nr], g2[:, nh * 512:(nh + 1) * 512], True, True, [hgT, g2], [C.pb[b2 + nh]])
            C.copy("act", g_[:nr, nh * 512:(nh + 1) * 512], C.ps[:nr, b2 + nh, :], [C.pb[b2 + nh]], [g_])
        C.dma(sc_d["g32"][r0:r0 + nr, :], g_[:nr, :], [g_], [sc_d["g32"]])
        C.dma(sc_d["v32"][r0:r0 + nr, :], v_[:nr, :], [v_], [sc_d["v32"]])
        if _STOP == 5:
            continue
        C.tt("pool", kk[:nr, :], k_[:nr, :], bc["rw_k_k"][:nr, :], ALU.mult, [k_, bc["rw_k_k"]], [kk])
        C.tt("dve", t1[:nr, :], kk[:nr, :], kk[:nr, :], ALU.mult, [kk], [t1])
        C.red(ss[:nr, :], v3(t1, nr), ALU.add, [t1], [ss])
        C.actf(ss[:nr, :], ss[:nr, :], AF.Sqrt, [ss], [ss])
        C.ts("dve", ss[:nr, :], ss[:nr, :], 1e-12, None, ALU.max, None, [ss], [ss])
        C.recip(ss[:nr, :], ss[:nr, :], [ss], [ss])
        C.tt("dve", v3(kk, nr), v3(kk, nr), ss[:nr, :].unsqueeze(2).to_broadcast([nr, 16, 64]), ALU.mult, [kk, ss], [kk])
        if _STOP == 6:
            continue
        C.ts("dve", t1[:nr, :], a_[:nr, :], -1.0, None, ALU.add, None, [a_], [t1])
        C.tt("pool", t1[:nr, :], t1[:nr, :], bc["rw_k_a"][:nr, :], ALU.mult, [t1, bc["rw_k_a"]], [t1])
        C.tt("dve", t1[:nr, :], t1[:nr, :], k_[:nr, :], ALU.mult, [t1, k_], [t1])
        C.tt("dve", k_[:nr, :], k_[:nr, :], t1[:nr, :], ALU.add, [t1, k_], [k_])
        C.tt("pool", t1[:nr, :], r_[:nr, :], bc["rw_r_kf"][:nr, :], ALU.mult, [r_, bc["rw_r_kf"]], [t1])
        C.tt("dve", t1[:nr, :], t1[:nr, :], k_[:nr, :], ALU.mult, [t1, k_], [t1])
        C.red(bon[:nr, :], v3(t1, nr), ALU.add, [t1], [bon])
        C.dma(sc_d["bon"][r0:r0 + nr, :], bon[:nr, :], [bon], [sc_d["bon"]])
        C.tt("dve", a_[:nr, :], a_[:nr, :], kk[:nr, :], ALU.mult, [a_, kk], [a_])
        if _STOP == 7:
            continue
        if smp:
            C.actf(lw[:nr, :], lw[:nr, :], AF.Exp, [lw], [lw])
            C.ts("dve", kk[:nr, :], kk[:nr, :], -1.0, None, ALU.mult, None, [kk], [kk])
            for qi, src in enumerate((r_, lw, k_, v_, kk, a_)):
                dst = sc_d["smp"].t[qi].rearrange("h b t k -> (b t) h k")
                C.dma(dst, v3(src, nr), [src], [sc_d["smp"]])
            continue
        b2 = BR.two()
        for nh in range(2):
            C.mm(C.ps[:nr, b2 + nh, :], Lt[:nr, :nr], lw[:nr, nh * 512:(nh + 1) * 512], True, True, [Lt, lw], [C.pb[b2 + nh]])
        if _STOP == 71:
            continue
        for nh in range(2):
            sl = slice(nh * 512, (nh + 1) * 512)
            pc = C.ps[:nr, b2 + nh, :]
            pbs = [C.pb[b2 + nh]]
            _SUB = os.environ.get("RWA_SUB", "1234")
            if "1" in _SUB:
                C.actf(t1[:nr, sl], pc, AF.Exp, pbs, [t1])
            if "2" in _SUB:
                C.actf(t2[:nr, sl], pc, AF.Exp, pbs, [t2], scale=-1.0)
            if "3" in _SUB:
                C.tt("dve", t3[:nr, sl], pc, lw[:nr, sl], ALU.subtract, pbs + [lw], [t3])
        if "4" in _SUB:
            C.actf(t3[:nr, :], t3[:nr, :], AF.Exp, [t3], [t3])
        if _STOP == 72:
            continue
        C.tt("dve", ob[0][:nr, :], r_[:nr, :], t1[:nr, :], ALU.mult, [r_, t1], [ob[0]])
        C.tt("pool", ob[1][:nr, :], k_[:nr, :], t2[:nr, :], ALU.mult, [k_, t2], [ob[1]])
        C.stt(ob[2][:nr, :], kk[:nr, :], -1.0, t3[:nr, :], ALU.mult, ALU.mult, [kk, t3], [ob[2]])
        C.tt("pool", ob[3][:nr, :], a_[:nr, :], t2[:nr, :], ALU.mult, [a_, t2], [ob[3]])
        C.copy("act", ob[4][:nr, :], v_[:nr, :], [v_], [ob[4]])
        if _STOP == 73:
            continue
        for i, nm in enumerate(("Rt", "Kt", "At", "Bt", "Vb")):
            C.dma(sc_d[nm][r0:r0 + nr, :], ob[i][:nr, :], [ob[i]], [sc_d[nm]])
        if _STOP == 8:
            continue
        b1 = BR.one()
        for hp in range(8):
            C.mm(C.ps[:, b1, 2 * hp:2 * hp + 2], lw[:nr, hp * 128:(hp + 1) * 128], onec[:nr, :], True, True, [lw, onec], [C.pb[b1]])
        C.actf(wc[:, :], C.ps[:, b1, 0:16].rearrange("p (h a) -> p h a", a=2)[:, :, 0], AF.Exp, [C.pb[b1]], [wc])
        C.dma(sc_d["wc"][blk], wc[:, :], [wc], [sc_d["wc"]])
    C.stage_end()


def rw_stage_b(C, sc_d, y_d, wkvp_out, nchunks=16):
    C.stage_begin()
    identf, identb, iotaf = load_consts(C)
    BR = BankRot()
    M2 = C.sb("M2", [128, 2, 128])
    C.dma(M2[:, 0, :], C.cst["lstrict"], [], [M2]); C.dma(M2[:, 1, :], C.cst["lincl"], [], [M2])
    mlow = C.sb("mlow", [128, 128]); C.dma(mlow[:], C.cst["llow"], [], [mlow])
    pm = C.sb("pm", [128, 2]); C.dma(pm[:], C.cst["pm"], [], [pm])
    tok = [[C.sb(f"tok{s}_{i}", [128, D], BF16) for i in range(5)] for s in range(2)]
    KtT = [C.sb(f"KtT{s}", [128, 8, 128], BF16) for s in range(2)]
    BtT = [C.sb(f"BtT{s}", [128, 8, 128], BF16) for s in range(2)]
    ARm = [[C.sb(f"ARm{s}_{h2}", [128, 8, 2, 128], BF16) for h2 in range(2)] for s in range(2)]
    NA = [C.sb(f"NA{s}", [128, 16, 2, 2, 128], BF16) for s in range(2)]
    AA = C.sb("AA", [128, 16, 128], BF16)
    Np = [C.sb(f"Np{i}", [128, 16, 128], BF16) for i in range(2)]
    Ap = [C.sb(f"Ap{i}", [128, 16, 128], BF16) for i in range(2)]
    T32 = C.sb("T32", [128, 16, 128]); Tb = [C.sb(f"Tb{s}", [128, 16, 128], BF16) for s in range(2)]
    Xb = C.sb("Xb", [128, 16, 64], BF16); Ub = C.sb("Ub", [128, 16, 64], BF16)
    P32 = C.sb("P32", [128, 8, 64]); Pb = [C.sb(f"Pb{i}", [128, 8, 64], BF16) for i in range(2)]
    Yo = [C.sb(f"Yo{i}", [128, D]) for i in range(2)]
    wcs = [C.sb(f"wcs{i}", [128, 8]) for i in range(2)]
    C.memset("dve", P32[:], 0.0, [P32]); C.memset("dve", Pb[0][:], 0.0, [Pb[0]])

    def hv(bank2, h):
        return C.ps[:, bank2 + h // 8, (h % 8) * 64:(h % 8 + 1) * 64], C.pb[bank2 + h // 8]

    for c in range(nchunks):
        s = c % 2
        r0 = c * 128
        Rtk, Ktk, Atk, Btk, Vtk = tok[s]
        for i, nm in enumerate(("Rt", "Kt", "At", "Bt", "Vb")):
            C.dma(tok[s][i][:], sc_d[nm][r0:r0 + 128, :], [sc_d[nm]], [tok[s][i]])
        C.dma(wcs[s][:], sc_d["wc"][c], [sc_d["wc"]], [wcs[s]])
        for src, kind in ((Ktk, "K"), (Btk, "B"), (Atk, 0), (Rtk, 1)):
            bank = BR.one()
            pst = C.ps[:, bank, :].bitcast(BF16)
            for hp in range(8):
                C.tr(pst[:, hp * 128:(hp + 1) * 128], src[:, hp * 128:(hp + 1) * 128], identb[:], [src, identb], [C.pb[bank]])
            pv = pst.rearrange("p (a t) -> p a t", a=8)
            if kind == "K":
                C.copy("act", KtT[s][:], pv, [C.pb[bank]], [KtT[s]])
            elif kind == "B":
                C.copy("act", BtT[s][:], pv, [C.pb[bank]], [BtT[s]])
            else:
                for h2 in range(2):
                    C.ts("dve", ARm[s][h2][:, :, kind, :], pv, pm[:, h2:h2 + 1], None, ALU.mult, None, [C.pb[bank], pm], [ARm[s][h2]])
        for h in range(16):
            hp, h2 = h // 2, h % 2
            bank = BR.one()
            rhs2 = ARm[s][h2][:, hp, :, :].rearrange("p a t -> p (a t)")
            C.mm(C.ps[:, bank, 0:256], BtT[s][:, hp, :], rhs2, True, True, [BtT[s], ARm[s][h2]], [C.pb[bank]])
            C.mm(C.ps[:, bank, 256:512], KtT[s][:, hp, :], rhs2, True, True, [KtT[s], ARm[s][h2]], [C.pb[bank]])
            C.tt("dve", NA[s][:, h], C.ps[:, bank, :].rearrange("p (q a t) -> p q a t", q=2, a=2),
                 M2[:].unsqueeze(1).to_broadcast([128, 2, 2, 128]), ALU.mult, [C.pb[bank], M2], [NA[s]])
        for g in range(4):
            bank = BR.one()
            for j in range(4):
                h = 4 * g + j
                hp, h2 = h // 2, h % 2
                C.mm(C.ps[:, bank, j * 128:(j + 1) * 128], ARm[s][h2][:, hp, 0, :], BtT[s][:, hp, :], True, True, [ARm[s][h2], BtT[s]], [C.pb[bank]])
            C.tt("dve", AA[:, 4 * g:4 * g + 4, :], C.ps[:, bank, :].rearrange("p (j t) -> p j t", j=4),
                 mlow[:].unsqueeze(1).to_broadcast([128, 4, 128]), ALU.mult, [C.pb[bank], mlow], [AA])
        N1 = NA[s][:, :, 0, 0, :]
        C.tt("dve", T32[:], N1, identf[:].unsqueeze(1).to_broadcast([128, 16, 128]), ALU.add, [NA[s], identf], [T32])
        C.copy("act", Tb[s][:], T32[:], [T32], [Tb[s]])
        Ncur, Nt, Acur, At_ = N1, NA[s], AA[:], AA
        for i in range(1, 7):
            last = i == 6
            Nn, An = Np[i % 2], Ap[i % 2]
            for g in range(4):
                bankA = BR.one()
                bankN = None if last else BR.one()
                for j in range(4):
                    h = 4 * g + j
                    if not last:
                        C.mm(C.ps[:, bankN, j * 128:(j + 1) * 128], Acur[:, h, :], Ncur[:, h, :], True, True, [At_, Nt], [C.pb[bankN]])
                    C.mm(C.ps[:, bankA, j * 128:(j + 1) * 128], Ncur[:, h, :], Acur[:, h, :], True, True, [At_, Nt], [C.pb[bankA]])
                if not last:
                    C.copy("act", Nn[:, 4 * g:4 * g + 4, :], C.ps[:, bankN, :].rearrange("p (j t) -> p j t", j=4), [C.pb[bankN]], [Nn])
                C.copy("dve", An[:, 4 * g:4 * g + 4, :], C.ps[:, bankA, :].rearrange("p (j t) -> p j t", j=4), [C.pb[bankA]], [An])
            for g in range(4):
                bankT = BR.one()
                for j in range(4):
                    h = 4 * g + j
                    C.mm(C.ps[:, bankT, j * 128:(j + 1) * 128], An[:, h, :], Tb[s][:, h, :], True, True, [An, Tb[s]], [C.pb[bankT]])
                C.tt("dve", T32[:, 4 * g:4 * g + 4, :], T32[:, 4 * g:4 * g + 4, :], C.ps[:, bankT, :].rearrange("p (j t) -> p j t", j=4), ALU.add, [T32, C.pb[bankT]], [T32])
            C.copy("act", Tb[s][:], T32[:], [T32], [Tb[s]])
            Ncur, Nt, Acur, At_ = Nn[:], Nn, An[:], An
        cur, nxt = Pb[c % 2], Pb[(c + 1) % 2]
        b2 = BR.two()
        for h in range(16):
            hp, h2 = h // 2, h % 2
            o, ob_ = hv(b2, h)
            C.mm(o, ARm[s][h2][:, hp, 0, :], cur[:, hp, :], True, False, [ARm[s][h2], cur], [ob_])
            C.mm(o, NA[s][:, h, 1, 0, :], Vtk[:, h * 64:(h + 1) * 64], False, True, [NA[s], Vtk], [ob_])
        for a in range(2):
            C.copy("act" if a else "dve", Xb[:, 8 * a:8 * a + 8, :], C.ps[:, b2 + a, :].rearrange("p (h v) -> p h v", h=8), [C.pb[b2 + a]], [Xb])
        b2 = BR.two()
        for h in range(16):
            o, ob_ = hv(b2, h)
            C.mm(o, Tb[s][:, h, :], Xb[:, h, :], True, True, [Tb[s], Xb], [ob_])
        for a in range(2):
            C.copy("act" if a else "dve", Ub[:, 8 * a:8 * a + 8, :], C.ps[:, b2 + a, :].rearrange("p (h v) -> p h v", h=8), [C.pb[b2 + a]], [Ub])
        b2 = BR.two()
        for h in range(16):
            hp, h2 = h // 2, h % 2
            o, ob_ = hv(b2, h)
            C.mm(o, ARm[s][h2][:, hp, 1, :], cur[:, hp, :], True, False, [ARm[s][h2], cur], [ob_])
            C.mm(o, NA[s][:, h, 0, 1, :], Ub[:, h, :], False, False, [NA[s], Ub], [ob_])
            C.mm(o, NA[s][:, h, 1, 1, :], Vtk[:, h * 64:(h + 1) * 64], False, True, [NA[s], Vtk], [ob_])
        YO = Yo[c % 2]
        for a in range(2):
            C.copy("act", YO[:, 512 * a:512 * a + 512], C.ps[:, b2 + a, :], [C.pb[b2 + a]], [YO])
        C.dma(y_d[r0:r0 + 128, :], YO[:], [YO], [y_d])
        b2 = BR.two()
        for hp in range(8):
            o = C.ps[:, b2 + hp // 4, (hp % 4) * 128:(hp % 4 + 1) * 128]
            ob_ = C.pb[b2 + hp // 4]
            C.mm(o, Btk[:, hp * 128:(hp + 1) * 128], Ub[:, 2 * hp:2 * hp + 2, :].rearrange("p a v -> p (a v)"), True, False, [Btk, Ub], [ob_])
            C.mm(o, Ktk[:, hp * 128:(hp + 1) * 128], Vtk[:, hp * 128:(hp + 1) * 128], False, True, [Ktk, Vtk], [ob_])
        for h2 in range(2):
            p0 = 64 * h2
            psv = C.ps[p0:p0 + 64, b2:b2 + 2, :].rearrange("p a (q h v) -> p (a q) h v", q=4, h=2)[:, :, h2, :]
            C.tt("dve", P32[p0:p0 + 64, :, :], P32[p0:p0 + 64, :, :], psv, ALU.add, [P32, C.pb[b2], C.pb[b2 + 1]], [P32])
        C.tt("dve", P32[:], P32[:], wcs[s][:, :].unsqueeze(2).to_broadcast([128, 8, 64]), ALU.mult, [P32, wcs[s]], [P32])
        C.copy("act", nxt[:], P32[:], [P32], [nxt])
    b2 = BR.two()
    for hp in range(8):
        C.tr(C.ps[:64, b2 + hp // 4, (hp % 4) * 128:(hp % 4 + 1) * 128], P32[:, hp, :], identf[:], [P32, identf], [C.pb[b2 + hp // 4]])
    so = C.sb("so", [64, D])
    for a in range(2):
        C.copy("act", so[:, 512 * a:512 * a + 512], C.ps[:64, b2 + a, :], [C.pb[b2 + a]], [so])
    C.dma(wkvp_out.t.rearrange("h v k -> v h k"), so[:, :].rearrange("p (h k) -> p h k", h=16), [so], [wkvp_out])
    C.stage_end()


def rw_stage_s(C, sc_d, wkv_in, y_d, wkvs_out):
    C.stage_begin()
    S = [C.sb(f"S{g}", [128, 64, 64]) for g in range(2)]
    q = [[C.sb(f"q{g}_{i}", [128, 4, 64]) for i in range(6)] for g in range(2)]
    tmp = C.sb("tmp", [128, 64, 64]); sa = C.sb("sa", [128, 64]); yy = [C.sb(f"yy{g}", [128, 4, 64]) for g in range(2)]
    for g in range(2):
        for hl in range(8):
            h = 8 * g + hl
            C.dma(S[g][hl * 16:(hl + 1) * 16, :, :], wkv_in[:, h, :, :], [], [S[g]])
        for i in range(6):
            C.dma(q[g][i][:], sc_d["smp"].t[i, 8 * g:8 * g + 8].rearrange("h b t k -> (h b) t k"), [sc_d["smp"]], [q[g][i]])
    for g in range(2):
        r_, w_, k_, v_, al, be = q[g]
        St = S[g]
        for t in range(4):
            def bk(x):
                return x[:, t, :].unsqueeze(1).to_broadcast([128, 64, 64])

            def bv(x):
                return x.unsqueeze(2).to_broadcast([128, 64, 64])
            C.tt("dve", tmp[:], St[:], bk(al), ALU.mult, [St, al], [tmp])
            C.red(sa[:], tmp[:], ALU.add, [tmp], [sa])
            C.tt("dve", St[:], St[:], bk(w_), ALU.mult, [St, w_], [St])
            C.tt("dve", tmp[:], bv(sa[:, :]), bk(be), ALU.mult, [sa, be], [tmp])
            C.tt("dve", St[:], St[:], tmp[:], ALU.add, [St, tmp], [St])
            C.tt("dve", tmp[:], bv(v_[:, t, :]), bk(k_), ALU.mult, [v_, k_], [tmp])
            C.tt("dve", St[:], St[:], tmp[:], ALU.add, [St, tmp], [St])
            C.tt("dve", tmp[:], St[:], bk(r_), ALU.mult, [St, r_], [tmp])
            C.red(yy[g][:, t, :], tmp[:], ALU.add, [tmp], [yy[g]])
        for hl in range(8):
            h = 8 * g + hl
            C.dma(wkvs_out[:, h, :, :], St[hl * 16:(hl + 1) * 16, :, :], [St], [wkvs_out])
            dst = y_d.t[NP_:NT, h * 64:(h + 1) * 64].rearrange("(b t) v -> b t v", t=4)
            C.dma(dst, yy[g][hl * 16:(hl + 1) * 16, :, :], [yy[g]], [y_d])
    C.stage_end()


def rw_stage_c(C, W, sc_d, y_d, h_out):
    C.stage_begin()
    identf, identb, iotaf = load_consts(C)
    BR = BankRot()
    wo = C.sb("wo", [128, 8, D], BF16)
    src = W["rw_w_o"][0].rearrange("(k p) n -> p k n", p=128)
    for i in range(2):
        C.dma(wo[:, :, i * 512:(i + 1) * 512], src[:, :, i * 512:(i + 1) * 512], [], [wo], q="pool")
    gg = C.sb("gng", [128, D]); gb = C.sb("gnb", [128, D])
    C.dma(gg[:], W["rw_gn_g"][0:1, :].partition_broadcast(128), [], [gg])
    C.dma(gb[:], W["rw_gn_b"][0:1, :].partition_broadcast(128), [], [gb])
    eps = C.sb("epsg", [128, 1]); C.memset("dve", eps[:], 64e-5, [eps])
    NB = 2
    y = [C.sb(f"y{i}", [128, D]) for i in range(NB)]; v = [C.sb(f"v{i}", [128, D]) for i in range(NB)]; g = [C.sb(f"g{i}", [128, D]) for i in range(NB)]
    bon = [C.sb(f"bon{i}", [128, 16]) for i in range(NB)]
    sq = C.sb("sq", [128, D]); mu = C.sb("mu", [128, 16]); var = C.sb("var", [128, 16])
    zb = [C.sb(f"zb{i}", [128, D], BF16) for i in range(NB)]; zT = [C.sb(f"zT{i}", [128, 8, 128], BF16) for i in range(NB)]
    ho = [C.sb(f"ho{i}", [128, D]) for i in range(NB)]

    def v3(t, nr):
        return t[:nr, :].rearrange("p (h k) -> p h k", h=16)

    def b16(t, nr):
        return t[:nr, :].unsqueeze(2).to_broadcast([nr, 16, 64])

    for blk in range(NBLK):
        r0, nr = blk_rows(blk)
        s = blk % NB
        Y, V, G, BON, ZB, ZT, HO = y[s], v[s], g[s], bon[s], zb[s], zT[s], ho[s]
        C.dma(Y[:nr, :], y_d[r0:r0 + nr, :], [y_d], [Y])
        C.dma(V[:nr, :], sc_d["v32"][r0:r0 + nr, :], [sc_d["v32"]], [V])
        C.dma(G[:nr, :], sc_d["g32"][r0:r0 + nr, :], [sc_d["g32"]], [G])
        C.dma(BON[:nr, :], sc_d["bon"][r0:r0 + nr, :], [sc_d["bon"]], [BON])
        C.red(mu[:nr, :], v3(Y, nr), ALU.add, [Y], [mu])
        C.ts("dve", mu[:nr, :], mu[:nr, :], 1.0 / 64, None, ALU.mult, None, [mu], [mu])
        C.tt("dve", v3(Y, nr), v3(Y, nr), b16(mu, nr), ALU.subtract, [Y, mu], [Y])
        C.tt("pool", sq[:nr, :], Y[:nr, :], Y[:nr, :], ALU.mult, [Y], [sq])
        C.red(var[:nr, :], v3(sq, nr), ALU.add, [sq], [var])
        C.actf(var[:nr, :], var[:nr, :], AF.Sqrt, [var], [var], bias=eps[:nr, 0:1], scale=1.0 / 64)
        C.recip(var[:nr, :], var[:nr, :], [var], [var])
        C.tt("dve", v3(Y, nr), v3(Y, nr), b16(var, nr), ALU.mult, [Y, var], [Y])
        C.tt("dve", Y[:nr, :], Y[:nr, :], gg[:nr, :], ALU.mult, [Y, gg], [Y])
        C.tt("pool", Y[:nr, :], Y[:nr, :], gb[:nr, :], ALU.add, [Y, gb], [Y])
        C.tt("dve", v3(V, nr), v3(V, nr), b16(BON, nr), ALU.mult, [V, BON], [V])
        C.tt("pool", Y[:nr, :], Y[:nr, :], V[:nr, :], ALU.add, [Y, V], [Y])
        C.tt("dve", ZB[:nr, :], Y[:nr, :], G[:nr, :], ALU.mult, [Y, G], [ZB])
        to_featmajor(C, nr, ZB, ZT, 0, identb, BR.one())
        b2 = BR.two()
        for nh in range(2):
            for kc in range(8):
                C.mm(C.ps[:nr, b2 + nh, :], ZT[:, kc, :nr], wo[:, kc, nh * 512:(nh + 1) * 512], kc == 0, kc == 7, [ZT, wo], [C.pb[b2 + nh]])
            C.copy("act", HO[:nr, nh * 512:(nh + 1) * 512], C.ps[:nr, b2 + nh, :], [C.pb[b2 + nh]], [HO])
        C.dma(h_out[r0:r0 + nr, :], HO[:nr, :], [HO], [h_out])
    C.stage_end()


def rw_scratch(C, kind="Internal"):
    sc = {}
    for nm in ("Rt", "Kt", "At", "Bt", "Vb"):
        sc[nm] = C.dram("rw_" + nm, [NT, D], BF16, kind=kind)
    sc["v32"] = C.dram("rw_v32", [NT, D], F32, kind=kind)
    sc["g32"] = C.dram("rw_g32", [NT, D], F32, kind=kind)
    sc["bon"] = C.dram("rw_bon", [NT, 16], F32, kind=kind)
    sc["wc"] = C.dram("rw_wc", [16, 128, 8], F32, kind=kind)
    sc["smp"] = C.dram("rw_smp", [6, 16, 16, 4, 64], F32, kind=kind)
    return sc


LAM_INIT = 0.8 - 0.6 * float(np.exp(-0.3 * 1))


def kvq_stage(C, W, x1T_d, k_out, v_out, kT_d, qT_d, vb_d):
    C.stage_begin()
    BR = BankRot()
    wts = {}
    for nm, src in (("wk", W["da_w_k"]), ("wv", W["da_w_v"]), ("wq", W["da_w_q"][0])):
        t = C.sb(nm, [128, 8, D], BF16)
        s_ = src.rearrange("(k p) n -> p k n", p=128)
        for i in range(2):
            C.dma(t[:, :, i * 512:(i + 1) * 512], s_[:, :, i * 512:(i + 1) * 512], [], [t], q="pool")
        wts[nm] = t
    NB = 2
    xT = [C.sb(f"xT{i}", [128, 8, 128], BF16) for i in range(NB)]
    ko = [C.sb(f"ko{i}", [128, D]) for i in range(NB)]; vo = [C.sb(f"vo{i}", [128, D]) for i in range(NB)]
    vb = [C.sb(f"vb{i}", [128, D], BF16) for i in range(NB)]
    kT = [C.sb(f"kT{i}", [128, 8, 128], BF16) for i in range(NB)]; qT = [C.sb(f"qT{i}", [128, 8, 128], BF16) for i in range(NB)]
    for blk in range(NBLK):
        r0, nr = blk_rows(blk)
        s = blk % NB
        XT = xT[s]
        C.dma(XT[:, :, :nr], x1T_d[:, :, r0:r0 + nr], [x1T_d], [XT])
        for wn, dst, outd in (("wk", ko[s], k_out), ("wv", vo[s], v_out)):
            b2 = BR.two()
            for nh in range(2):
                for kc in range(8):
                    C.mm(C.ps[:nr, b2 + nh, :], XT[:, kc, :nr], wts[wn][:, kc, nh * 512:(nh + 1) * 512], kc == 0, kc == 7, [XT, wts[wn]], [C.pb[b2 + nh]])
                C.copy("act" if nh else "dve", dst[:nr, nh * 512:(nh + 1) * 512], C.ps[:nr, b2 + nh, :], [C.pb[b2 + nh]], [dst])
            C.dma(outd[r0:r0 + nr, :], dst[:nr, :], [dst], [outd])
        C.copy("act", vb[s][:nr, :], vo[s][:nr, :], [vo[s]], [vb[s]])
        C.dma(vb_d[r0:r0 + nr, :], vb[s][:nr, :], [vb[s]], [vb_d])
        for wn, dst, outd, scl in (("wk", kT[s], kT_d, 1.0), ("wq", qT[s], qT_d, 0.125)):
            for hg in range(2):
                bank = BR.one()
                for j in range(4):
                    h = 4 * hg + j
                    for kc in range(8):
                        C.mm(C.ps[:, bank, j * 128:j * 128 + nr], wts[wn][:, kc, h * 128:(h + 1) * 128], XT[:, kc, :nr], kc == 0, kc == 7, [XT, wts[wn]], [C.pb[bank]])
                C.actf(dst[:, 4 * hg:4 * hg + 4, :nr], C.ps[:, bank, :].rearrange("p (j t) -> p j t", j=4)[:, :, :nr], AF.Identity, [C.pb[bank]], [dst], scale=scl)
            C.dma(outd[:, :, r0:r0 + nr], dst[:, :, :nr], [dst], [outd])
    C.stage_end()


def lam_compute(C, W):
    lv = C.sb("lv", [128, 256]); pr = C.sb("lpr", [128, 2, 64]); s2 = C.sb("ls2", [128, 2]); lam = C.sb("lam", [128, 1])
    C.dma(lv[:], W["da_lam"][0].rearrange("a b -> (a b)").partition_broadcast(128), [], [lv])
    l4 = lv[:, :].rearrange("p (a b) -> p a b", a=4)
    C.tt("dve", pr[:, 0, :], l4[:, 0, :], l4[:, 1, :], ALU.mult, [lv], [pr])
    C.tt("dve", pr[:, 1, :], l4[:, 2, :], l4[:, 3, :], ALU.mult, [lv], [pr])
    C.red(s2[:, :], pr[:, :, :], ALU.add, [pr], [s2])
    C.actf(s2[:, :], s2[:, :], AF.Exp, [s2], [s2])
    C.tt("dve", lam[:, :], s2[:, 0:1], s2[:, 1:2], ALU.subtract, [s2], [lam])
    C.ts("dve", lam[:, :], lam[:, :], LAM_INIT, None, ALU.add, None, [lam], [lam])
    return lam


def attn_finish(C, W, nr, O, scr, identb, wo, sg, h_out, r0, BR):
    sq, ss, ob, oT, ho, eps = scr["sq"], scr["ss"], scr["ob"], scr["oT"], scr["ho"], scr["eps"]
    o3 = O[:nr, :].rearrange("p (h v) -> p h v", h=8)
    C.tt("pool", sq[:nr, :], O[:nr, :], O[:nr, :], ALU.mult, [O], [sq])
    C.red(ss[:nr, :], sq[:nr, :].rearrange("p (h v) -> p h v", h=8), ALU.add, [sq], [ss])
    C.actf(ss[:nr, :], ss[:nr, :], AF.Sqrt, [ss], [ss], bias=eps[:nr, 0:1], scale=1.0 / 128)
    C.recip(ss[:nr, :], ss[:nr, :], [ss], [ss])
    C.tt("dve", o3, o3, ss[:nr, :].unsqueeze(2).to_broadcast([nr, 8, 128]), ALU.mult, [O, ss], [O])
    C.tt("dve", ob[:nr, :].rearrange("p (h v) -> p h v", h=8), o3, sg[:nr, :].unsqueeze(1).to_broadcast([nr, 8, 128]), ALU.mult, [O, sg], [ob])
    to_featmajor(C, nr, ob, oT, 0, identb, BR.one())
    bks = BR.pair()
    for nh in range(2):
        bk = bks[nh]
        for h in range(8):
            C.mm(C.ps[:nr, bk, :], oT[:, h, :nr], wo[:, h, nh * 512:(nh + 1) * 512], h == 0, h == 7, [oT, wo], [C.pb[bk]])
        C.copy("act", ho[:nr, nh * 512:(nh + 1) * 512], C.ps[:nr, bk, :], [C.pb[bk]], [ho])
    C.dma(h_out[r0:r0 + nr, :], ho[:nr, :], [ho], [h_out])


def attn_common(C, W):
    identf, identb, iotaf = load_consts(C)
    wo = C.sb("wo_a", [128, 8, D], BF16)
    src = W["da_w_o"][0].rearrange("(h p) n -> p h n", p=128)
    for i in range(2):
        C.dma(wo[:, :, i * 512:(i + 1) * 512], src[:, :, i * 512:(i + 1) * 512], [], [wo], q="pool")
    sg = C.sb("sg_a", [128, 128])
    C.dma(sg[:], W["da_subln_g"][0:1, :].partition_broadcast(128), [], [sg])
    C.ts("dve", sg[:], sg[:], 1.0 - LAM_INIT, None, ALU.mult, None, [sg], [sg])
    eps = C.sb("eps_a", [128, 1]); C.memset("dve", eps[:], 1e-5, [eps])
    scr = {"sq": C.sb("sq_a", [128, D]), "ss": C.sb("ss_a", [128, 8]), "ob": C.sb("ob_a", [128, D], BF16),
           "oT": C.sb("oT_a", [128, 8, 128], BF16), "ho": C.sb("ho_a", [128, D]), "eps": eps}
    lam = lam_compute(C, W)
    return identf, identb, wo, sg, scr, lam


def attn_prompt_stage(C, W, kT_d, qT_d, vb_d, h_out, nqb=16):
    C.stage_begin()
    identf, identb, wo, sg, scr, lam = attn_common(C, W)
    BR = BankRot()
    kT = C.sb("kTa", [128, 8, NP_], BF16); qT = C.sb("qTa", [128, 8, NP_], BF16)
    for i in range(4):
        C.dma(kT[:, :, i * 512:(i + 1) * 512], kT_d[:, :, i * 512:(i + 1) * 512], [kT_d], [kT])
        C.dma(qT[:, :, i * 512:(i + 1) * 512], qT_d[:, :, i * 512:(i + 1) * 512], [qT_d], [qT])
    vA = C.sb("vA", [128, 16, 8, 129], BF16)
    C.memset("pool", vA[:, :, :, 128:129], 1.0, [vA])
    for kb in range(16):
        C.dma(vA[:, kb, :, 0:128], vb_d[kb * 128:(kb + 1) * 128, :].rearrange("p (h v) -> p h v", h=8), [vb_d], [vA])
    btab = C.sb("btab", [128, 8, 16]); C.dma(btab[:], C.cst["btab"], [], [btab])
    msk = C.sb("msk", [128, 128], BF16); mskf = C.sb("mskf", [128, 128])
    C.dma(mskf[:], C.cst["lincl"], [], [mskf]); C.copy("dve", msk[:], mskf[:], [mskf], [msk])
    PT = [C.sb(f"PT{i}", [128, 2, 128], BF16) for i in range(3)]
    pm = C.sb("pm_a", [128, 2]); C.dma(pm[:], C.cst["pm"], [], [pm])
    qz = [C.sb(f"qz{i}", [128, 8, 2, 128], BF16) for i in range(2)]
    O = [C.sb(f"O{i}", [128, D]) for i in range(2)]
    rr = C.sb("rr", [128, 2]); tmp = C.sb("tmpo", [128, 128])
    items = [(qb, h, kb) for qb in range(nqb) for h in range(8) for kb in range(qb + 1)]
    accs = {}
    for n_, (qb, h) in enumerate((qb, h) for qb in range(nqb) for h in range(8)):
        accs[(qb, h)] = (0, 1) if n_ % 2 == 0 else (2, 3)

    def emit_score(i):
        qb, h, kb = items[i]
        QZ = qz[qb % 2]
        if h == 0 and kb == 0:
            for c in range(2):
                C.ts("dve" if c else "pool", QZ[:, :, c, :], qT[:, :, qb * 128:(qb + 1) * 128], pm[:, c:c + 1], None, ALU.mult, None, [qT, pm], [QZ])
        bs = 4 + (i % 3)
        C.mm(C.ps[:, bs, 0:256], kT[:, h, kb * 128:(kb + 1) * 128], QZ[:, h, :, :].rearrange("p c q -> p (c q)"), True, True, [kT, QZ], [C.pb[bs]])

    def emit_rest(i):
        qb, h, kb = items[i]
        accb = accs[(qb, h)]
        OO = O[qb % 2]
        bs = 4 + (i % 3)
        P_ = PT[i % 3]
        C.actf(P_[:, :, :], C.ps[:, bs, 0:256].rearrange("p (c q) -> p c q", c=2), AF.Exp, [C.pb[bs], btab], [P_], bias=btab[:, h, qb - kb:qb - kb + 1])
        if kb == qb:
            C.tt("dve", P_[:, :, :], P_[:, :, :], msk[:].unsqueeze(1).to_broadcast([128, 2, 128]), ALU.mult, [P_, msk], [P_])
        for c in range(2):
            C.mm(C.ps[:, accb[c], 0:129], P_[:, c, :], vA[:, kb, h, :], kb == 0, kb == qb, [P_, vA], [C.pb[accb[c]]])
        if kb == qb:
            for c in range(2):
                C.recip(rr[:, c:c + 1], C.ps[:, accb[c], 128:129], [C.pb[accb[c]]], [rr])
            C.tt("dve", rr[:, 1:2], rr[:, 1:2], lam[:, :], ALU.mult, [rr, lam], [rr])
            C.ts("dve", tmp[:], C.ps[:, accb[1], 0:128], rr[:, 1:2], None, ALU.mult, None, [C.pb[accb[1]], rr], [tmp])
            C.stt(OO[:, h * 128:(h + 1) * 128], C.ps[:, accb[0], 0:128], rr[:, 0:1], tmp[:], ALU.mult, ALU.subtract, [C.pb[accb[0]], rr, tmp], [OO])

    pending = []
    emit_score(0)
    for i in range(len(items)):
        if i + 1 < len(items):
            emit_score(i + 1)
        emit_rest(i)
        qb, h, kb = items[i]
        if h == 7 and kb == qb:
            pending.append((i + 5, qb))
        if pending and pending[0][0] <= i:
            _, qf = pending.pop(0)
            attn_finish(C, W, 128, O[qf % 2], scr, identb, wo, sg, h_out, qf * 128, BR2(BR))
    for _, qf in pending:
        attn_finish(C, W, 128, O[qf % 2], scr, identb, wo, sg, h_out, qf * 128, BR2(BR))
    C.stage_end()


class BR2:
    def __init__(self, br=None):
        pass

    def one(self):
        return 7

    def pair(self):
        return (7, 7)


def attn_sample_stage(C, W, pt_ap, ckT_ap, cv_ap, kT_d, qT_d, vb_d, h_out, nseq=16):
    C.stage_begin()
    identf, identb, wo, sg, scr, lam = attn_common(C, W)
    pti = C.sb("pti", [128, 256], I32); ptf = C.sb("ptf", [128, 256]); idxi = C.sb("idxi", [128, 256], I32)
    iop = C.sb("iop", [128, 1]); C.dma(iop[:], C.cst["iotap"], [], [iop])
    C.dma(pti[:], pt_ap.rearrange("a b -> (a b)").partition_broadcast(128), [], [pti])
    C.copy("dve", ptf[:], pti[:], [pti], [ptf])
    C.ts("dve", ptf[:], ptf[:], 128.0, iop[:, 0:1], ALU.mult, ALU.add, [ptf, iop], [ptf])
    C.copy("dve", idxi[:], ptf[:], [ptf], [idxi])
    qTs = C.sb("qTs", [128, 8, NS], BF16); kTs = C.sb("kTs", [128, 8, NS], BF16)
    C.dma(qTs[:], qT_d[:, :, NP_:NT], [qT_d], [qTs]); C.dma(kTs[:], kT_d[:, :, NP_:NT], [kT_d], [kTs])
    qblk = C.sb("qblk", [128, 16, 8, 8], BF16)
    C.memset("pool", qblk[:], 0.0, [qblk])
    for c in range(2):
        C.copy("dve", qblk[c * 64:(c + 1) * 64, :, :, c * 4:(c + 1) * 4], qTs[c * 64:(c + 1) * 64, :, :].rearrange("p h (b q) -> p b h q", q=4), [qTs], [qblk])
    tabS = C.sb("tabS", [128, 16, 64]); C.dma(tabS[:], C.cst["tabS"], [], [tabS])
    tabN = C.sb("tabN", [128, 64]); C.dma(tabN[:], C.cst["tabN"], [], [tabN])
    onesf = C.sb("onesf", [64, 128]); C.memset("dve", onesf[:], 1.0, [onesf])
    cmbW = C.sb("cmbW", [8, 124]); cA = C.sb("cA", [8, 124]); cB = C.sb("cB", [8, 124])
    C.dma(cA[:], C.cst["cmbA"], [], [cA]); C.dma(cB[:], C.cst["cmbB"], [], [cB])
    C.stt(cmbW[:], cB[:], lam[:8, 0:1], cA[:], ALU.mult, ALU.add, [cB, cA, lam], [cmbW])
    stg = [C.sb(f"stg{i}", [128, D]) for i in range(4)]
    kpg = [C.sb(f"kpg{i}", [128, 8, 128], BF16) for i in range(4)]
    vres = [C.sb(f"vres{i}", [128, 17, 8, 129], BF16) for i in range(2)]
    for i in range(2):
        C.memset("pool", vres[i][:, 0:16, :, 128:129], 1.0, [vres[i]])
        C.memset("pool", vres[i][:, 16, :, :], 0.0, [vres[i]])
        C.memset("pool", vres[i][0:32, 16, :, 128:129], 1.0, [vres[i]])
    E = [C.sb(f"E{i}", [128, 17, 64]) for i in range(2)]
    PTs = [C.sb(f"PTs{i}", [128, 17, 64], BF16) for i in range(2)]
    m1 = C.sb("m1", [128, 64]); mcol = C.sb("mcol", [64, 1]); dg = C.sb("dg", [64, 64])
    rrs = C.sb("rrs", [8, 8]); on32 = C.sb("on32", [8, D])
    Os = C.sb("Os", [128, D])
    ng = 0
    for b in range(nseq):
        s = b % 2
        VR, EE, PP = vres[s], E[s], PTs[s]
        for pg in range(16):
            col = b * 16 + pg
            for which in range(2):
                st = stg[ng % 4]
                ng += 1
                src = ckT_ap if which == 0 else cv_ap
                C.P.dma("pool", (lambda st, src, col: (lambda e: e.indirect_dma_start(out=st[:], out_offset=None, in_=src,
                        in_offset=bass.IndirectOffsetOnAxis(ap=idxi[:, col:col + 1], axis=0))))(st, src, col), [_b(idxi)], [_b(st)])
                if which == 0:
                    KP = kpg[pg % 4]
                    C.copy("act", KP[:], st[:, :].rearrange("p (h k) -> p h k", h=8), [st], [KP])
                    bank = pg % 2
                    for h in range(8):
                        C.mm(C.ps[:, bank, h * 8:(h + 1) * 8], KP[:, h, :], qblk[:, b, h, :], True, True, [KP, qblk], [C.pb[bank]])
                    C.tt("dve", EE[:, pg, :], C.ps[:, bank, 0:64], tabS[:, pg, :], ALU.add, [C.pb[bank], tabS], [EE])
                else:
                    C.copy("pool", VR[:, pg, :, 0:128], st[:, :].rearrange("p (h v) -> p h v", h=8), [st], [VR])
        C.dma(VR[0:4, 16, :, 0:128], vb_d[NP_ + 4 * b:NP_ + 4 * b + 4, :].rearrange("p (h v) -> p h v", h=8), [vb_d], [VR])
        C.copy("dve", EE[:, 16, :], tabN[:], [tabN], [EE])
        for h in range(8):
            C.mm(C.ps[:4, 2, h * 8:(h + 1) * 8], kTs[:, h, 4 * b:4 * b + 4], qblk[:, b, h, :], True, True, [kTs, qblk], [C.pb[2]])
        C.tt("dve", EE[0:4, 16, :], C.ps[0:4, 2, 0:64], tabN[0:4, :], ALU.add, [C.pb[2], tabN], [EE])
        C.red(m1[:], EE[:].rearrange("p k j -> p j k"), ALU.max, [EE], [m1])
        C.tr(C.ps[:64, 2, 0:128], m1[:], identf[:], [m1, identf], [C.pb[2]])
        C.red(mcol[:], C.ps[:64, 2, 0:128], ALU.max, [C.pb[2]], [mcol])
        C.ts("dve", dg[:], identf[:64, :64], mcol[:, 0:1], None, ALU.mult, None, [identf, mcol], [dg])
        C.mm(C.ps[:, 2, 128:192], onesf[:], dg[:], True, True, [onesf, dg], [C.pb[2]])
        C.tt("dve", EE[:], EE[:], C.ps[:, 2, 128:192].unsqueeze(1).to_broadcast([128, 17, 64]), ALU.subtract, [EE, C.pb[2]], [EE])
        C.actf(PP[:], EE[:], AF.Exp, [EE], [PP])
        for h in range(8):
            bank = 3 + h // 3
            o = C.ps[:8, bank, (h % 3) * 129:(h % 3) * 129 + 129]
            for kb in range(17):
                C.mm(o, PP[:, kb, h * 8:(h + 1) * 8], VR[:, kb, h, :], kb == 0, kb == 16, [PP, VR], [C.pb[bank]])
        for g3 in range(3):
            nh_ = 3 if g3 < 2 else 2
            pv = C.ps[:8, 3 + g3, 0:nh_ * 129].rearrange("p (h v) -> p h v", v=129)
            C.recip(rrs[:, 3 * g3:3 * g3 + nh_], pv[:, :, 128], [C.pb[3 + g3]], [rrs])
            C.tt("dve", on32[:, 3 * g3 * 128:(3 * g3 + nh_) * 128].rearrange("p (h v) -> p h v", v=128), pv[:, :, 0:128],
                 rrs[:, 3 * g3:3 * g3 + nh_].unsqueeze(2).to_broadcast([8, nh_, 128]), ALU.mult, [C.pb[3 + g3], rrs], [on32])
        for nh in range(2):
            C.mm(C.ps[:64, 6 + nh, :], cmbW[:, 60 - 4 * b:124 - 4 * b], on32[:, nh * 512:(nh + 1) * 512], b == 0, b == nseq - 1, [cmbW, on32], [C.pb[6 + nh]])
    for nh in range(2):
        C.copy("act", Os[:64, nh * 512:(nh + 1) * 512], C.ps[:64, 6 + nh, :], [C.pb[6 + nh]], [Os])
    attn_finish(C, W, 64, Os, scr, identb, wo, sg, h_out, NP_, BankRot())
    C.stage_end()


W_SHAPES = None


def build_program(wshapes, cshapes, n_pool_rows, dev=False, scopes=False):
    nc = bass.Bass("TRN2", target_bir_lowering=False)
    es = ExitStack()
    C = Ctx(nc, es)
    C.scopes = scopes

    def ext(name, shape, dt=F32):
        return nc.dram_tensor(name, list(shape), dt, kind="ExternalInput").ap()

    W = {k: ext(k, s) for k, s in wshapes.items()}
    C.cst = {k: ext("cst_" + k, s) for k, s in cshapes.items()}
    x0 = Tile(ext("x0", [NT, D])); x0T = Tile(ext("x0T", [D, NT])); xpT = Tile(ext("xpT", [D, NT]))
    wkv_in = ext("wkv_in", [16, 16, 64, 64])
    pT = [ext("pT0", [256, NT]), ext("pT1", [256, NT])]
    pt = ext("ptab", [16, 16], I32)
    ckT = ext("ckT", [n_pool_rows, D]); cv = ext("cv", [n_pool_rows, D])
    kind = "ExternalOutput" if dev else "Internal"
    out = lambda name, shape: C.dram(name, shape, F32, kind="ExternalOutput")
    y = out("y", [NT, D]); shp = out("shift_p", [1, D]); wkvp = out("wkv_p", [16, 64, 64]); k_o = out("k_o", [NT, D]); v_o = out("v_o", [NT, D])
    shs = out("shift_s", [16, D]); wkvs = out("wkv_s", [16, 16, 64, 64])
    sc = rw_scratch(C)
    y_d = C.dram("y_d", [NT, D], F32, kind=kind); h0 = C.dram("h0_d", [NT, D], F32, kind=kind)
    xa = C.dram("xa_d", [NT, D], F32); xaT = C.dram("xaT_d", [128, 8, NT], BF16); idx = C.dram("idx_d", [128, 3, NT], F32)
    c_d = C.dram("c_d", [NT, D], F32, kind=kind); x1 = C.dram("x1_d", [NT, D], F32, kind=kind); x1T = C.dram("x1T_d", [128, 8, NT], BF16)
    kT = C.dram("kT_d", [128, 8, NT], BF16); qT = C.dram("qT_d", [128, 8, NT], BF16); vb = C.dram("vb_d", [NT, D], BF16)
    h1 = C.dram("h1_d", [NT, D], F32, kind=kind)
    C.stage_begin()
    C.dma(shp[:, :], x0[NP_ - 1:NP_, :], [], [shp])
    C.dma(shs[:, :], x0.t[NP_:NT, :].rearrange("(b t) d -> b t d", t=4)[:, 3, :], [], [shs])
    C.stage_end()
    rw_stage_a(C, W, x0T, xpT, sc)
    rw_stage_b(C, sc, y_d, wkvp)
    rw_stage_s(C, sc, wkv_in, y_d, wkvs)
    rw_stage_c(C, W, sc, y_d, h0)
    ffn_stage_a(C, 0, W, x0, h0, xa, xaT, idx)
    ffn_stage_b(C, 0, W, xaT, idx, c_d)
    ffn_stage_c(C, 0, W, xa, c_d, pT[0], x1, x1T)
    kvq_stage(C, W, x1T, k_o, v_o, kT, qT, vb)
    attn_prompt_stage(C, W, kT, qT, vb, h1)
    attn_sample_stage(C, W, pt, ckT, cv, kT, qT, vb, h1)
    ffn_stage_a(C, 1, W, x1, h1, xa, xaT, idx)
    ffn_stage_b(C, 1, W, xaT, idx, c_d)
    ffn_stage_c(C, 1, W, xa, c_d, pT[1], y)
    es.close()
    return nc


def host_inputs(inputs, cores):
    w = prep_weights({k: v for k, v in inputs.items() if k.startswith(("rw_", "da_", "ln", "peer", "ple"))})
    cst = make_consts()
    ck = inputs["cache_k"]
    npool = ck.shape[0]
    ckT = np.ascontiguousarray(ck.transpose(0, 3, 2, 1)).reshape(npool * 128, D)
    cv = np.ascontiguousarray(inputs["cache_v"]).reshape(npool * 128, D)
    shared = dict(w)
    shared.update({"cst_" + k: v for k, v in cst.items()})
    shared["ckT"] = ckT
    shared["cv"] = cv
    maps = []
    for c in cores:
        sl = slice(16 * c, 16 * c + 16)
        xs = inputs["x_sample"][sl]
        x0 = np.concatenate([inputs["x_prompt"][c], xs.reshape(NS, D)], 0)
        xprev = np.zeros_like(x0)
        xprev[1:NP_] = x0[0:NP_ - 1]
        xprev[NP_:] = np.concatenate([inputs["state_shift"][0, sl][:, None, :], xs[:, :3]], 1).reshape(NS, D)
        m = dict(shared)
        m["x0"] = np.ascontiguousarray(x0)
        m["x0T"] = np.ascontiguousarray(x0.T)
        m["xpT"] = np.ascontiguousarray(xprev.T)
        m["wkv_in"] = np.ascontiguousarray(inputs["state_wkv"][0, sl])
        for L in range(2):
            p = np.concatenate([inputs["p_prompt"][L, c], inputs["p_sample"][L, sl].reshape(NS, 256)], 0)
            m[f"pT{L}"] = np.ascontiguousarray(p.T)
        m["ptab"] = np.ascontiguousarray(inputs["page_table"][sl]).astype(np.int32)
        maps.append(m)
    return maps, {k: v.shape for k, v in w.items()}, {k: v.shape for k, v in cst.items()}, npool * 128


def kernel(**inputs):
    inputs = {k: np.asarray(v) for k, v in inputs.items()}
    n = 8
    maps, wsh, csh, nrows = host_inputs(inputs, list(range(n)))
    nc = build_program(wsh, csh, nrows)
    res = run_bass_kernel_spmd(nc, maps, core_ids=list(range(n)))
    R = res.results
    f = np.float32
    y_p = np.stack([R[c]["y"][:NP_] for c in range(n)]).astype(f)
    y_s = np.concatenate([R[c]["y"][NP_:].reshape(16, 4, D) for c in range(n)]).astype(f)
    sh_p = np.stack([R[c]["shift_p"][0] for c in range(n)])[None].astype(f)
    wkv_p = np.stack([R[c]["wkv_p"] for c in range(n)])[None].astype(f)
    k_p = np.stack([R[c]["k_o"][:NP_].reshape(NP_, 8, 128) for c in range(n)]).astype(f)
    v_p = np.stack([R[c]["v_o"][:NP_].reshape(NP_, 8, 128) for c in range(n)]).astype(f)
    sh_s = np.concatenate([R[c]["shift_s"] for c in range(n)])[None].astype(f)
    wkv_s = np.concatenate([R[c]["wkv_s"] for c in range(n)])[None].astype(f)
    k_s = np.concatenate([R[c]["k_o"][NP_:].reshape(16, 4, 8, 128) for c in range(n)]).astype(f)
    v_s = np.concatenate([R[c]["v_o"][NP_:].reshape(16, 4, 8, 128) for c in range(n)]).astype(f)
    return (y_p, y_s, sh_p, wkv_p, k_p, v_p, sh_s, wkv_s, k_s, v_s)
```

```python
import numpy as np
from contextlib import ExitStack
import concourse.bass as bass
import concourse.mybir as mybir
from concourse.bass_utils import run_bass_kernel_spmd

F32 = mybir.dt.float32
BF16 = mybir.dt.bfloat16
I32 = mybir.dt.int32
U32 = mybir.dt.uint32
ALU = mybir.AluOpType
AF = mybir.ActivationFunctionType
AX = mybir.AxisListType

D = 1024
NP_ = 2048
NS = 64
NT = NP_ + NS
NBLK = 17
DN_ALPHA = 4.0 ** 0.25
LN_EPS = 1e-5


def blk_rows(b):
    return (b * 128, 128) if b < 16 else (2048, 64)


class Buf:
    __slots__ = ("w", "r", "x")

    def __init__(self, x=False):
        self.w = None
        self.r = {}
        self.x = x


class Tile:
    def __init__(self, t):
        self.t = t
        self.b = Buf()

    def __getitem__(self, k):
        return self.t[k]


def _b(x):
    return x.b if isinstance(x, Tile) else x


class Prog:
    ENG = ["pe", "dve", "act", "pool", "sp"]
    EPOCH = 30000
    NDMA = 32

    def __init__(self, nc, es):
        self.nc = nc
        self.es = es
        self.ops = {e: [] for e in self.ENG}
        self.cnt = {e: 0 for e in self.ENG}
        self.sems = {e: [] for e in self.ENG}
        self.dsem = [es.enter_context(nc.semaphore(f"dq{i}")) for i in range(self.NDMA)]
        self.dval = [0] * self.NDMA
        self.dnext = 0
        self.seen = {e: {} for e in self.ENG}

    def _sem(self, e, epoch):
        while len(self.sems[e]) <= epoch:
            self.sems[e].append(self.es.enter_context(self.nc.semaphore(f"c_{e}{len(self.sems[e])}")))
        return self.sems[e][epoch]

    def _need(self, e, tok, waits):
        if tok[0] == "c":
            _, e2, idx = tok
            if e2 == e and e == "pe":
                return
            key = ("c", e2)
            if self.seen[e].get(key, 0) >= idx:
                return
            self.seen[e][key] = idx
            waits.append((self._sem(e2, (idx - 1) // self.EPOCH), (idx - 1) % self.EPOCH + 1))
        else:
            _, k, val = tok
            key = ("d", k)
            if self.seen[e].get(key, 0) >= val:
                return
            self.seen[e][key] = val
            waits.append((self.dsem[k], val))

    def _deps(self, e, reads, writes):
        waits = []
        for b in reads:
            if b.w is not None:
                self._need(e, b.w, waits)
            if b.x:
                for key, val in b.r.items():
                    if key[1] != e:
                        self._need(e, (key[0], key[1], val), waits)
        for b in writes:
            if b.w is not None:
                self._need(e, b.w, waits)
            for key, val in b.r.items():
                self._need(e, (key[0], key[1], val), waits)
        return waits

    def _mark(self, tok, reads, writes):
        key = (tok[0], tok[1])
        for b in reads:
            if b.r.get(key, 0) < tok[2]:
                b.r[key] = tok[2]
        for b in writes:
            b.w = tok
            b.r = {}

    def op(self, e, fn, reads=(), writes=()):
        reads = [_b(x) for x in reads]
        writes = [_b(x) for x in writes]
        waits = self._deps(e, reads, writes)
        self.cnt[e] += 1
        idx = self.cnt[e]
        self.ops[e].append((waits, fn, self._sem(e, (idx - 1) // self.EPOCH), 1))
        self._mark(("c", e, idx), reads, writes)

    def dma(self, q, fn, reads=(), writes=()):
        reads = [_b(x) for x in reads]
        writes = [_b(x) for x in writes]
        waits = self._deps(q, reads, writes)
        k = self.dnext
        self.dnext = (k + 1) % self.NDMA
        if self.dval[k] > 0:
            self._need(q, ("d", k, self.dval[k]), waits)
        self.dval[k] += 16
        self.ops[q].append((waits, fn, self.dsem[k], 16))
        self._mark(("d", k, self.dval[k]), reads, writes)

    def barrier(self):
        for e in self.ENG:
            waits = []
            for e2 in self.ENG:
                if e2 != e and self.cnt[e2] > 0:
                    self._need(e, ("c", e2, self.cnt[e2]), waits)
            for k in range(self.NDMA):
                if self.dval[k] > 0:
                    self._need(e, ("d", k, self.dval[k]), waits)
            if waits:
                self.ops[e].append((waits, None, None, 0))

    def emit(self):
        P = self

        def replay(e, eng):
            for waits, fn, sem, inc in P.ops[e]:
                for s, v in waits:
                    eng.wait_ge(s, v)
                if fn is not None:
                    fn(eng).then_inc(sem, inc)
            P.ops[e] = []

        with self.nc.Block() as block:
            @block.tensor
            def _(eng):
                replay("pe", eng)

            @block.vector
            def _(eng):
                replay("dve", eng)

            @block.scalar
            def _(eng):
                replay("act", eng)

            @block.gpsimd
            def _(eng):
                replay("pool", eng)

            @block.sync
            def _(eng):
                replay("sp", eng)


class Ctx:
    def __init__(self, nc, es):
        self.nc = nc
        self.es = es
        self.P = Prog(nc, es)
        self.ses = None
        ps = es.enter_context(nc.psum_tensor("psum_all", [128, 8, 512], F32))
        self.ps = ps
        self.pb = [Buf(True) for _ in range(8)]
        self.dr = {}

    def sb(self, name, shape, dtype=F32):
        return Tile(self.ses.enter_context(self.nc.sbuf_tensor(f"s{self.stage_id}_{name}", shape, dtype)))

    def dram(self, name, shape, dtype=F32, kind="Internal"):
        t = Tile(self.nc.dram_tensor(name, shape, dtype, kind=kind))
        self.dr[name] = t
        return t

    def stage_begin(self):
        self.stage_id = getattr(self, "stage_id", 0) + 1
        self.ses = ExitStack()
        self.ses.__enter__()

    def stage_end(self):
        self.P.barrier()
        if getattr(self, "scopes", False):
            import inspect
            nm = inspect.stack()[1].function + f"_{self.stage_id}"
            with self.nc.named_scope(nm):
                self.P.emit()
        else:
            self.P.emit()
        self.ses.__exit__(None, None, None)
        self.ses = None

    def mm(self, out, lhsT, rhs, start, stop, r, w):
        self.P.op("pe", lambda e: e.matmul(out, lhsT=lhsT, rhs=rhs, start=start, stop=stop), r, w)

    def tr(self, out, in_, ident, r, w):
        self.P.op("pe", lambda e: e.transpose(out=out, in_=in_, identity=ident), r, w)

    def dma(self, out, in_, r, w, q="sp", slow=False):
        if slow:
            self.P.dma(q, lambda e: e.dma_start(out=out, in_=in_, allow_slow_non_contiguous=True), r, w)
        else:
            self.P.dma(q, lambda e: e.dma_start(out=out, in_=in_), r, w)

    def copy(self, eng, out, in_, r, w):
        if eng == "act":
            self.P.op("act", lambda e: e.copy(out=out, in_=in_), r, w)
        else:
            self.P.op(eng, lambda e: e.tensor_copy(out=out, in_=in_), r, w)

    def actf(self, out, in_, func, r, w, bias=None, scale=None, accum_out=None):
        kw = {}
        if bias is not None:
            kw["bias"] = bias
        if scale is not None:
            kw["scale"] = scale
        if accum_out is not None:
            kw["accum_out"] = accum_out
        self.P.op("act", lambda e: e.activation(out=out, in_=in_, func=func, **kw), r, w)

    def tt(self, eng, out, in0, in1, op, r, w):
        self.P.op(eng, lambda e: e.tensor_tensor(out=out, in0=in0, in1=in1, op=op), r, w)

    def ts(self, eng, out, in0, s1, s2, op0, op1, r, w):
        if op1 is None:
            self.P.op(eng, lambda e: e.tensor_scalar(out=out, in0=in0, scalar1=s1, scalar2=None, op0=op0), r, w)
        else:
            self.P.op(eng, lambda e: e.tensor_scalar(out=out, in0=in0, scalar1=s1, scalar2=s2, op0=op0, op1=op1), r, w)

    def stt(self, out, in0, scalar, in1, op0, op1, r, w):
        self.P.op("dve", lambda e: e.scalar_tensor_tensor(out=out, in0=in0, scalar=scalar, in1=in1, op0=op0, op1=op1), r, w)

    def red(self, out, in_, op, r, w, axis=AX.X):
        self.P.op("dve", lambda e: e.tensor_reduce(out=out, in_=in_, axis=axis, op=op), r, w)

    def memset(self, eng, out, val, w):
        self.P.op(eng, lambda e: e.memset(out, val), [], w)

    def recip(self, out, in_, r, w):
        self.P.op("dve", lambda e: e.reciprocal(out=out, in_=in_), r, w)

    def max8(self, out, in_, r, w):
        self.P.op("dve", lambda e: e.max(out=out, in_=in_), r, w)

    def maxidx(self, out, in_max, in_values, r, w):
        self.P.op("dve", lambda e: e.max_index(out=out, in_max=in_max, in_values=in_values), r, w)

    def mrep(self, out, rep, vals, r, w):
        self.P.op("dve", lambda e: e.match_replace(out=out, in_to_replace=rep, in_values=vals, imm_value=-1e30), r, w)

    def dvefn(self, fn, r, w):
        self.P.op("dve", fn, r, w)


def layer_norm_tok(C, nr, s, gam, bet, outp, scr):
    st, mv, rs = scr["st"], scr["mv"], scr["rs"]
    C.dvefn(lambda e: e.bn_stats(out=st[:nr, 0, :], in_=s[:nr, 0:512]), [s], [st])
    C.dvefn(lambda e: e.bn_stats(out=st[:nr, 1, :], in_=s[:nr, 512:1024]), [s], [st])
    C.dvefn(lambda e: e.bn_aggr(out=mv[:nr, :], in_=st[:nr, :, :].rearrange("p a b -> p (a b)")), [st], [mv])
    C.actf(rs[:nr, :], mv[:nr, 1:2], AF.Sqrt, [mv], [rs], bias=scr["eps"][:nr, 0:1])
    C.recip(rs[:nr, :], rs[:nr, :], [rs], [rs])
    C.ts("dve", outp[:nr, :], s[:nr, :], mv[:nr, 0:1], rs[:nr, 0:1], ALU.subtract, ALU.mult, [s, mv, rs], [outp])
    C.tt("dve", outp[:nr, :], outp[:nr, :], gam[:nr, :], ALU.mult, [outp, gam], [outp])
    C.tt("dve", outp[:nr, :], outp[:nr, :], bet[:nr, :], ALU.add, [outp, bet], [outp])


def to_featmajor(C, nr, xb16, outT, c0, identb, bank):
    pst = C.ps[:, bank, :].bitcast(BF16)
    for kc in range(8):
        C.tr(pst[:, kc * 128:kc * 128 + nr], xb16[:nr, kc * 128:(kc + 1) * 128], identb[:nr, :nr], [xb16, identb], [C.pb[bank]])
    C.copy("act", outT[:, :, c0:c0 + nr], pst.rearrange("p (k t) -> p k t", k=8)[:, :, :nr], [C.pb[bank]], [outT])


def ffn_stage_a(C, L, W, x_in, h_in, xa_d, xaT_d, idx_d):
    nc = C.nc
    C.stage_begin()
    cst = C.cst
    identf, identb, iotaf = load_consts(C)
    g1 = C.sb("g1", [128, D]); b1 = C.sb("b1", [128, D])
    C.dma(g1[:], W["ln1_g"][L:L + 1, :].partition_broadcast(128), [], [g1])
    C.dma(b1[:], W["ln1_b"][L:L + 1, :].partition_broadcast(128), [], [b1])
    wq = C.sb("wq", [128, 8, 2048], BF16)
    wq_src = W["peer_w_q"][L].rearrange("(k p) n -> p k n", p=128)
    for i in range(4):
        C.dma(wq[:, :, i * 512:(i + 1) * 512], wq_src[:, :, i * 512:(i + 1) * 512], [], [wq], q="pool")
    bq = C.sb("bq", [128, 16])
    C.dma(bq[:], W["peer_b_qT"][L], [], [bq])
    skT = C.sb("skT", [128, 2, 128], BF16)
    C.dma(skT[:], W["peer_skT"][L], [], [skT], q="pool")
    eps = C.sb("eps", [128, 1]); C.memset("dve", eps[:], LN_EPS, [eps])
    scr = {"st": C.sb("st", [128, 2, 6]), "mv": C.sb("mv", [128, 2]), "rs": C.sb("rs", [128, 1]), "eps": eps}
    NB = 2
    xin = [C.sb(f"xin{i}", [128, D]) for i in range(NB)]
    hin = [C.sb(f"hin{i}", [128, D]) for i in range(NB)]
    xa = [C.sb(f"xa{i}", [128, D]) for i in range(NB)]
    xab = [C.sb(f"xab{i}", [128, D], BF16) for i in range(NB)]
    xaT = [C.sb(f"xaT{i}", [128, 8, 128], BF16) for i in range(NB)]
    qT = [C.sb(f"qT{i}", [128, 16, 128], BF16) for i in range(NB)]
    sc = [C.sb(f"sc{i}", [128, 16, 128]) for i in range(NB)]
    top = C.sb("top", [128, 16, 16]); idxu = C.sb("idxu", [128, 16, 16], U32); idxf = C.sb("idxf", [128, 16, 16])
    cand = C.sb("cand", [128, 8, 256])
    sel = C.sb("sel", [128, 8, 16]); cidx = C.sb("cidx", [128, 8, 16], U32); cf = C.sb("cf", [128, 8, 16])
    thr = C.sb("thr", [128, 15])
    for m in range(15):
        C.memset("pool", thr[:, m:m + 1], 16.0 * (m + 1) - 0.5, [thr])
    ge = C.sb("ge", [128, 128, 15]); af = C.sb("af", [128, 8, 16]); bf_ = C.sb("bf_", [128, 8, 16])
    oh = C.sb("oh", [128, 8, 16, 16])
    res3 = [C.sb(f"res3_{i}", [128, 3, 128]) for i in range(NB)]
    mx = C.sb("mx", [128, 8]); zs = C.sb("zs", [128, 8])
    resT = [C.sb(f"resT{i}", [128, 3, 128]) for i in range(NB)]
    scbs = [[Buf() for _ in range(16)] for _ in range(NB)]
    topb = [Buf() for _ in range(16)]; idxb = [Buf() for _ in range(16)]
    candb = [Buf() for _ in range(8)]; selb = [Buf() for _ in range(8)]; cidxb = [Buf() for _ in range(8)]

    for blk in range(NBLK):
        r0, nr = blk_rows(blk)
        s_ = blk % NB
        X, H, XA, XAB, XAT, QT, SC, R3, RT = xin[s_], hin[s_], xa[s_], xab[s_], xaT[s_], qT[s_], sc[s_], res3[s_], resT[s_]
        C.dma(X[:nr, :], x_in[r0:r0 + nr, :], [x_in], [X])
        C.dma(H[:nr, :], h_in[r0:r0 + nr, :], [h_in], [H])
        C.stt(X[:nr, :], X[:nr, :], DN_ALPHA, H[:nr, :], ALU.mult, ALU.add, [X, H], [X])
        layer_norm_tok(C, nr, X, g1, b1, XA, scr)
        C.dma(xa_d[r0:r0 + nr, :], XA[:nr, :], [XA], [xa_d], q="act")
        C.copy("act", XAB[:nr, :], XA[:nr, :], [XA], [XAB])
        to_featmajor(C, nr, XAB, XAT, 0, identb, 0)
        C.dma(xaT_d[:, :, r0:r0 + nr], XAT[:, :, :nr], [XAT], [xaT_d], q="act")
        for c16 in range(16):
            bank = 1 + (c16 // 4) % 2
            pq = C.ps[:, bank, (c16 % 4) * 128:(c16 % 4) * 128 + nr]
            for kc in range(8):
                C.mm(pq, wq[:, kc, c16 * 128:(c16 + 1) * 128], XAT[:, kc, :nr], kc == 0, kc == 7, [wq, XAT], [C.pb[bank]])
            C.actf(QT[:, c16, :nr], pq, AF.Identity, [C.pb[bank], bq], [QT], bias=bq[:, c16:c16 + 1])
        for c16 in range(16):
            bank = 3 + c16 // 4
            C.mm(C.ps[:nr, bank, (c16 % 4) * 128:(c16 % 4 + 1) * 128], QT[:, c16, :nr], skT[:, c16 % 2, :], True, True, [QT, skT], [C.pb[bank]])
        for bk in range(4):
            C.copy("act" if bk % 2 else "dve", SC[:nr, bk * 4:(bk + 1) * 4, :], C.ps[:nr, 3 + bk, :].rearrange("p (a n) -> p a n", a=4), [C.pb[3 + bk]], scbs[s_][4 * bk:4 * bk + 4])
        if getattr(C, "dbg", None) and blk == 0:
            C.dma(C.dbg["sc"][:], SC[:], scbs[s_], [C.dbg["sc"]])
        scb = scbs[s_]
        for g4 in range(4):
            sgs = range(4 * g4, 4 * g4 + 4)
            for sg in sgs:
                C.max8(top[:nr, sg, 0:8], SC[:nr, sg, :], [scb[sg]], [topb[sg]])
            for sg in sgs:
                C.maxidx(idxu[:nr, sg, 0:8], top[:nr, sg, 0:8], SC[:nr, sg, :], [scb[sg], topb[sg]], [idxb[sg]])
            for sg in sgs:
                C.mrep(SC[:nr, sg, :], top[:nr, sg, 0:8], SC[:nr, sg, :], [scb[sg], topb[sg]], [scb[sg]])
            for sg in sgs:
                C.max8(top[:nr, sg, 8:16], SC[:nr, sg, :], [scb[sg]], [topb[sg]])
            for sg in sgs:
                C.maxidx(idxu[:nr, sg, 8:16], top[:nr, sg, 8:16], SC[:nr, sg, :], [scb[sg], topb[sg]], [idxb[sg]])
        C.copy("dve", idxf[:nr], idxu[:nr], idxb, [idxf])
        tv = top[:nr].rearrange("p (h c) k -> p h c k", c=2)
        C.tt("dve", cand[:nr].rearrange("p h (a b) -> p h a b", a=16), tv[:, :, 0, :].unsqueeze(3).to_broadcast([nr, 8, 16, 16]),
             tv[:, :, 1, :].unsqueeze(2).to_broadcast([nr, 8, 16, 16]), ALU.add, topb, candb)
        for g4 in range(2):
            hs = range(4 * g4, 4 * g4 + 4)
            for h in hs:
                C.max8(sel[:nr, h, 0:8], cand[:nr, h, :], [candb[h]], [selb[h]])
            for h in hs:
                C.maxidx(cidx[:nr, h, 0:8], sel[:nr, h, 0:8], cand[:nr, h, :], [candb[h], selb[h]], [cidxb[h]])
            for h in hs:
                C.mrep(cand[:nr, h, :], sel[:nr, h, 0:8], cand[:nr, h, :], [candb[h], selb[h]], [candb[h]])
            for h in hs:
                C.max8(sel[:nr, h, 8:16], cand[:nr, h, :], [candb[h]], [selb[h]])
            for h in hs:
                C.maxidx(cidx[:nr, h, 8:16], sel[:nr, h, 8:16], cand[:nr, h, :], [candb[h], selb[h]], [cidxb[h]])
        C.copy("dve", cf[:nr], cidx[:nr], cidxb, [cf])
        if getattr(C, "dbg", None) and blk == 0:
            C.dma(C.dbg["top"][:], top[:], topb, [C.dbg["top"]])
            C.dma(C.dbg["idxf"][:], idxf[:], [idxf], [C.dbg["idxf"]])
            C.dma(C.dbg["sel"][:], sel[:], selb, [C.dbg["sel"]])
            C.dma(C.dbg["cf"][:], cf[:], [cf], [C.dbg["cf"]])
        cfl = cf[:nr].rearrange("p h k -> p (h k)")
        C.tt("dve", ge[:nr], cfl.unsqueeze(2).to_broadcast([nr, 128, 15]), thr[:nr, :].unsqueeze(1).to_broadcast([nr, 128, 15]), ALU.is_ge, [cf, thr], [ge])
        C.red(af[:nr].rearrange("p h k -> p (h k)"), ge[:nr], ALU.add, [ge], [af])
        C.stt(bf_[:nr], af[:nr], -16.0, cf[:nr], ALU.mult, ALU.add, [af, cf], [bf_])
        iv = idxf[:nr].rearrange("p (h c) k -> p h c k", c=2)
        for which, sel_ab in ((0, af), (1, bf_)):
            C.tt("dve", oh[:nr], sel_ab[:nr].unsqueeze(3).to_broadcast([nr, 8, 16, 16]),
                 iotaf[:nr, 0:16].unsqueeze(1).unsqueeze(1).to_broadcast([nr, 8, 16, 16]), ALU.is_equal, [sel_ab, iotaf], [oh])
            C.tt("dve", oh[:nr], oh[:nr], iv[:, :, which, :].unsqueeze(2).to_broadcast([nr, 8, 16, 16]), ALU.mult, [oh, idxf], [oh])
            C.red(R3[:nr, which, :], oh[:nr].rearrange("p h k a -> p (h k) a"), ALU.add, [oh], [R3])
        C.ts("dve", mx[:nr, :], sel[:nr, :, 0], -1.0, None, ALU.mult, None, selb, [mx])
        gv = R3[:nr, 2, :].rearrange("p (h k) -> p h k", h=8)
        C.tt("dve", gv, sel[:nr], mx[:nr, :].unsqueeze(2).to_broadcast([nr, 8, 16]), ALU.add, selb + [mx], [R3])
        C.actf(gv, gv, AF.Exp, [R3], [R3])
        C.red(zs[:nr, :], gv, ALU.add, [R3], [zs])
        C.recip(zs[:nr, :], zs[:nr, :], [zs], [zs])
        C.tt("dve", gv, gv, zs[:nr, :].unsqueeze(2).to_broadcast([nr, 8, 16]), ALU.mult, [R3, zs], [R3])
        for q3 in range(3):
            C.tr(C.ps[:, 7, q3 * 128:q3 * 128 + nr], R3[:nr, q3, :], identf[:nr, :nr], [R3, identf], [C.pb[7]])
        C.copy("act", RT[:, :, :nr], C.ps[:, 7, 0:384].rearrange("p (a t) -> p a t", a=3)[:, :, :nr], [C.pb[7]], [RT])
        C.dma(idx_d[:, :, r0:r0 + nr], RT[:, :, :nr], [RT], [idx_d], q="act")
    C.stage_end()


def load_consts(C):
    identf = C.sb("identf", [128, 128]); identb = C.sb("identb", [128, 128], BF16); iotaf = C.sb("iotaf", [128, 128])
    C.dma(identf[:], C.cst["ident"], [], [identf])
    C.dma(iotaf[:], C.cst["iota"], [], [iotaf])
    C.copy("dve", identb[:], identf[:], [identf], [identb])
    return identf, identb, iotaf


CHUNKS = [[0, 1, 2], [3, 4, 5], [6, 7, 8], [9, 10, 11], [12, 13, 14], [15, 16]]


def ffn_stage_b(C, L, W, xaT_d, idx_d, c_d, chunks=None):
    C.stage_begin()
    identf, identb, iotaf = load_consts(C)
    TC = 384
    Gall = C.sb("Gall", [128, 128, TC], BF16)
    xaT = C.sb("xaTc", [128, 8, TC], BF16)
    idxT = C.sb("idxTc", [128, 3, TC])
    NAB = 4
    At = [C.sb(f"At{i}", [128, 128], BF16) for i in range(NAB)]
    Bt = [C.sb(f"Bt{i}", [128, 128], BF16) for i in range(NAB)]
    NW = 4
    ut = [C.sb(f"ut{i}", [128, 8, 128], BF16) for i in range(NW)]
    vt = [C.sb(f"vt{i}", [128, D], BF16) for i in range(NW)]
    hg = [C.sb(f"hg{i}", [128, TC], BF16) for i in range(2)]
    actT = [C.sb(f"actT{i}", [128, TC], BF16) for i in range(2)]
    cout = [C.sb(f"cout{i}", [128, D]) for i in range(2)]
    U_src = W["peer_uh"][L]
    V_src = W["peer_v"][L]
    for ch in (chunks if chunks is not None else CHUNKS):
        cols = []
        c0 = 0
        for blk in ch:
            r0, nr = blk_rows(blk)
            cols.append((blk, r0, nr, c0))
            c0 += nr
        tc = c0
        R0 = cols[0][1]
        C.dma(xaT[:, :, :tc], xaT_d[:, :, R0:R0 + tc], [xaT_d], [xaT])
        C.dma(idxT[:, :, :tc], idx_d[:, :, R0:R0 + tc], [idx_d], [idxT])
        for t in range(tc):
            s_ = t % NAB
            C.ts("dve", At[s_][:], iotaf[:], idxT[:, 0, t:t + 1], idxT[:, 2, t:t + 1], ALU.is_equal, ALU.mult, [iotaf, idxT], [At[s_]])
            C.ts("dve", Bt[s_][:], iotaf[:], idxT[:, 1, t:t + 1], None, ALU.is_equal, None, [iotaf, idxT], [Bt[s_]])
            bank = (t // 4) % 2
            C.mm(C.ps[:, bank, (t % 4) * 128:(t % 4 + 1) * 128], Bt[s_][:], At[s_][:], True, True, [At[s_], Bt[s_]], [C.pb[bank]])
            if t % 4 == 3 or t == tc - 1:
                n = t % 4 + 1
                t0 = t - (n - 1)
                C.copy("act", Gall[:, :, t0:t0 + n], C.ps[:, bank, 0:n * 128].rearrange("p (q i) -> p i q", q=n), [C.pb[bank]], [Gall])
        def emit_h(i1):
            w_ = i1 % NW
            C.dma(ut[w_][:], U_src[i1].rearrange("p (k e) -> p k e", k=8), [], [ut[w_]], q="pool")
            C.dma(vt[w_][:], V_src[i1 * 128:(i1 + 1) * 128, :], [], [vt[w_]], q="pool")
            hb = 6 + i1 % 2
            for kc in range(8):
                C.mm(C.ps[:, hb, :tc], ut[w_][:, kc, :], xaT[:, kc, :tc], kc == 0, kc == 7, [ut[w_], xaT], [C.pb[hb]])

        def emit_rest(i1):
            w_ = i1 % NW
            hb = 6 + i1 % 2
            HG, AT = hg[i1 % 2], actT[i1 % 2]
            C.actf(HG[:, :tc], C.ps[:, hb, :tc], AF.Gelu_apprx_tanh, [C.pb[hb]], [HG])
            C.tt("dve", AT[:, :tc], HG[:, :tc], Gall[:, i1, :tc], ALU.mult, [HG, Gall], [AT])
            for bi, (blk, r0, nr, cc) in enumerate(cols):
                for nh in range(2):
                    bank = 2 * bi + nh
                    C.mm(C.ps[:nr, bank, :], AT[:, cc:cc + nr], vt[w_][:, nh * 512:(nh + 1) * 512], i1 == 0, i1 == 127, [AT, vt[w_]], [C.pb[bank]])

        emit_h(0)
        for i1 in range(128):
            if i1 + 1 < 128:
                emit_h(i1 + 1)
            emit_rest(i1)
        for bi, (blk, r0, nr, cc) in enumerate(cols):
            co = cout[bi % 2]
            C.copy("act", co[:nr, 0:512], C.ps[:nr, 2 * bi, :], [C.pb[2 * bi]], [co])
            C.copy("dve", co[:nr, 512:1024], C.ps[:nr, 2 * bi + 1, :], [C.pb[2 * bi + 1]], [co])
            C.dma(c_d[r0:r0 + nr, :], co[:nr, :], [co], [c_d])
    C.stage_end()


def ffn_stage_c(C, L, W, xa_d, c_d, pT_d, x_out, xoT_d=None):
    C.stage_begin()
    identf, identb, iotaf = load_consts(C)
    g2 = C.sb("g2", [128, D]); b2 = C.sb("b2", [128, D])
    C.dma(g2[:], W["ln2_g"][L:L + 1, :].partition_broadcast(128), [], [g2])
    C.dma(b2[:], W["ln2_b"][L:L + 1, :].partition_broadcast(128), [], [b2])
    wg = C.sb("wg", [128, 8, D], BF16)
    wg_src = W["ple_w_g"][L].rearrange("(k p) n -> p k n", p=128)
    for i in range(2):
        C.dma(wg[:, :, i * 512:(i + 1) * 512], wg_src[:, :, i * 512:(i + 1) * 512], [], [wg], q="pool")
    wp = C.sb("wp", [128, 2, D], BF16)
    C.dma(wp[:], W["ple_w_p"][L].rearrange("(k p) n -> p k n", p=128), [], [wp], q="pool")
    bg = C.sb("bg", [1, D], BF16)
    C.dma(bg[:], W["ple_b_g"][L:L + 1, :], [], [bg], q="pool")
    ones = C.sb("ones1", [1, 128], BF16); C.memset("dve", ones[:], 1.0, [ones])
    eps = C.sb("eps", [128, 1]); C.memset("dve", eps[:], LN_EPS, [eps])
    scr = {"st": C.sb("st", [128, 2, 6]), "mv": C.sb("mv", [128, 2]), "rs": C.sb("rs", [128, 1]), "eps": eps}
    NB = 2
    xa = [C.sb(f"xa{i}", [128, D]) for i in range(NB)]
    cc = [C.sb(f"cc{i}", [128, D]) for i in range(NB)]
    xb = [C.sb(f"xb{i}", [128, D]) for i in range(NB)]
    xbb = [C.sb(f"xbb{i}", [128, D], BF16) for i in range(NB)]
    xbT = [C.sb(f"xbT{i}", [128, 8, 128], BF16) for i in range(NB)]
    pT = [C.sb(f"pT{i}", [128, 2, 128], BF16) for i in range(NB)]
    sig = [C.sb(f"sig{i}", [128, D]) for i in range(NB)]
    xo = [C.sb(f"xo{i}", [128, D]) for i in range(NB)]
    xob = [C.sb(f"xob{i}", [128, D], BF16) for i in range(NB)]
    xoT = [C.sb(f"xoT{i}", [128, 8, 128], BF16) for i in range(NB)]
    for blk in range(NBLK):
        r0, nr = blk_rows(blk)
        s_ = blk % NB
        XA, CC, XB, XBB, XBT, PT, SG, XO = xa[s_], cc[s_], xb[s_], xbb[s_], xbT[s_], pT[s_], sig[s_], xo[s_]
        pbase = 4 * s_
        C.dma(XA[:nr, :], xa_d[r0:r0 + nr, :], [xa_d], [XA])
        C.dma(CC[:nr, :], c_d[r0:r0 + nr, :], [c_d], [CC])
        C.dma(PT[:, :, :nr], pT_d[:, r0:r0 + nr].rearrange("(k p) t -> p k t", p=128), [], [PT], q="pool")
        C.stt(XA[:nr, :], XA[:nr, :], DN_ALPHA, CC[:nr, :], ALU.mult, ALU.add, [XA, CC], [XA])
        layer_norm_tok(C, nr, XA, g2, b2, XB, scr)
        C.copy("act", XBB[:nr, :], XB[:nr, :], [XB], [XBB])
        to_featmajor(C, nr, XBB, XBT, 0, identb, pbase)
        for nh in range(2):
            bank = pbase + nh
            for kc in range(8):
                C.mm(C.ps[:nr, bank, :], XBT[:, kc, :nr], wg[:, kc, nh * 512:(nh + 1) * 512], kc == 0, False, [XBT, wg], [C.pb[bank]])
            C.mm(C.ps[:nr, bank, :], ones[0:1, :nr], bg[0:1, nh * 512:(nh + 1) * 512], False, True, [ones, bg], [C.pb[bank]])
            C.actf(SG[:nr, nh * 512:(nh + 1) * 512], C.ps[:nr, bank, :], AF.Sigmoid, [C.pb[bank]], [SG])
        for nh in range(2):
            bank = pbase + 2 + nh
            for kc in range(2):
                C.mm(C.ps[:nr, bank, :], PT[:, kc, :nr], wp[:, kc, nh * 512:(nh + 1) * 512], kc == 0, kc == 1, [PT, wp], [C.pb[bank]])
            C.tt("dve", SG[:nr, nh * 512:(nh + 1) * 512], SG[:nr, nh * 512:(nh + 1) * 512], C.ps[:nr, bank, :], ALU.mult, [SG, C.pb[bank]], [SG])
        C.tt("dve", XO[:nr, :], SG[:nr, :], XB[:nr, :], ALU.add, [SG, XB], [XO])
        C.dma(x_out[r0:r0 + nr, :], XO[:nr, :], [XO], [x_out], q="act")
        if xoT_d is not None:
            XOB, XOT = xob[s_], xoT[s_]
            C.copy("act", XOB[:nr, :], XO[:nr, :], [XO], [XOB])
            to_featmajor(C, nr, XOB, XOT, 0, identb, pbase)
            C.dma(xoT_d[:, :, r0:r0 + nr], XOT[:, :, :nr], [XOT], [xoT_d], q="act")
    C.stage_end()


def make_consts():
    ar = np.arange(128)
    pm = np.zeros((128, 2), np.float32)
    pm[:64, 0] = 1.0
    pm[64:, 1] = 1.0
    return {
        "ident": np.eye(128, dtype=np.float32),
        "iota": np.ascontiguousarray(np.broadcast_to(np.arange(128, dtype=np.float32)[None, :], (128, 128))),
        "lincl": (ar[:, None] <= ar[None, :]).astype(np.float32),
        "lstrict": (ar[:, None] < ar[None, :]).astype(np.float32),
        "llow": (ar[None, :] < ar[:, None]).astype(np.float32),
        "pm": pm,
        **attn_consts(),
    }


def attn_consts():
    p = np.arange(128, dtype=np.float64)
    slopes = 2.0 ** (-8.0 * np.arange(1, 9, dtype=np.float64) / 8.0)
    btab = slopes[None, :, None] * (p[:, None, None] - 127.0 - 128.0 * np.arange(16)[None, None, :])
    q = np.arange(4, dtype=np.float64)
    tabS = np.zeros((128, 16, 8, 2, 4))
    tabS += slopes[None, None, :, None, None] * (np.arange(16)[None, :, None, None, None] * 128.0 + p[:, None, None, None, None] - 2048.0 - q[None, None, None, None, :])
    tabN = np.full((128, 8, 2, 4), -1e30)
    for pp in range(4):
        for qq in range(4):
            if pp <= qq:
                tabN[pp, :, :, qq] = slopes[:, None] * (pp - qq)
    cA = np.zeros((8, 124)); cB = np.zeros((8, 124))
    for i in range(4):
        cA[i, 60 + i] = 1.0
        cB[4 + i, 60 + i] = -1.0
    f = lambda a: np.ascontiguousarray(a.astype(np.float32))
    return {"btab": f(btab), "tabS": f(tabS.reshape(128, 16, 64)), "tabN": f(tabN.reshape(128, 64)), "cmbA": f(cA), "cmbB": f(cB),
            "iotap": f(p.reshape(128, 1))}


def prep_weights(w):
    o = {}
    for k in ("ln1_g", "ln1_b", "ln2_g", "ln2_b", "peer_w_q", "peer_v", "ple_w_p", "ple_w_g", "ple_b_g"):
        if k in w:
            o[k] = np.ascontiguousarray(w[k])
    for k in ("rw_w_r", "rw_w_k", "rw_w_v", "rw_w0", "rw_w1", "rw_w2", "rw_a0", "rw_a1", "rw_a2", "rw_g1", "rw_g2", "rw_k_k", "rw_k_a",
              "rw_gn_g", "rw_gn_b", "rw_w_o", "da_w_k", "da_w_v", "da_w_q", "da_lam", "da_subln_g", "da_w_o"):
        if k in w:
            o[k] = np.ascontiguousarray(w[k])
    if "rw_mu" in w:
        o["rw_muT"] = np.ascontiguousarray(w["rw_mu"][0].reshape(6, 8, 128).transpose(2, 0, 1))
        o["rw_r_kf"] = np.ascontiguousarray(w["rw_r_k"].reshape(1, 1024))
    if "peer_b_q" in w:
        o["peer_b_qT"] = np.ascontiguousarray(w["peer_b_q"].reshape(2, 16, 128).transpose(0, 2, 1))
        o["peer_skT"] = np.ascontiguousarray(w["peer_subkeys"].transpose(0, 3, 1, 2))
        u = w["peer_u"].reshape(2, 128, 128, 8, 128)
        o["peer_uh"] = np.ascontiguousarray(u.transpose(0, 1, 4, 3, 2)).reshape(2, 128, 128, 1024)
    return o


class BankRot:
    def __init__(self):
        self.i1 = 0
        self.i2 = 0

    def one(self):
        b = self.i1 % 8
        self.i1 += 1
        return b

    def two(self):
        b = (self.i2 % 4) * 2
        self.i2 += 1
        return b

    def pair(self):
        b = self.two()
        return (b, b + 1)


def rw_stage_a(C, W, x0T_d, xpT_d, sc_d, blks=None):
    import os
    _STOP = int(os.environ.get('RWA_STOP', '99'))
    C.stage_begin()
    identf, identb, iotaf = load_consts(C)
    BR = BankRot()
    wts = {}
    for nm in ("rw_w_r", "rw_w_k", "rw_w_v"):
        t = C.sb(nm, [128, 8, D], BF16)
        src = W[nm][0].rearrange("(k p) n -> p k n", p=128)
        for i in range(2):
            C.dma(t[:, :, i * 512:(i + 1) * 512], src[:, :, i * 512:(i + 1) * 512], [], [t], q="pool")
        wts[nm] = t
    w1 = C.sb("w1", [128, 8, 64], BF16); C.dma(w1[:], W["rw_w1"][0].rearrange("(k p) n -> p k n", p=128), [], [w1], q="pool")
    a1 = C.sb("a1", [128, 8, 64], BF16); C.dma(a1[:], W["rw_a1"][0].rearrange("(k p) n -> p k n", p=128), [], [a1], q="pool")
    g1 = C.sb("g1w", [128, 8, 128], BF16); C.dma(g1[:], W["rw_g1"][0].rearrange("(k p) n -> p k n", p=128), [], [g1], q="pool")
    w2 = C.sb("w2", [64, D], BF16); C.dma(w2[:], W["rw_w2"][0], [], [w2], q="pool")
    a2 = C.sb("a2", [64, D], BF16); C.dma(a2[:], W["rw_a2"][0], [], [a2], q="pool")
    g2 = C.sb("g2w", [128, D], BF16); C.dma(g2[:], W["rw_g2"][0], [], [g2], q="pool")
    bc = {}
    for nm in ("rw_w0", "rw_a0", "rw_k_k", "rw_k_a", "rw_r_kf"):
        t = C.sb("b_" + nm, [128, D])
        C.dma(t[:], W[nm][0:1, :].partition_broadcast(128), [], [t])
        bc[nm] = t
    mu = C.sb("mu", [128, 6, 8]); C.dma(mu[:], W["rw_muT"], [], [mu])
    Lt = C.sb("Lt", [128, 128]); C.dma(Lt[:], C.cst["lincl"], [], [Lt])
    onec = C.sb("onec", [128, 2]); C.memset("dve", onec[:], 1.0, [onec])
    xc = C.sb("xc", [128, 8, 128]); xp = C.sb("xp", [128, 8, 128]); xx = C.sb("xx", [128, 8, 128])
    mix = [C.sb(f"mix{i}", [128, 8, 128], BF16) for i in range(6)]
    hwT = C.sb("hwT", [64, 128], BF16); haT = C.sb("haT", [64, 128], BF16); hgT = C.sb("hgT", [128, 128], BF16)
    r_ = C.sb("r_", [128, D]); k_ = C.sb("k_", [128, D]); v_ = C.sb("v_", [128, D]); lw = C.sb("lw", [128, D]); a_ = C.sb("a_", [128, D]); g_ = C.sb("g_", [128, D])
    kk = C.sb("kk", [128, D]); t1 = C.sb("t1", [128, D]); t2 = C.sb("t2", [128, D]); t3 = C.sb("t3", [128, D])
    ss = C.sb("ss", [128, 16]); bon = C.sb("bon", [128, 16]); wc = C.sb("wc", [128, 8])
    ob = [C.sb(f"ob{i}", [128, D], BF16) for i in range(5)]

    def v3(t, nr):
        return t[:nr, :].rearrange("p (h k) -> p h k", h=16)

    for blk in (range(NBLK) if blks is None else blks):
        r0, nr = blk_rows(blk)
        smp = blk == 16
        xsrc = x0T_d.t.rearrange("(k p) t -> p k t", p=128)
        psrc = xpT_d.t.rearrange("(k p) t -> p k t", p=128)
        C.dma(xc[:, :, :nr], xsrc[:, :, r0:r0 + nr], [], [xc])
        C.dma(xp[:, :, :nr], psrc[:, :, r0:r0 + nr], [], [xp])
        C.tt("dve", xx[:, :, :nr], xp[:, :, :nr], xc[:, :, :nr], ALU.subtract, [xp, xc], [xx])
        for i in range(6):
            for kc in range(8):
                C.stt(mix[i][:, kc, :nr], xx[:, kc, :nr], mu[:, i, kc:kc + 1], xc[:, kc, :nr], ALU.mult, ALU.add, [xx, xc, mu], [mix[i]])
        if _STOP == 1:
            continue
        xr, xw, xk, xv, xa, xg = mix
        for nm, xm, dst in (("rw_w_r", xr, r_), ("rw_w_k", xk, k_), ("rw_w_v", xv, v_)):
            b2 = BR.two()
            for nh in range(2):
                for kc in range(8):
                    C.mm(C.ps[:nr, b2 + nh, :], xm[:, kc, :nr], wts[nm][:, kc, nh * 512:(nh + 1) * 512], kc == 0, kc == 7, [xm, wts[nm]], [C.pb[b2 + nh]])
                C.copy("act", dst[:nr, nh * 512:(nh + 1) * 512], C.ps[:nr, b2 + nh, :], [C.pb[b2 + nh]], [dst])
        if _STOP == 2:
            continue
        b1 = BR.one()
        for kc in range(8):
            C.mm(C.ps[:64, b1, :nr], w1[:, kc, :], xw[:, kc, :nr], kc == 0, kc == 7, [w1, xw], [C.pb[b1]])
        C.actf(hwT[:, :nr], C.ps[:64, b1, :nr], AF.Tanh, [C.pb[b1]], [hwT])
        b2 = BR.two()
        for nh in range(2):
            C.mm(C.ps[:nr, b2 + nh, :], hwT[:, :nr], w2[:, nh * 512:(nh + 1) * 512], True, True, [hwT, w2], [C.pb[b2 + nh]])
            C.tt("dve", lw[:nr, nh * 512:(nh + 1) * 512], C.ps[:nr, b2 + nh, :], bc["rw_w0"][:nr, nh * 512:(nh + 1) * 512], ALU.add, [C.pb[b2 + nh], bc["rw_w0"]], [lw])
        C.actf(lw[:nr, :], lw[:nr, :], AF.Sigmoid, [lw], [lw])
        C.ts("dve", lw[:nr, :], lw[:nr, :], -float(np.exp(-0.5)), None, ALU.mult, None, [lw], [lw])
        if _STOP == 3:
            continue
        b1 = BR.one()
        for kc in range(8):
            C.mm(C.ps[:64, b1, :nr], a1[:, kc, :], xa[:, kc, :nr], kc == 0, kc == 7, [a1, xa], [C.pb[b1]])
        C.copy("act", haT[:, :nr], C.ps[:64, b1, :nr], [C.pb[b1]], [haT])
        b2 = BR.two()
        for nh in range(2):
            C.mm(C.ps[:nr, b2 + nh, :], haT[:, :nr], a2[:, nh * 512:(nh + 1) * 512], True, True, [haT, a2], [C.pb[b2 + nh]])
            C.tt("dve", a_[:nr, nh * 512:(nh + 1) * 512], C.ps[:nr, b2 + nh, :], bc["rw_a0"][:nr, nh * 512:(nh + 1) * 512], ALU.add, [C.pb[b2 + nh], bc["rw_a0"]], [a_])
        C.actf(a_[:nr, :], a_[:nr, :], AF.Sigmoid, [a_], [a_])
        if _STOP == 4:
            continue
        b1 = BR.one()
        for kc in range(8):
            C.mm(C.ps[:, b1, :nr], g1[:, kc, :], xg[:, kc, :nr], kc == 0, kc == 7, [g1, xg], [C.pb[b1]])
        C.actf(hgT[:, :nr], C.ps[:, b1, :nr], AF.Sigmoid, [C.pb[b1]], [hgT])
        b2 = BR.two()
        for nh in range(2):
            C.mm(C.ps[:nr, b2 + nh, :], hgT[:, :nr], g2[:, nh * 512:(nh + 1) * 512], True, True, [hgT, g2], [C.pb[b2 + nh]])
            C.copy("act", g_[:nr, nh * 512:(nh + 1) * 512], C.ps[:nr, b2 + nh, :], [C.pb[b2 + nh]], [g_])
        C.dma(sc_d["g32"][r0:r0 + nr, :], g_[:nr, :], [g_], [sc_d["g32"]], q="pool")
        C.dma(sc_d["v32"][r0:r0 + nr, :], v_[:nr, :], [v_], [sc_d["v32"]], q="pool")
        if _STOP == 5:
            continue
        C.tt("pool", kk[:nr, :], k_[:nr, :], bc["rw_k_k"][:nr, :], ALU.mult, [k_, bc["rw_k_k"]], [kk])
        C.tt("dve", t1[:nr, :], kk[:nr, :], kk[:nr, :], ALU.mult, [kk], [t1])
        C.red(ss[:nr, :], v3(t1, nr), ALU.add, [t1], [ss])
        C.actf(ss[:nr, :], ss[:nr, :], AF.Sqrt, [ss], [ss])
        C.ts("dve", ss[:nr, :], ss[:nr, :], 1e-12, None, ALU.max, None, [ss], [ss])
        C.recip(ss[:nr, :], ss[:nr, :], [ss], [ss])
        C.tt("dve", v3(kk, nr), v3(kk, nr), ss[:nr, :].unsqueeze(2).to_broadcast([nr, 16, 64]), ALU.mult, [kk, ss], [kk])
        if _STOP == 6:
            continue
        C.ts("dve", t1[:nr, :], a_[:nr, :], -1.0, None, ALU.add, None, [a_], [t1])
        C.tt("pool", t1[:nr, :], t1[:nr, :], bc["rw_k_a"][:nr, :], ALU.mult, [t1, bc["rw_k_a"]], [t1])
        C.tt("dve", t1[:nr, :], t1[:nr, :], k_[:nr, :], ALU.mult, [t1, k_], [t1])
        C.tt("dve", k_[:nr, :], k_[:nr, :], t1[:nr, :], ALU.add, [t1, k_], [k_])
        C.tt("pool", t1[:nr, :], r_[:nr, :], bc["rw_r_kf"][:nr, :], ALU.mult, [r_, bc["rw_r_kf"]], [t1])
        C.tt("dve", t1[:nr, :], t1[:nr, :], k_[:nr, :], ALU.mult, [t1, k_], [t1])
        C.red(bon[:nr, :], v3(t1, nr), ALU.add, [t1], [bon])
        C.dma(sc_d["bon"][r0:r0 + nr, :], bon[:nr, :], [bon], [sc_d["bon"]], q="pool")
        C.tt("dve", a_[:nr, :], a_[:nr, :], kk[:nr, :], ALU.mult, [a_, kk], [a_])
        if _STOP == 7:
            continue
        if smp:
            C.actf(lw[:nr, :], lw[:nr, :], AF.Exp, [lw], [lw])
            C.ts("dve", kk[:nr, :], kk[:nr, :], -1.0, None, ALU.mult, None, [kk], [kk])
            for qi, src in enumerate((r_, lw, k_, v_, kk, a_)):
                dst = sc_d["smp"].t[qi].rearrange("h b t k -> (b t) h k")
                C.dma(dst, v3(src, nr), [src], [sc_d["smp"]])
            continue
        b2 = BR.two()
        for nh in range(2):
            C.mm(C.ps[:nr, b2 + nh, :], Lt[:nr, :nr], lw[:nr, nh * 512:(nh + 1) * 512], True, True, [Lt, lw], [C.pb[b2 + nh]])
        if _STOP == 71:
            continue
        for nh in range(2):
            sl = slice(nh * 512, (nh + 1) * 512)
            pc = C.ps[:nr, b2 + nh, :]
            pbs = [C.pb[b2 + nh]]
            _SUB = os.environ.get("RWA_SUB", "1234")
            if "1" in _SUB:
                C.actf(t1[:nr, sl], pc, AF.Exp, pbs, [t1])
            if "2" in _SUB:
                C.actf(t2[:nr, sl], pc, AF.Exp, pbs, [t2], scale=-1.0)
            if "3" in _SUB:
                C.tt("dve", t3[:nr, sl], pc, lw[:nr, sl], ALU.subtract, pbs + [lw], [t3])
        if "4" in _SUB:
            C.actf(t3[:nr, :], t3[:nr, :], AF.Exp, [t3], [t3])
        if _STOP == 72:
            continue
        C.tt("dve", ob[0][:nr, :], r_[:nr, :], t1[:nr, :], ALU.mult, [r_, t1], [ob[0]])
        C.tt("pool", ob[1][:nr, :], k_[:nr, :], t2[:nr, :], ALU.mult, [k_, t2], [ob[1]])
        C.stt(ob[2][:nr, :], kk[:nr, :], -1.0, t3[:nr, :], ALU.mult, ALU.mult, [kk, t3], [ob[2]])
        C.tt("pool", ob[3][:nr, :], a_[:nr, :], t2[:nr, :], ALU.mult, [a_, t2], [ob[3]])
        C.copy("act", ob[4][:nr, :], v_[:nr, :], [v_], [ob[4]])
        if _STOP == 73:
            continue
        for i, nm in enumerate(("Rt", "Kt", "At", "Bt", "Vb")):
            C.dma(sc_d[nm][r0:r0 + nr, :], ob[i][:nr, :], [ob[i]], [sc_d[nm]], q="pool")
        if _STOP == 8:
            continue
        b1 = BR.one()
        for hp in range(8):
            C.mm(C.ps[:, b1, 2 * hp:2 * hp + 2], lw[:nr, hp * 128:(hp + 1) * 128], onec[:nr, :], True, True, [lw, onec], [C.pb[b1]])
        C.actf(wc[:, :], C.ps[:, b1, 0:16].rearrange("p (h a) -> p h a", a=2)[:, :, 0], AF.Exp, [C.pb[b1]], [wc])
        C.dma(sc_d["wc"][blk], wc[:, :], [wc], [sc_d["wc"]], q="pool")
    C.stage_end()


def rw_stage_b(C, sc_d, y_d, wkvp_out, nchunks=16):
    C.stage_begin()
    identf, identb, iotaf = load_consts(C)
    BR = BankRot()
    M2 = C.sb("M2", [128, 2, 128])
    C.dma(M2[:, 0, :], C.cst["lstrict"], [], [M2]); C.dma(M2[:, 1, :], C.cst["lincl"], [], [M2])
    mlow = C.sb("mlow", [128, 128]); C.dma(mlow[:], C.cst["llow"], [], [mlow])
    pm = C.sb("pm", [128, 2]); C.dma(pm[:], C.cst["pm"], [], [pm])
    tok = [[C.sb(f"tok{s}_{i}", [128, D], BF16) for i in range(5)] for s in range(2)]
    KtT = [C.sb(f"KtT{s}", [128, 8, 128], BF16) for s in range(2)]
    BtT = [C.sb(f"BtT{s}", [128, 8, 128], BF16) for s in range(2)]
    ARm = [[C.sb(f"ARm{s}_{h2}", [128, 8, 2, 128], BF16) for h2 in range(2)] for s in range(2)]
    NA = [C.sb(f"NA{s}", [128, 16, 2, 2, 128], BF16) for s in range(2)]
    AA = C.sb("AA", [128, 16, 128], BF16)
    Np = [C.sb(f"Np{i}", [128, 16, 128], BF16) for i in range(2)]
    Ap = [C.sb(f"Ap{i}", [128, 16, 128], BF16) for i in range(2)]
    T32 = C.sb("T32", [128, 16, 128]); Tb = [C.sb(f"Tb{s}", [128, 16, 128], BF16) for s in range(2)]
    Xb = C.sb("Xb", [128, 16, 64], BF16); Ub = C.sb("Ub", [128, 16, 64], BF16)
    P32 = C.sb("P32", [128, 8, 64]); Pb = [C.sb(f"Pb{i}", [128, 8, 64], BF16) for i in range(2)]
    Yo = [C.sb(f"Yo{i}", [128, D]) for i in range(2)]
    wcs = [C.sb(f"wcs{i}", [128, 8]) for i in range(2)]
    C.memset("dve", P32[:], 0.0, [P32]); C.memset("dve", Pb[0][:], 0.0, [Pb[0]])

    def hv(bank2, h):
        return C.ps[:, bank2 + h // 8, (h % 8) * 64:(h % 8 + 1) * 64], C.pb[bank2 + h // 8]

    for c in range(nchunks):
        s = c % 2
        r0 = c * 128
        Rtk, Ktk, Atk, Btk, Vtk = tok[s]
        for i, nm in enumerate(("Rt", "Kt", "At", "Bt", "Vb")):
            C.dma(tok[s][i][:], sc_d[nm][r0:r0 + 128, :], [sc_d[nm]], [tok[s][i]])
        C.dma(wcs[s][:], sc_d["wc"][c], [sc_d["wc"]], [wcs[s]])
        for src, kind in ((Ktk, "K"), (Btk, "B"), (Atk, 0), (Rtk, 1)):
            bank = BR.one()
            pst = C.ps[:, bank, :].bitcast(BF16)
            for hp in range(8):
                C.tr(pst[:, hp * 128:(hp + 1) * 128], src[:, hp * 128:(hp + 1) * 128], identb[:], [src, identb], [C.pb[bank]])
            pv = pst.rearrange("p (a t) -> p a t", a=8)
            if kind == "K":
                C.copy("act", KtT[s][:], pv, [C.pb[bank]], [KtT[s]])
            elif kind == "B":
                C.copy("act", BtT[s][:], pv, [C.pb[bank]], [BtT[s]])
            else:
                for h2 in range(2):
                    C.ts("dve", ARm[s][h2][:, :, kind, :], pv, pm[:, h2:h2 + 1], None, ALU.mult, None, [C.pb[bank], pm], [ARm[s][h2]])
        for h in range(16):
            hp, h2 = h // 2, h % 2
            bank = BR.one()
            rhs2 = ARm[s][h2][:, hp, :, :].rearrange("p a t -> p (a t)")
            C.mm(C.ps[:, bank, 0:256], BtT[s][:, hp, :], rhs2, True, True, [BtT[s], ARm[s][h2]], [C.pb[bank]])
            C.mm(C.ps[:, bank, 256:512], KtT[s][:, hp, :], rhs2, True, True, [KtT[s], ARm[s][h2]], [C.pb[bank]])
            C.tt("dve", NA[s][:, h], C.ps[:, bank, :].rearrange("p (q a t) -> p q a t", q=2, a=2),
                 M2[:].unsqueeze(1).to_broadcast([128, 2, 2, 128]), ALU.mult, [C.pb[bank], M2], [NA[s]])
        for g in range(4):
            bank = BR.one()
            for j in range(4):
                h = 4 * g + j
                hp, h2 = h // 2, h % 2
                C.mm(C.ps[:, bank, j * 128:(j + 1) * 128], ARm[s][h2][:, hp, 0, :], BtT[s][:, hp, :], True, True, [ARm[s][h2], BtT[s]], [C.pb[bank]])
            C.tt("dve", AA[:, 4 * g:4 * g + 4, :], C.ps[:, bank, :].rearrange("p (j t) -> p j t", j=4),
                 mlow[:].unsqueeze(1).to_broadcast([128, 4, 128]), ALU.mult, [C.pb[bank], mlow], [AA])
        N1 = NA[s][:, :, 0, 0, :]
        C.tt("dve", T32[:], N1, identf[:].unsqueeze(1).to_broadcast([128, 16, 128]), ALU.add, [NA[s], identf], [T32])
        C.copy("act", Tb[s][:], T32[:], [T32], [Tb[s]])
        Ncur, Nt, Acur, At_ = N1, NA[s], AA[:], AA
        for i in range(1, 7):
            last = i == 6
            Nn, An = Np[i % 2], Ap[i % 2]
            for g in range(4):
                bankA = BR.one()
                bankN = None if last else BR.one()
                for j in range(4):
                    h = 4 * g + j
                    if not last:
                        C.mm(C.ps[:, bankN, j * 128:(j + 1) * 128], Acur[:, h, :], Ncur[:, h, :], True, True, [At_, Nt], [C.pb[bankN]])
                    C.mm(C.ps[:, bankA, j * 128:(j + 1) * 128], Ncur[:, h, :], Acur[:, h, :], True, True, [At_, Nt], [C.pb[bankA]])
                if not last:
                    C.copy("act", Nn[:, 4 * g:4 * g + 4, :], C.ps[:, bankN, :].rearrange("p (j t) -> p j t", j=4), [C.pb[bankN]], [Nn])
                C.copy("dve", An[:, 4 * g:4 * g + 4, :], C.ps[:, bankA, :].rearrange("p (j t) -> p j t", j=4), [C.pb[bankA]], [An])
            for g in range(4):
                bankT = BR.one()
                for j in range(4):
                    h = 4 * g + j
                    C.mm(C.ps[:, bankT, j * 128:(j + 1) * 128], An[:, h, :], Tb[s][:, h, :], True, True, [An, Tb[s]], [C.pb[bankT]])
                C.tt("dve", T32[:, 4 * g:4 * g + 4, :], T32[:, 4 * g:4 * g + 4, :], C.ps[:, bankT, :].rearrange("p (j t) -> p j t", j=4), ALU.add, [T32, C.pb[bankT]], [T32])
            C.copy("act", Tb[s][:], T32[:], [T32], [Tb[s]])
            Ncur, Nt, Acur, At_ = Nn[:], Nn, An[:], An
        cur, nxt = Pb[c % 2], Pb[(c + 1) % 2]
        b2 = BR.two()
        for h in range(16):
            hp, h2 = h // 2, h % 2
            o, ob_ = hv(b2, h)
            C.mm(o, ARm[s][h2][:, hp, 0, :], cur[:, hp, :], True, False, [ARm[s][h2], cur], [ob_])
            C.mm(o, NA[s][:, h, 1, 0, :], Vtk[:, h * 64:(h + 1) * 64], False, True, [NA[s], Vtk], [ob_])
        for a in range(2):
            C.copy("act" if a else "dve", Xb[:, 8 * a:8 * a + 8, :], C.ps[:, b2 + a, :].rearrange("p (h v) -> p h v", h=8), [C.pb[b2 + a]], [Xb])
        b2 = BR.two()
        for h in range(16):
            o, ob_ = hv(b2, h)
            C.mm(o, Tb[s][:, h, :], Xb[:, h, :], True, True, [Tb[s], Xb], [ob_])
        for a in range(2):
            C.copy("act" if a else "dve", Ub[:, 8 * a:8 * a + 8, :], C.ps[:, b2 + a, :].rearrange("p (h v) -> p h v", h=8), [C.pb[b2 + a]], [Ub])
        b2 = BR.two()
        for h in range(16):
            hp, h2 = h // 2, h % 2
            o, ob_ = hv(b2, h)
            C.mm(o, ARm[s][h2][:, hp, 1, :], cur[:, hp, :], True, False, [ARm[s][h2], cur], [ob_])
            C.mm(o, NA[s][:, h, 0, 1, :], Ub[:, h, :], False, False, [NA[s], Ub], [ob_])
            C.mm(o, NA[s][:, h, 1, 1, :], Vtk[:, h * 64:(h + 1) * 64], False, True, [NA[s], Vtk], [ob_])
        YO = Yo[c % 2]
        for a in range(2):
            C.copy("act", YO[:, 512 * a:512 * a + 512], C.ps[:, b2 + a, :], [C.pb[b2 + a]], [YO])
        C.dma(y_d[r0:r0 + 128, :], YO[:], [YO], [y_d], q="act")
        b2 = BR.two()
        for hp in range(8):
            o = C.ps[:, b2 + hp // 4, (hp % 4) * 128:(hp % 4 + 1) * 128]
            ob_ = C.pb[b2 + hp // 4]
            C.mm(o, Btk[:, hp * 128:(hp + 1) * 128], Ub[:, 2 * hp:2 * hp + 2, :].rearrange("p a v -> p (a v)"), True, False, [Btk, Ub], [ob_])
            C.mm(o, Ktk[:, hp * 128:(hp + 1) * 128], Vtk[:, hp * 128:(hp + 1) * 128], False, True, [Ktk, Vtk], [ob_])
        for h2 in range(2):
            p0 = 64 * h2
            psv = C.ps[p0:p0 + 64, b2:b2 + 2, :].rearrange("p a (q h v) -> p (a q) h v", q=4, h=2)[:, :, h2, :]
            C.tt("dve", P32[p0:p0 + 64, :, :], P32[p0:p0 + 64, :, :], psv, ALU.add, [P32, C.pb[b2], C.pb[b2 + 1]], [P32])
        C.tt("dve", P32[:], P32[:], wcs[s][:, :].unsqueeze(2).to_broadcast([128, 8, 64]), ALU.mult, [P32, wcs[s]], [P32])
        C.copy("act", nxt[:], P32[:], [P32], [nxt])
    b2 = BR.two()
    for hp in range(8):
        C.tr(C.ps[:64, b2 + hp // 4, (hp % 4) * 128:(hp % 4 + 1) * 128], P32[:, hp, :], identf[:], [P32, identf], [C.pb[b2 + hp // 4]])
    so = C.sb("so", [64, D])
    for a in range(2):
        C.copy("act", so[:, 512 * a:512 * a + 512], C.ps[:64, b2 + a, :], [C.pb[b2 + a]], [so])
    C.dma(wkvp_out.t.rearrange("h v k -> v h k"), so[:, :].rearrange("p (h k) -> p h k", h=16), [so], [wkvp_out])
    C.stage_end()


def rw_stage_s(C, sc_d, wkv_in, y_d, wkvs_out):
    C.stage_begin()
    S = [C.sb(f"S{g}", [128, 64, 64]) for g in range(2)]
    q = [[C.sb(f"q{g}_{i}", [128, 4, 64]) for i in range(6)] for g in range(2)]
    tmp = C.sb("tmp", [128, 64, 64]); sa = C.sb("sa", [128, 64]); yy = [C.sb(f"yy{g}", [128, 4, 64]) for g in range(2)]
    for g in range(2):
        for hl in range(8):
            h = 8 * g + hl
            C.dma(S[g][hl * 16:(hl + 1) * 16, :, :], wkv_in[:, h, :, :], [], [S[g]])
        for i in range(6):
            C.dma(q[g][i][:], sc_d["smp"].t[i, 8 * g:8 * g + 8].rearrange("h b t k -> (h b) t k"), [sc_d["smp"]], [q[g][i]])
    for g in range(2):
        r_, w_, k_, v_, al, be = q[g]
        St = S[g]
        for t in range(4):
            def bk(x):
                return x[:, t, :].unsqueeze(1).to_broadcast([128, 64, 64])

            def bv(x):
                return x.unsqueeze(2).to_broadcast([128, 64, 64])
            C.tt("dve", tmp[:], St[:], bk(al), ALU.mult, [St, al], [tmp])
            C.red(sa[:], tmp[:], ALU.add, [tmp], [sa])
            C.tt("dve", St[:], St[:], bk(w_), ALU.mult, [St, w_], [St])
            C.tt("dve", tmp[:], bv(sa[:, :]), bk(be), ALU.mult, [sa, be], [tmp])
            C.tt("dve", St[:], St[:], tmp[:], ALU.add, [St, tmp], [St])
            C.tt("dve", tmp[:], bv(v_[:, t, :]), bk(k_), ALU.mult, [v_, k_], [tmp])
            C.tt("dve", St[:], St[:], tmp[:], ALU.add, [St, tmp], [St])
            C.tt("dve", tmp[:], St[:], bk(r_), ALU.mult, [St, r_], [tmp])
            C.red(yy[g][:, t, :], tmp[:], ALU.add, [tmp], [yy[g]])
        for hl in range(8):
            h = 8 * g + hl
            C.dma(wkvs_out[:, h, :, :], St[hl * 16:(hl + 1) * 16, :, :], [St], [wkvs_out])
            dst = y_d.t[NP_:NT, h * 64:(h + 1) * 64].rearrange("(b t) v -> b t v", t=4)
            C.dma(dst, yy[g][hl * 16:(hl + 1) * 16, :, :], [yy[g]], [y_d])
    C.stage_end()


def rw_stage_c(C, W, sc_d, y_d, h_out):
    C.stage_begin()
    identf, identb, iotaf = load_consts(C)
    BR = BankRot()
    wo = C.sb("wo", [128, 8, D], BF16)
    src = W["rw_w_o"][0].rearrange("(k p) n -> p k n", p=128)
    for i in range(2):
        C.dma(wo[:, :, i * 512:(i + 1) * 512], src[:, :, i * 512:(i + 1) * 512], [], [wo], q="pool")
    gg = C.sb("gng", [128, D]); gb = C.sb("gnb", [128, D])
    C.dma(gg[:], W["rw_gn_g"][0:1, :].partition_broadcast(128), [], [gg])
    C.dma(gb[:], W["rw_gn_b"][0:1, :].partition_broadcast(128), [], [gb])
    eps = C.sb("epsg", [128, 1]); C.memset("dve", eps[:], 64e-5, [eps])
    NB = 2
    y = [C.sb(f"y{i}", [128, D]) for i in range(NB)]; v = [C.sb(f"v{i}", [128, D]) for i in range(NB)]; g = [C.sb(f"g{i}", [128, D]) for i in range(NB)]
    bon = [C.sb(f"bon{i}", [128, 16]) for i in range(NB)]
    sq = C.sb("sq", [128, D]); mu = C.sb("mu", [128, 16]); var = C.sb("var", [128, 16])
    zb = [C.sb(f"zb{i}", [128, D], BF16) for i in range(NB)]; zT = [C.sb(f"zT{i}", [128, 8, 128], BF16) for i in range(NB)]
    ho = [C.sb(f"ho{i}", [128, D]) for i in range(NB)]

    def v3(t, nr):
        return t[:nr, :].rearrange("p (h k) -> p h k", h=16)

    def b16(t, nr):
        return t[:nr, :].unsqueeze(2).to_broadcast([nr, 16, 64])

    for blk in range(NBLK):
        r0, nr = blk_rows(blk)
        s = blk % NB
        Y, V, G, BON, ZB, ZT, HO = y[s], v[s], g[s], bon[s], zb[s], zT[s], ho[s]
        C.dma(Y[:nr, :], y_d[r0:r0 + nr, :], [y_d], [Y])
        C.dma(V[:nr, :], sc_d["v32"][r0:r0 + nr, :], [sc_d["v32"]], [V])
        C.dma(G[:nr, :], sc_d["g32"][r0:r0 + nr, :], [sc_d["g32"]], [G])
        C.dma(BON[:nr, :], sc_d["bon"][r0:r0 + nr, :], [sc_d["bon"]], [BON])
        C.red(mu[:nr, :], v3(Y, nr), ALU.add, [Y], [mu])
        C.ts("dve", mu[:nr, :], mu[:nr, :], 1.0 / 64, None, ALU.mult, None, [mu], [mu])
        C.tt("dve", v3(Y, nr), v3(Y, nr), b16(mu, nr), ALU.subtract, [Y, mu], [Y])
        C.tt("pool", sq[:nr, :], Y[:nr, :], Y[:nr, :], ALU.mult, [Y], [sq])
        C.red(var[:nr, :], v3(sq, nr), ALU.add, [sq], [var])
        C.actf(var[:nr, :], var[:nr, :], AF.Sqrt, [var], [var], bias=eps[:nr, 0:1], scale=1.0 / 64)
        C.recip(var[:nr, :], var[:nr, :], [var], [var])
        C.tt("dve", v3(Y, nr), v3(Y, nr), b16(var, nr), ALU.mult, [Y, var], [Y])
        C.tt("dve", Y[:nr, :], Y[:nr, :], gg[:nr, :], ALU.mult, [Y, gg], [Y])
        C.tt("pool", Y[:nr, :], Y[:nr, :], gb[:nr, :], ALU.add, [Y, gb], [Y])
        C.tt("dve", v3(V, nr), v3(V, nr), b16(BON, nr), ALU.mult, [V, BON], [V])
        C.tt("pool", Y[:nr, :], Y[:nr, :], V[:nr, :], ALU.add, [Y, V], [Y])
        C.tt("dve", ZB[:nr, :], Y[:nr, :], G[:nr, :], ALU.mult, [Y, G], [ZB])
        to_featmajor(C, nr, ZB, ZT, 0, identb, BR.one())
        b2 = BR.two()
        for nh in range(2):
            for kc in range(8):
                C.mm(C.ps[:nr, b2 + nh, :], ZT[:, kc, :nr], wo[:, kc, nh * 512:(nh + 1) * 512], kc == 0, kc == 7, [ZT, wo], [C.pb[b2 + nh]])
            C.copy("act", HO[:nr, nh * 512:(nh + 1) * 512], C.ps[:nr, b2 + nh, :], [C.pb[b2 + nh]], [HO])
        C.dma(h_out[r0:r0 + nr, :], HO[:nr, :], [HO], [h_out], q="act")
    C.stage_end()


def rw_scratch(C, kind="Internal"):
    sc = {}
    for nm in ("Rt", "Kt", "At", "Bt", "Vb"):
        sc[nm] = C.dram("rw_" + nm, [NT, D], BF16, kind=kind)
    sc["v32"] = C.dram("rw_v32", [NT, D], F32, kind=kind)
    sc["g32"] = C.dram("rw_g32", [NT, D], F32, kind=kind)
    sc["bon"] = C.dram("rw_bon", [NT, 16], F32, kind=kind)
    sc["wc"] = C.dram("rw_wc", [16, 128, 8], F32, kind=kind)
    sc["smp"] = C.dram("rw_smp", [6, 16, 16, 4, 64], F32, kind=kind)
    return sc


LAM_INIT = 0.8 - 0.6 * float(np.exp(-0.3 * 1))


def kvq_stage(C, W, x1T_d, k_out, v_out, kT_d, qT_d, vb_d):
    C.stage_begin()
    BR = BankRot()
    wts = {}
    for nm, src in (("wk", W["da_w_k"]), ("wv", W["da_w_v"]), ("wq", W["da_w_q"][0])):
        t = C.sb(nm, [128, 8, D], BF16)
        s_ = src.rearrange("(k p) n -> p k n", p=128)
        for i in range(2):
            C.dma(t[:, :, i * 512:(i + 1) * 512], s_[:, :, i * 512:(i + 1) * 512], [], [t], q="pool")
        wts[nm] = t
    NB = 2
    xT = [C.sb(f"xT{i}", [128, 8, 128], BF16) for i in range(NB)]
    ko = [C.sb(f"ko{i}", [128, D]) for i in range(NB)]; vo = [C.sb(f"vo{i}", [128, D]) for i in range(NB)]
    vb = [C.sb(f"vb{i}", [128, D], BF16) for i in range(NB)]
    kT = [C.sb(f"kT{i}", [128, 8, 128], BF16) for i in range(NB)]; qT = [C.sb(f"qT{i}", [128, 8, 128], BF16) for i in range(NB)]
    for blk in range(NBLK):
        r0, nr = blk_rows(blk)
        s = blk % NB
        XT = xT[s]
        C.dma(XT[:, :, :nr], x1T_d[:, :, r0:r0 + nr], [x1T_d], [XT])
        for wn, dst, outd in (("wk", ko[s], k_out), ("wv", vo[s], v_out)):
            b2 = BR.two()
            for nh in range(2):
                for kc in range(8):
                    C.mm(C.ps[:nr, b2 + nh, :], XT[:, kc, :nr], wts[wn][:, kc, nh * 512:(nh + 1) * 512], kc == 0, kc == 7, [XT, wts[wn]], [C.pb[b2 + nh]])
                C.copy("act" if nh else "dve", dst[:nr, nh * 512:(nh + 1) * 512], C.ps[:nr, b2 + nh, :], [C.pb[b2 + nh]], [dst])
            C.dma(outd[r0:r0 + nr, :], dst[:nr, :], [dst], [outd], q="act")
        C.copy("act", vb[s][:nr, :], vo[s][:nr, :], [vo[s]], [vb[s]])
        C.dma(vb_d[r0:r0 + nr, :], vb[s][:nr, :], [vb[s]], [vb_d], q="act")
        for wn, dst, outd, scl in (("wk", kT[s], kT_d, 1.0), ("wq", qT[s], qT_d, 0.125)):
            for hg in range(2):
                bank = BR.one()
                for j in range(4):
                    h = 4 * hg + j
                    for kc in range(8):
                        C.mm(C.ps[:, bank, j * 128:j * 128 + nr], wts[wn][:, kc, h * 128:(h + 1) * 128], XT[:, kc, :nr], kc == 0, kc == 7, [XT, wts[wn]], [C.pb[bank]])
                C.actf(dst[:, 4 * hg:4 * hg + 4, :nr], C.ps[:, bank, :].rearrange("p (j t) -> p j t", j=4)[:, :, :nr], AF.Identity, [C.pb[bank]], [dst], scale=scl)
            C.dma(outd[:, :, r0:r0 + nr], dst[:, :, :nr], [dst], [outd], q="act")
    C.stage_end()


def lam_compute(C, W):
    lv = C.sb("lv", [128, 256]); pr = C.sb("lpr", [128, 2, 64]); s2 = C.sb("ls2", [128, 2]); lam = C.sb("lam", [128, 1])
    C.dma(lv[:], W["da_lam"][0].rearrange("a b -> (a b)").partition_broadcast(128), [], [lv])
    l4 = lv[:, :].rearrange("p (a b) -> p a b", a=4)
    C.tt("dve", pr[:, 0, :], l4[:, 0, :], l4[:, 1, :], ALU.mult, [lv], [pr])
    C.tt("dve", pr[:, 1, :], l4[:, 2, :], l4[:, 3, :], ALU.mult, [lv], [pr])
    C.red(s2[:, :], pr[:, :, :], ALU.add, [pr], [s2])
    C.actf(s2[:, :], s2[:, :], AF.Exp, [s2], [s2])
    C.tt("dve", lam[:, :], s2[:, 0:1], s2[:, 1:2], ALU.subtract, [s2], [lam])
    C.ts("dve", lam[:, :], lam[:, :], LAM_INIT, None, ALU.add, None, [lam], [lam])
    return lam


def attn_finish(C, W, nr, O, scr, identb, wo, sg, h_out, r0, BR):
    sq, ss, ob, oT, ho, eps = scr["sq"], scr["ss"], scr["ob"], scr["oT"], scr["ho"], scr["eps"]
    o3 = O[:nr, :].rearrange("p (h v) -> p h v", h=8)
    C.tt("pool", sq[:nr, :], O[:nr, :], O[:nr, :], ALU.mult, [O], [sq])
    C.red(ss[:nr, :], sq[:nr, :].rearrange("p (h v) -> p h v", h=8), ALU.add, [sq], [ss])
    C.actf(ss[:nr, :], ss[:nr, :], AF.Sqrt, [ss], [ss], bias=eps[:nr, 0:1], scale=1.0 / 128)
    C.recip(ss[:nr, :], ss[:nr, :], [ss], [ss])
    C.tt("dve", o3, o3, ss[:nr, :].unsqueeze(2).to_broadcast([nr, 8, 128]), ALU.mult, [O, ss], [O])
    C.tt("dve", ob[:nr, :].rearrange("p (h v) -> p h v", h=8), o3, sg[:nr, :].unsqueeze(1).to_broadcast([nr, 8, 128]), ALU.mult, [O, sg], [ob])
    to_featmajor(C, nr, ob, oT, 0, identb, BR.one())
    bks = BR.pair()
    for nh in range(2):
        bk = bks[nh]
        for h in range(8):
            C.mm(C.ps[:nr, bk, :], oT[:, h, :nr], wo[:, h, nh * 512:(nh + 1) * 512], h == 0, h == 7, [oT, wo], [C.pb[bk]])
        C.copy("act", ho[:nr, nh * 512:(nh + 1) * 512], C.ps[:nr, bk, :], [C.pb[bk]], [ho])
    C.dma(h_out[r0:r0 + nr, :], ho[:nr, :], [ho], [h_out], q="act")


def attn_common(C, W):
    identf, identb, iotaf = load_consts(C)
    wo = C.sb("wo_a", [128, 8, D], BF16)
    src = W["da_w_o"][0].rearrange("(h p) n -> p h n", p=128)
    for i in range(2):
        C.dma(wo[:, :, i * 512:(i + 1) * 512], src[:, :, i * 512:(i + 1) * 512], [], [wo], q="pool")
    sg = C.sb("sg_a", [128, 128])
    C.dma(sg[:], W["da_subln_g"][0:1, :].partition_broadcast(128), [], [sg])
    C.ts("dve", sg[:], sg[:], 1.0 - LAM_INIT, None, ALU.mult, None, [sg], [sg])
    eps = C.sb("eps_a", [128, 1]); C.memset("dve", eps[:], 1e-5, [eps])
    scr = {"sq": C.sb("sq_a", [128, D]), "ss": C.sb("ss_a", [128, 8]), "ob": C.sb("ob_a", [128, D], BF16),
           "oT": C.sb("oT_a", [128, 8, 128], BF16), "ho": C.sb("ho_a", [128, D]), "eps": eps}
    lam = lam_compute(C, W)
    return identf, identb, wo, sg, scr, lam


def attn_prompt_stage(C, W, kT_d, qT_d, vb_d, h_out, nqb=16):
    C.stage_begin()
    identf, identb, wo, sg, scr, lam = attn_common(C, W)
    BR = BankRot()
    kT = C.sb("kTa", [128, 8, NP_], BF16); qT = C.sb("qTa", [128, 8, NP_], BF16)
    for i in range(4):
        C.dma(kT[:, :, i * 512:(i + 1) * 512], kT_d[:, :, i * 512:(i + 1) * 512], [kT_d], [kT])
        C.dma(qT[:, :, i * 512:(i + 1) * 512], qT_d[:, :, i * 512:(i + 1) * 512], [qT_d], [qT])
    vA = C.sb("vA", [128, 16, 8, 129], BF16)
    C.memset("pool", vA[:, :, :, 128:129], 1.0, [vA])
    for kb in range(16):
        C.dma(vA[:, kb, :, 0:128], vb_d[kb * 128:(kb + 1) * 128, :].rearrange("p (h v) -> p h v", h=8), [vb_d], [vA])
    btab = C.sb("btab", [128, 8, 16]); C.dma(btab[:], C.cst["btab"], [], [btab])
    msk = C.sb("msk", [128, 128], BF16); mskf = C.sb("mskf", [128, 128])
    C.dma(mskf[:], C.cst["lincl"], [], [mskf]); C.copy("dve", msk[:], mskf[:], [mskf], [msk])
    PT = [C.sb(f"PT{i}", [128, 2, 128], BF16) for i in range(3)]
    pm = C.sb("pm_a", [128, 2]); C.dma(pm[:], C.cst["pm"], [], [pm])
    qz = [C.sb(f"qz{i}", [128, 8, 2, 128], BF16) for i in range(2)]
    O = [C.sb(f"O{i}", [128, D]) for i in range(2)]
    rr = C.sb("rr", [128, 2]); tmp = C.sb("tmpo", [128, 128])
    items = [(qb, h, kb) for qb in range(nqb) for h in range(8) for kb in range(qb + 1)]
    accs = {}
    for n_, (qb, h) in enumerate((qb, h) for qb in range(nqb) for h in range(8)):
        accs[(qb, h)] = (0, 1) if n_ % 2 == 0 else (2, 3)

    def emit_score(i):
        qb, h, kb = items[i]
        QZ = qz[qb % 2]
        if h == 0 and kb == 0:
            for c in range(2):
                C.ts("dve" if c else "pool", QZ[:, :, c, :], qT[:, :, qb * 128:(qb + 1) * 128], pm[:, c:c + 1], None, ALU.mult, None, [qT, pm], [QZ])
        bs = 4 + (i % 3)
        C.mm(C.ps[:, bs, 0:256], kT[:, h, kb * 128:(kb + 1) * 128], QZ[:, h, :, :].rearrange("p c q -> p (c q)"), True, True, [kT, QZ], [C.pb[bs]])

    def emit_rest(i):
        qb, h, kb = items[i]
        accb = accs[(qb, h)]
        OO = O[qb % 2]
        bs = 4 + (i % 3)
        P_ = PT[i % 3]
        C.actf(P_[:, :, :], C.ps[:, bs, 0:256].rearrange("p (c q) -> p c q", c=2), AF.Exp, [C.pb[bs], btab], [P_], bias=btab[:, h, qb - kb:qb - kb + 1])
        if kb == qb:
            C.tt("dve", P_[:, :, :], P_[:, :, :], msk[:].unsqueeze(1).to_broadcast([128, 2, 128]), ALU.mult, [P_, msk], [P_])
        for c in range(2):
            C.mm(C.ps[:, accb[c], 0:129], P_[:, c, :], vA[:, kb, h, :], kb == 0, kb == qb, [P_, vA], [C.pb[accb[c]]])
        if kb == qb:
            for c in range(2):
                C.recip(rr[:, c:c + 1], C.ps[:, accb[c], 128:129], [C.pb[accb[c]]], [rr])
            C.tt("dve", rr[:, 1:2], rr[:, 1:2], lam[:, :], ALU.mult, [rr, lam], [rr])
            C.ts("dve", tmp[:], C.ps[:, accb[1], 0:128], rr[:, 1:2], None, ALU.mult, None, [C.pb[accb[1]], rr], [tmp])
            C.stt(OO[:, h * 128:(h + 1) * 128], C.ps[:, accb[0], 0:128], rr[:, 0:1], tmp[:], ALU.mult, ALU.subtract, [C.pb[accb[0]], rr, tmp], [OO])

    pending = []
    emit_score(0)
    for i in range(len(items)):
        if i + 1 < len(items):
            emit_score(i + 1)
        emit_rest(i)
        qb, h, kb = items[i]
        if h == 7 and kb == qb:
            pending.append((i + 5, qb))
        if pending and pending[0][0] <= i:
            _, qf = pending.pop(0)
            attn_finish(C, W, 128, O[qf % 2], scr, identb, wo, sg, h_out, qf * 128, BR2(BR))
    for _, qf in pending:
        attn_finish(C, W, 128, O[qf % 2], scr, identb, wo, sg, h_out, qf * 128, BR2(BR))
    C.stage_end()


class BR2:
    def __init__(self, br=None):
        pass

    def one(self):
        return 7

    def pair(self):
        return (7, 7)


def attn_sample_stage(C, W, pt_ap, ckT_ap, cv_ap, kT_d, qT_d, vb_d, h_out, nseq=16):
    C.stage_begin()
    identf, identb, wo, sg, scr, lam = attn_common(C, W)
    pti = C.sb("pti", [128, 256], I32); ptf = C.sb("ptf", [128, 256]); idxi = C.sb("idxi", [128, 256], I32)
    iop = C.sb("iop", [128, 1]); C.dma(iop[:], C.cst["iotap"], [], [iop])
    C.dma(pti[:], pt_ap.rearrange("a b -> (a b)").partition_broadcast(128), [], [pti])
    C.copy("dve", ptf[:], pti[:], [pti], [ptf])
    C.ts("dve", ptf[:], ptf[:], 128.0, iop[:, 0:1], ALU.mult, ALU.add, [ptf, iop], [ptf])
    C.copy("dve", idxi[:], ptf[:], [ptf], [idxi])
    qTs = C.sb("qTs", [128, 8, NS], BF16); kTs = C.sb("kTs", [128, 8, NS], BF16)
    C.dma(qTs[:], qT_d[:, :, NP_:NT], [qT_d], [qTs]); C.dma(kTs[:], kT_d[:, :, NP_:NT], [kT_d], [kTs])
    qblk = C.sb("qblk", [128, 16, 8, 8], BF16)
    C.memset("pool", qblk[:], 0.0, [qblk])
    for c in range(2):
        C.copy("dve", qblk[c * 64:(c + 1) * 64, :, :, c * 4:(c + 1) * 4], qTs[c * 64:(c + 1) * 64, :, :].rearrange("p h (b q) -> p b h q", q=4), [qTs], [qblk])
    tabS = C.sb("tabS", [128, 16, 64]); C.dma(tabS[:], C.cst["tabS"], [], [tabS])
    tabN = C.sb("tabN", [128, 64]); C.dma(tabN[:], C.cst["tabN"], [], [tabN])
    onesf = C.sb("onesf", [64, 128]); C.memset("dve", onesf[:], 1.0, [onesf])
    cmbW = C.sb("cmbW", [8, 124]); cA = C.sb("cA", [8, 124]); cB = C.sb("cB", [8, 124])
    C.dma(cA[:], C.cst["cmbA"], [], [cA]); C.dma(cB[:], C.cst["cmbB"], [], [cB])
    C.stt(cmbW[:], cB[:], lam[:8, 0:1], cA[:], ALU.mult, ALU.add, [cB, cA, lam], [cmbW])
    stg = [C.sb(f"stg{i}", [128, D]) for i in range(4)]
    kpg = [C.sb(f"kpg{i}", [128, 8, 128], BF16) for i in range(4)]
    vres = [C.sb(f"vres{i}", [128, 17, 8, 129], BF16) for i in range(2)]
    for i in range(2):
        C.memset("pool", vres[i][:, 0:16, :, 128:129], 1.0, [vres[i]])
        C.memset("pool", vres[i][:, 16, :, :], 0.0, [vres[i]])
        C.memset("pool", vres[i][0:32, 16, :, 128:129], 1.0, [vres[i]])
    E = [C.sb(f"E{i}", [128, 17, 64]) for i in range(2)]
    PTs = [C.sb(f"PTs{i}", [128, 17, 64], BF16) for i in range(2)]
    m1 = C.sb("m1", [128, 64]); mcol = C.sb("mcol", [64, 1]); dg = C.sb("dg", [64, 64])
    rrs = C.sb("rrs", [8, 8]); on32 = C.sb("on32", [8, D])
    Os = C.sb("Os", [128, D])
    ng = 0
    for b in range(nseq):
        s = b % 2
        VR, EE, PP = vres[s], E[s], PTs[s]
        for pg in range(16):
            col = b * 16 + pg
            for which in range(2):
                st = stg[ng % 4]
                ng += 1
                src = ckT_ap if which == 0 else cv_ap
                C.P.dma("pool", (lambda st, src, col: (lambda e: e.indirect_dma_start(out=st[:], out_offset=None, in_=src,
                        in_offset=bass.IndirectOffsetOnAxis(ap=idxi[:, col:col + 1], axis=0))))(st, src, col), [_b(idxi)], [_b(st)])
                if which == 0:
                    KP = kpg[pg % 4]
                    C.copy("act", KP[:], st[:, :].rearrange("p (h k) -> p h k", h=8), [st], [KP])
                    bank = pg % 2
                    for h in range(8):
                        C.mm(C.ps[:, bank, h * 8:(h + 1) * 8], KP[:, h, :], qblk[:, b, h, :], True, True, [KP, qblk], [C.pb[bank]])
                    C.tt("dve", EE[:, pg, :], C.ps[:, bank, 0:64], tabS[:, pg, :], ALU.add, [C.pb[bank], tabS], [EE])
                else:
                    C.copy("dve", VR[:, pg, :, 0:128], st[:, :].rearrange("p (h v) -> p h v", h=8), [st], [VR])
        C.dma(VR[0:4, 16, :, 0:128], vb_d[NP_ + 4 * b:NP_ + 4 * b + 4, :].rearrange("p (h v) -> p h v", h=8), [vb_d], [VR])
        C.copy("dve", EE[:, 16, :], tabN[:], [tabN], [EE])
        for h in range(8):
            C.mm(C.ps[:4, 2, h * 8:(h + 1) * 8], kTs[:, h, 4 * b:4 * b + 4], qblk[:, b, h, :], True, True, [kTs, qblk], [C.pb[2]])
        C.tt("dve", EE[0:4, 16, :], C.ps[0:4, 2, 0:64], tabN[0:4, :], ALU.add, [C.pb[2], tabN], [EE])
        C.red(m1[:], EE[:].rearrange("p k j -> p j k"), ALU.max, [EE], [m1])
        C.tr(C.ps[:64, 2, 0:128], m1[:], identf[:], [m1, identf], [C.pb[2]])
        C.red(mcol[:], C.ps[:64, 2, 0:128], ALU.max, [C.pb[2]], [mcol])
        C.ts("dve", dg[:], identf[:64, :64], mcol[:, 0:1], None, ALU.mult, None, [identf, mcol], [dg])
        C.mm(C.ps[:, 2, 128:192], onesf[:], dg[:], True, True, [onesf, dg], [C.pb[2]])
        C.tt("dve", EE[:], EE[:], C.ps[:, 2, 128:192].unsqueeze(1).to_broadcast([128, 17, 64]), ALU.subtract, [EE, C.pb[2]], [EE])
        C.actf(PP[:], EE[:], AF.Exp, [EE], [PP])
        for h in range(8):
            bank = 3 + h // 3
            o = C.ps[:8, bank, (h % 3) * 129:(h % 3) * 129 + 129]
            for kb in range(17):
                C.mm(o, PP[:, kb, h * 8:(h + 1) * 8], VR[:, kb, h, :], kb == 0, kb == 16, [PP, VR], [C.pb[bank]])
        for g3 in range(3):
            nh_ = 3 if g3 < 2 else 2
            pv = C.ps[:8, 3 + g3, 0:nh_ * 129].rearrange("p (h v) -> p h v", v=129)
            C.recip(rrs[:, 3 * g3:3 * g3 + nh_], pv[:, :, 128], [C.pb[3 + g3]], [rrs])
            C.tt("dve", on32[:, 3 * g3 * 128:(3 * g3 + nh_) * 128].rearrange("p (h v) -> p h v", v=128), pv[:, :, 0:128],
                 rrs[:, 3 * g3:3 * g3 + nh_].unsqueeze(2).to_broadcast([8, nh_, 128]), ALU.mult, [C.pb[3 + g3], rrs], [on32])
        for nh in range(2):
            C.mm(C.ps[:64, 6 + nh, :], cmbW[:, 60 - 4 * b:124 - 4 * b], on32[:, nh * 512:(nh + 1) * 512], b == 0, b == nseq - 1, [cmbW, on32], [C.pb[6 + nh]])
    for nh in range(2):
        C.copy("act", Os[:64, nh * 512:(nh + 1) * 512], C.ps[:64, 6 + nh, :], [C.pb[6 + nh]], [Os])
    attn_finish(C, W, 64, Os, scr, identb, wo, sg, h_out, NP_, BankRot())
    C.stage_end()


W_SHAPES = None


def build_program(wshapes, cshapes, n_pool_rows, dev=False, scopes=False):
    nc = bass.Bass("TRN2", target_bir_lowering=False)
    es = ExitStack()
    C = Ctx(nc, es)
    C.scopes = scopes

    def ext(name, shape, dt=F32):
        return nc.dram_tensor(name, list(shape), dt, kind="ExternalInput").ap()

    W = {k: ext(k, s) for k, s in wshapes.items()}
    C.cst = {k: ext("cst_" + k, s) for k, s in cshapes.items()}
    x0 = Tile(ext("x0", [NT, D])); x0T = Tile(ext("x0T", [D, NT])); xpT = Tile(ext("xpT", [D, NT]))
    wkv_in = ext("wkv_in", [16, 16, 64, 64])
    pT = [ext("pT0", [256, NT]), ext("pT1", [256, NT])]
    pt = ext("ptab", [16, 16], I32)
    ckT = ext("ckT", [n_pool_rows, D]); cv = ext("cv", [n_pool_rows, D])
    kind = "ExternalOutput" if dev else "Internal"
    out = lambda name, shape: C.dram(name, shape, F32, kind="ExternalOutput")
    y = out("y", [NT, D]); shp = out("shift_p", [1, D]); wkvp = out("wkv_p", [16, 64, 64]); k_o = out("k_o", [NT, D]); v_o = out("v_o", [NT, D])
    shs = out("shift_s", [16, D]); wkvs = out("wkv_s", [16, 16, 64, 64])
    sc = rw_scratch(C)
    y_d = C.dram("y_d", [NT, D], F32, kind=kind); h0 = C.dram("h0_d", [NT, D], F32, kind=kind)
    xa = C.dram("xa_d", [NT, D], F32); xaT = C.dram("xaT_d", [128, 8, NT], BF16); idx = C.dram("idx_d", [128, 3, NT], F32)
    c_d = C.dram("c_d", [NT, D], F32, kind=kind); x1 = C.dram("x1_d", [NT, D], F32, kind=kind); x1T = C.dram("x1T_d", [128, 8, NT], BF16)
    kT = C.dram("kT_d", [128, 8, NT], BF16); qT = C.dram("qT_d", [128, 8, NT], BF16); vb = C.dram("vb_d", [NT, D], BF16)
    h1 = C.dram("h1_d", [NT, D], F32, kind=kind)
    C.stage_begin()
    C.dma(shp[:, :], x0[NP_ - 1:NP_, :], [], [shp])
    C.dma(shs[:, :], x0.t[NP_:NT, :].rearrange("(b t) d -> b t d", t=4)[:, 3, :], [], [shs])
    C.stage_end()
    rw_stage_a(C, W, x0T, xpT, sc)
    rw_stage_b(C, sc, y_d, wkvp)
    rw_stage_s(C, sc, wkv_in, y_d, wkvs)
    rw_stage_c(C, W, sc, y_d, h0)
    ffn_stage_a(C, 0, W, x0, h0, xa, xaT, idx)
    ffn_stage_b(C, 0, W, xaT, idx, c_d)
    ffn_stage_c(C, 0, W, xa, c_d, pT[0], x1, x1T)
    kvq_stage(C, W, x1T, k_o, v_o, kT, qT, vb)
    attn_prompt_stage(C, W, kT, qT, vb, h1)
    attn_sample_stage(C, W, pt, ckT, cv, kT, qT, vb, h1)
    ffn_stage_a(C, 1, W, x1, h1, xa, xaT, idx)
    ffn_stage_b(C, 1, W, xaT, idx, c_d)
    ffn_stage_c(C, 1, W, xa, c_d, pT[1], y)
    es.close()
    return nc


def host_inputs(inputs, cores):
    w = prep_weights({k: v for k, v in inputs.items() if k.startswith(("rw_", "da_", "ln", "peer", "ple"))})
    cst = make_consts()
    ck = inputs["cache_k"]
    npool = ck.shape[0]
    ckT = np.ascontiguousarray(ck.transpose(0, 3, 2, 1)).reshape(npool * 128, D)
    cv = np.ascontiguousarray(inputs["cache_v"]).reshape(npool * 128, D)
    shared = dict(w)
    shared.update({"cst_" + k: v for k, v in cst.items()})
    shared["ckT"] = ckT
    shared["cv"] = cv
    maps = []
    for c in cores:
        sl = slice(16 * c, 16 * c + 16)
        xs = inputs["x_sample"][sl]
        x0 = np.concatenate([inputs["x_prompt"][c], xs.reshape(NS, D)], 0)
        xprev = np.zeros_like(x0)
        xprev[1:NP_] = x0[0:NP_ - 1]
        xprev[NP_:] = np.concatenate([inputs["state_shift"][0, sl][:, None, :], xs[:, :3]], 1).reshape(NS, D)
        m = dict(shared)
        m["x0"] = np.ascontiguousarray(x0)
        m["x0T"] = np.ascontiguousarray(x0.T)
        m["xpT"] = np.ascontiguousarray(xprev.T)
        m["wkv_in"] = np.ascontiguousarray(inputs["state_wkv"][0, sl])
        for L in range(2):
            p = np.concatenate([inputs["p_prompt"][L, c], inputs["p_sample"][L, sl].reshape(NS, 256)], 0)
            m[f"pT{L}"] = np.ascontiguousarray(p.T)
        m["ptab"] = np.ascontiguousarray(inputs["page_table"][sl]).astype(np.int32)
        maps.append(m)
    return maps, {k: v.shape for k, v in w.items()}, {k: v.shape for k, v in cst.items()}, npool * 128


def kernel(**inputs):
    inputs = {k: np.asarray(v) for k, v in inputs.items()}
    n = 8
    maps, wsh, csh, nrows = host_inputs(inputs, list(range(n)))
    nc = build_program(wsh, csh, nrows)
    res = run_bass_kernel_spmd(nc, maps, core_ids=list(range(n)))
    R = res.results
    f = np.float32
    y_p = np.stack([R[c]["y"][:NP_] for c in range(n)]).astype(f)
    y_s = np.concatenate([R[c]["y"][NP_:].reshape(16, 4, D) for c in range(n)]).astype(f)
    sh_p = np.stack([R[c]["shift_p"][0] for c in range(n)])[None].astype(f)
    wkv_p = np.stack([R[c]["wkv_p"] for c in range(n)])[None].astype(f)
    k_p = np.stack([R[c]["k_o"][:NP_].reshape(NP_, 8, 128) for c in range(n)]).astype(f)
    v_p = np.stack([R[c]["v_o"][:NP_].reshape(NP_, 8, 128) for c in range(n)]).astype(f)
    sh_s = np.concatenate([R[c]["shift_s"] for c in range(n)])[None].astype(f)
    wkv_s = np.concatenate([R[c]["wkv_s"] for c in range(n)])[None].astype(f)
    k_s = np.concatenate([R[c]["k_o"][NP_:].reshape(16, 4, 8, 128) for c in range(n)]).astype(f)
    v_s = np.concatenate([R[c]["v_o"][NP_:].reshape(16, 4, 8, 128) for c in range(n)]).astype(f)
    return (y_p, y_s, sh_p, wkv_p, k_p, v_p, sh_s, wkv_s, k_s, v_s)
```
